# Optimizing a Trainium2 kernel written in Bass

```python
import math
import jax, jax.numpy as jnp
from jax import lax
import numpy as np

D_MODEL = 2048
BATCH = 2
SEQ = 4096
DEPTH = 1
DEC_BATCH = 8
DEC_SEQ = 4
PAST_LEN = 16384
PAGE_SIZE = 128

N_META = 16
POOL_WIDTH = D_MODEL // 2
POOL_WINDOWS = (2, 4, 8, 16)
N_POOL_GROUPS = len(POOL_WINDOWS)
POOL_GROUP_DIM = POOL_WIDTH // N_POOL_GROUPS
POOL_BUF = max(POOL_WINDOWS) - 1
ATTN_WIDTH = D_MODEL - POOL_WIDTH
N_HEADS = 8
HEAD_DIM = ATTN_WIDTH // N_HEADS
QK_HALF = HEAD_DIM // 2
D_FF = ((8 * D_MODEL // 3 + 255) // 256) * 256
Q_BLOCK = 128
EPS = 1e-6
NEG_INF = -1e30

kernel_name = 'hymba_pool_diffattn_macaron_step'


def rmsnorm(x, g):
    xf = x.astype(jnp.float32)
    y = xf * lax.rsqrt(jnp.mean(xf * xf, axis=-1, keepdims=True) + EPS)
    return (y * g.astype(jnp.float32)).astype(x.dtype)


def swiglu_half(x, g, w_gate, w_up, w_down):
    h = rmsnorm(x, g)
    return x + 0.5 * ((jax.nn.silu(h @ w_gate) * (h @ w_up)) @ w_down)


def alibi_slopes():
    return jnp.asarray(2.0 ** (-8.0 * np.arange(1, N_HEADS + 1) / N_HEADS), dtype=jnp.float32)


def multiscale_pool(p, buf, pos0, w_pool, pool_scale):
    B, T, _ = p.shape
    xp = jnp.concatenate([buf.astype(p.dtype), p], axis=1)
    cs = jnp.cumsum(xp.astype(jnp.float32), axis=1)
    cs = jnp.concatenate([jnp.zeros((B, 1, POOL_WIDTH), jnp.float32), cs], axis=1)
    cs = cs.reshape(B, POOL_BUF + T + 1, N_POOL_GROUPS, POOL_GROUP_DIM)
    end = cs[:, POOL_BUF + 1:]
    start = jnp.stack([cs[:, POOL_BUF + 1 - w:POOL_BUF + 1 - w + T, gi]
                       for gi, w in enumerate(POOL_WINDOWS)], axis=2)
    pos = pos0 + jnp.arange(T)
    win = jnp.asarray(POOL_WINDOWS, jnp.float32)
    count = jnp.minimum((pos + 1).astype(jnp.float32)[:, None], win[None, :])
    feat = (end - start) / count[None, :, :, None] - p.astype(jnp.float32).reshape(B, T, N_POOL_GROUPS, POOL_GROUP_DIM)
    out = jnp.einsum('btgc,gcd->btgd', feat.astype(p.dtype), w_pool).reshape(B, T, POOL_WIDTH) * pool_scale
    return out.astype(p.dtype), xp[:, -POOL_BUF:]


def diff_attention(q, k, v, q_pos, k_pos, lam, lam_init, subln_gain):
    scale = QK_HALF ** -0.5
    s1 = jnp.einsum('bqhd,bkhd->bhqk', q[..., :QK_HALF], k[..., :QK_HALF]).astype(jnp.float32) * scale
    s2 = jnp.einsum('bqhd,bkhd->bhqk', q[..., QK_HALF:], k[..., QK_HALF:]).astype(jnp.float32) * scale
    dist = q_pos[:, None] - k_pos[None, :]
    bias = -alibi_slopes()[:, None, None] * dist.astype(jnp.float32)[None]
    mask = (dist >= 0)[None, None]
    a1 = jax.nn.softmax(jnp.where(mask, s1 + bias, NEG_INF), axis=-1)
    a2 = jax.nn.softmax(jnp.where(mask, s2 + bias, NEG_INF), axis=-1)
    o = jnp.einsum('bhqk,bkhd->bqhd', a1 - lam * a2, v.astype(jnp.float32))
    o = rmsnorm(o, subln_gain) * (1.0 - lam_init)
    return o.astype(q.dtype)


def token_mixer(h, pos0, pool_buf, past_k, past_v, lam_init, w_in, w_pool, pool_scale,
                lambda_q1, lambda_k1, lambda_q2, lambda_k2, subln_gain, w_out):
    B, T, _ = h.shape
    proj = h @ w_in
    p = proj[..., :POOL_WIDTH]
    q = proj[..., POOL_WIDTH:POOL_WIDTH + ATTN_WIDTH].reshape(B, T, N_HEADS, HEAD_DIM)
    k = proj[..., POOL_WIDTH + ATTN_WIDTH:POOL_WIDTH + 2 * ATTN_WIDTH].reshape(B, T, N_HEADS, HEAD_DIM)
    v = proj[..., POOL_WIDTH + 2 * ATTN_WIDTH:].reshape(B, T, N_HEADS, HEAD_DIM)
    pool_out, new_buf = multiscale_pool(p, pool_buf, pos0, w_pool, pool_scale)
    if past_k is None:
        keys, values = k, v
    else:
        keys = jnp.concatenate([past_k.astype(k.dtype), k], axis=1)
        values = jnp.concatenate([past_v.astype(v.dtype), v], axis=1)
    k_pos = jnp.arange(keys.shape[1])
    lam = (jnp.exp(jnp.sum(lambda_q1.astype(jnp.float32) * lambda_k1.astype(jnp.float32)))
           - jnp.exp(jnp.sum(lambda_q2.astype(jnp.float32) * lambda_k2.astype(jnp.float32))) + lam_init)
    qblk = min(Q_BLOCK, T)
    n_blk = -(-T // qblk)
    pad = n_blk * qblk - T
    qb = jnp.pad(q, ((0, 0), (0, pad), (0, 0), (0, 0))).reshape(B, n_blk, qblk, N_HEADS, HEAD_DIM).transpose(1, 0, 2, 3, 4)
    posb = (pos0 + jnp.arange(n_blk * qblk)).reshape(n_blk, qblk)

    def attend(args):
        q_blk, q_pos = args
        return diff_attention(q_blk, keys, values, q_pos, k_pos, lam, lam_init, subln_gain)

    ob = lax.map(attend, (qb, posb))
    o = ob.transpose(1, 0, 2, 3, 4).reshape(B, n_blk * qblk, ATTN_WIDTH)[:, :T]
    mixed = jnp.concatenate([pool_out, o.astype(h.dtype)], axis=-1) @ w_out
    return mixed, k, v, new_buf


def decoder_layer(x, pos0, pool_buf, past_k, past_v, lam_init,
                  norm_ffn1, w_gate1, w_up1, w_down1, norm_mix, w_in, w_pool, pool_scale,
                  lambda_q1, lambda_k1, lambda_q2, lambda_k2, subln_gain, w_out,
                  norm_ffn2, w_gate2, w_up2, w_down2):
    x = swiglu_half(x, norm_ffn1, w_gate1, w_up1, w_down1)
    mixed, k_new, v_new, new_buf = token_mixer(rmsnorm(x, norm_mix), pos0, pool_buf, past_k, past_v, lam_init,
                                               w_in, w_pool, pool_scale, lambda_q1, lambda_k1, lambda_q2,
                                               lambda_k2, subln_gain, w_out)
    x = x + mixed
    x = swiglu_half(x, norm_ffn2, w_gate2, w_up2, w_down2)
    return x, k_new, v_new, new_buf


def setup_inputs(seed: int = 0) -> dict:
    key = jax.random.key(seed)
    ks = jax.random.split(key, 32)
    f32 = jnp.float32

    def nrm(k, shape, scale):
        return jax.random.normal(k, shape, f32) * scale

    n_pages = PAST_LEN // PAGE_SIZE
    n_used = DEC_BATCH * n_pages
    n_pool = n_used + max(1, n_used // 4)
    page_table = jax.random.permutation(ks[0], n_pool)[:n_used].reshape(DEC_BATCH, n_pages).astype(jnp.int32)
    return {
        'x_prompt': nrm(ks[1], (BATCH, SEQ, D_MODEL), 1.0),
        'x_sample': nrm(ks[2], (DEC_BATCH, DEC_SEQ, D_MODEL), 1.0),
        'cache_k': nrm(ks[3], (DEPTH, n_pool, PAGE_SIZE, N_HEADS, HEAD_DIM), 1.0),
        'cache_v': nrm(ks[4], (DEPTH, n_pool, PAGE_SIZE, N_HEADS, HEAD_DIM), 1.0),
        'state_pool': nrm(ks[5], (DEPTH, DEC_BATCH, POOL_BUF, POOL_WIDTH), 1.0),
        'page_table': page_table,
        'meta_tokens': nrm(ks[6], (N_META, D_MODEL), 1.0),
        'norm_ffn1': 1.0 + nrm(ks[7], (DEPTH, D_MODEL), 0.02),
        'w_gate1': nrm(ks[8], (DEPTH, D_MODEL, D_FF), D_MODEL ** -0.5),
        'w_up1': nrm(ks[9], (DEPTH, D_MODEL, D_FF), D_MODEL ** -0.5),
        'w_down1': nrm(ks[10], (DEPTH, D_FF, D_MODEL), D_FF ** -0.5),
        'norm_mix': 1.0 + nrm(ks[11], (DEPTH, D_MODEL), 0.02),
        'w_in': nrm(ks[12], (DEPTH, D_MODEL, POOL_WIDTH + 3 * ATTN_WIDTH), D_MODEL ** -0.5),
        'w_pool': nrm(ks[13], (DEPTH, N_POOL_GROUPS, POOL_GROUP_DIM, POOL_GROUP_DIM), POOL_GROUP_DIM ** -0.5),
        'pool_scale': 1.0 + nrm(ks[14], (DEPTH, POOL_WIDTH), 0.02),
        'lambda_q1': nrm(ks[15], (DEPTH, QK_HALF), 0.1),
        'lambda_k1': nrm(ks[16], (DEPTH, QK_HALF), 0.1),
        'lambda_q2': nrm(ks[17], (DEPTH, QK_HALF), 0.1),
        'lambda_k2': nrm(ks[18], (DEPTH, QK_HALF), 0.1),
        'subln_gain': 1.0 + nrm(ks[19], (DEPTH, HEAD_DIM), 0.02),
        'w_out': nrm(ks[20], (DEPTH, POOL_WIDTH + ATTN_WIDTH, D_MODEL), (POOL_WIDTH + ATTN_WIDTH) ** -0.5),
        'norm_ffn2': 1.0 + nrm(ks[21], (DEPTH, D_MODEL), 0.02),
        'w_gate2': nrm(ks[22], (DEPTH, D_MODEL, D_FF), D_MODEL ** -0.5),
        'w_up2': nrm(ks[23], (DEPTH, D_MODEL, D_FF), D_MODEL ** -0.5),
        'w_down2': nrm(ks[24], (DEPTH, D_FF, D_MODEL), D_FF ** -0.5),
        'norm_final': 1.0 + nrm(ks[25], (D_MODEL,), 0.02),
    }


def reference(x_prompt, x_sample, cache_k, cache_v, state_pool, page_table, meta_tokens,
              norm_ffn1, w_gate1, w_up1, w_down1, norm_mix, w_in, w_pool, pool_scale,
              lambda_q1, lambda_k1, lambda_q2, lambda_k2, subln_gain, w_out,
              norm_ffn2, w_gate2, w_up2, w_down2, norm_final):
    b_prompt = x_prompt.shape[0]
    b_sample, n_pages = page_table.shape
    xp = jnp.concatenate([jnp.broadcast_to(meta_tokens.astype(x_prompt.dtype)[None], (b_prompt, N_META, D_MODEL)),
                          x_prompt], axis=1)
    xs = x_sample
    kp_l, vp_l, bp_l, ks_l, vs_l, bs_l = [], [], [], [], [], []
    for i in range(DEPTH):
        lam_init = 0.8 - 0.6 * math.exp(-0.3 * i)
        lw = (norm_ffn1[i], w_gate1[i], w_up1[i], w_down1[i], norm_mix[i], w_in[i], w_pool[i], pool_scale[i],
              lambda_q1[i], lambda_k1[i], lambda_q2[i], lambda_k2[i], subln_gain[i], w_out[i],
              norm_ffn2[i], w_gate2[i], w_up2[i], w_down2[i])
        zero_buf = jnp.zeros((b_prompt, POOL_BUF, POOL_WIDTH), xp.dtype)
        xp, kp, vp, bp = decoder_layer(xp, 0, zero_buf, None, None, lam_init, *lw)
        past_k = cache_k[i, page_table].reshape(b_sample, n_pages * PAGE_SIZE, N_HEADS, HEAD_DIM)
        past_v = cache_v[i, page_table].reshape(b_sample, n_pages * PAGE_SIZE, N_HEADS, HEAD_DIM)
        xs, kn, vn, bn = decoder_layer(xs, PAST_LEN, state_pool[i], past_k, past_v, lam_init, *lw)
        kp_l.append(kp); vp_l.append(vp); bp_l.append(bp)
        ks_l.append(kn); vs_l.append(vn); bs_l.append(bn)
    y_prompt = rmsnorm(xp[:, N_META:], norm_final)
    y_sample = rmsnorm(xs, norm_final)
    k_prompt = jnp.stack(kp_l)
    v_prompt = jnp.stack(vp_l)
    pool_prompt = jnp.stack(bp_l)
    k_sample = jnp.stack(ks_l)
    v_sample = jnp.stack(vs_l)
    pool_sample = jnp.stack(bs_l)
    return (y_prompt, y_sample, k_prompt, v_prompt, pool_prompt, k_sample, v_sample, pool_sample)
```

```python
import numpy as np
import concourse.bass as bass
import concourse.mybir as mybir
from concourse.bass_utils import run_bass_kernel_spmd

F32 = mybir.dt.float32
BF16 = mybir.dt.bfloat16
I32 = mybir.dt.int32
ALU = mybir.AluOpType
AF = mybir.ActivationFunctionType

D = 2048
KC = 16
NH = 8
EPS = 1e-6
NEG = -30000.0
SLOPES = [2.0 ** (-8.0 * (h + 1) / NH) for h in range(NH)]
LAM_INIT = 0.2
WINS = (2, 4, 8, 16)


class Buf:
    __slots__ = ("name", "w", "r")

    def __init__(self, name):
        self.name = name
        self.w = None
        self.r = []


class Prog:
    def __init__(self):
        self.recs = []
        self.nslot = {"sp": 8, "pool": 8}

    def barrier(self):
        self.recs.append(("sp", [], [], [], "bar"))

    def add(self, eng, fns, reads=(), writes=(), kind="c"):
        if not isinstance(fns, (list, tuple)):
            fns = [fns]
        self.recs.append((eng, list(fns), list(reads), list(writes), kind))

    def emit(self, nc, sems, block_engines):
        cnt = {e: 0 for e in ("pe", "act", "dve", "pool")}
        slot_use = {q: [0] * n for q, n in self.nslot.items()}
        slot_next = {q: 0 for q in self.nslot}
        ncc = 0
        waited = {}
        streams = {e: [] for e in ("pe", "act", "dve", "pool", "sp")}

        def need(E, tok):
            if tok is None:
                return
            key = (E, tok[0])
            if waited.get(key, 0) >= tok[1]:
                return
            waited[key] = tok[1]
            streams[E].append(("w", tok[0], tok[1]))

        cc_done = []
        for eng, fns, reads, writes, kind in self.recs:
            E = eng
            if kind == "bar":
                alltok = [(e, cnt[e]) for e in cnt if cnt[e]]
                for q, n in self.nslot.items():
                    for s in range(n):
                        if slot_use[q][s]:
                            alltok.append((f"{q}{s}", 16 * slot_use[q][s]))
                alltok += cc_done
                for E2 in streams:
                    for t in alltok:
                        if t[0] == E2:
                            continue
                        need(E2, t)
                continue
            toks = []
            for b in reads:
                toks.append(b.w)
            for b in writes:
                toks.append(b.w)
                toks.extend(b.r)
            if kind == "c":
                cnt[E] += 1
                tok = (E, cnt[E])
                inc = 1
            elif kind == "d":
                q = E
                s = slot_next[q]
                slot_next[q] = (s + 1) % self.nslot[q]
                prev = slot_use[q][s]
                if prev:
                    toks.append((f"{q}{s}", 16 * prev))
                slot_use[q][s] = prev + 1
                tok = (f"{q}{s}", 16 * (prev + 1))
                inc = 16
            else:
                tok = (f"cc{ncc}", 1)
                cc_done.append(tok)
                ncc += 1
                inc = 1
            for t in toks:
                if t is None:
                    continue
                if t[0] == E and E == "pe":
                    continue
                need(E, t)
            streams[E].append(("i", fns, tok[0], inc))
            for b in writes:
                b.w = tok
                b.r = []
            for b in reads:
                if b not in writes:
                    b.r.append(tok)
        final = []
        for q, n in self.nslot.items():
            for s in range(n):
                if slot_use[q][s]:
                    final.append((f"{q}{s}", 16 * slot_use[q][s]))
        for t in final:
            need("sp", t)
        for e in ("pe", "act", "dve", "pool"):
            if cnt[e]:
                need("sp", (e, cnt[e]))
        return streams


def run_streams(streams, sems, engs):
    for e, lst in streams.items():
        eng = engs[e]
        for it in lst:
            if it[0] == "w":
                eng.wait_ge(sems[it[1]], it[2])
            else:
                _, fns, semname, inc = it
                last = None
                for f in fns:
                    last = f(eng)
                last.then_inc(sems[semname], inc)


def make_cfg(seq=4096, dff=5632, page=128, npool=1280):
    ch = seq // 4
    return dict(SEQ=seq, CH=ch, DFF=dff, PAGE=page, NPOOL=npool, NT=ch + 48, NSUB=page // 16)


def token_tiles(cfg):
    ch = cfg["CH"]
    tl = [(i * 512, 512) for i in range(ch // 512)]
    tl.append((ch, 48))
    return tl


GROUPS4 = [[0, 1, 2, 3], [4, 5, 6, 7]]
PAIRS = [[0, 4], [1, 5], [2, 6], [3, 7]]


def build(cfg):
    from contextlib import ExitStack
    CH, DFF, PAGE, NT, NPOOL, NSUB = cfg["CH"], cfg["DFF"], cfg["PAGE"], cfg["NT"], cfg["NPOOL"], cfg["NSUB"]
    NF = DFF // 128
    TT = token_tiles(cfg)
    NTT = len(TT)
    NKT = CH // 128
    NQT = CH // 512
    nc = bass.Bass("TRN2", target_bir_lowering=False, num_devices=8)
    P = Prog()
    CO = cfg["OFF"]

    def din(name, shape, dt=F32):
        return nc.dram_tensor(name, list(shape), dt, kind="ExternalInput").ap()

    def dout(name, shape, dt=F32):
        return nc.dram_tensor(name, list(shape), dt, kind="ExternalOutput").ap()

    def dint(name, shape, dt):
        return nc.dram_tensor(name, list(shape), dt, kind="Internal").ap()

    xin = din("xin", [NT, D])
    wts = {}
    for nm, shp in (("w_gate1", [D, DFF]), ("w_up1", [D, DFF]), ("w_down1", [DFF, D]), ("w_in", [D, 4096]),
                    ("w_out", [D, D]), ("w_gate2", [D, DFF]), ("w_up2", [D, DFF]), ("w_down2", [DFF, D]),
                    ("w_pool", [1024, 256]), ("w_in_hd", [D, 384])):
        wts[nm] = din(nm, shp)
    cvec = din("cvec", [128, 72])
    gain_bc = din("gain_bc", [128, 128])
    lamv = din("lamv", [1, 256])
    cks = [din(f"ck{u}", [NPOOL, 2048]) for u in range(NSUB)]
    cvs = [din(f"cv{u}", [NPOOL, 2048]) for u in range(NSUB)]
    ptab = din("ptab", [128, 8], I32)
    spool = din("spool", [120, 1024])
    cf32 = din("cf32", [128, cfg["NCF"]])
    cbf = din("cbf", [128, cfg["NCB"]], BF16)

    y_o = dout("y", [NT, D])
    k_o = dout("ko", [NT, 1024])
    v_o = dout("vo", [NT, 1024])
    pt_o = dout("ptail", [128, 120])
    psn_o = dout("psnew", [128, 256])
    pso_o = dout("psold", [88, 1024])

    src_kT = [dint(f"src_kT{t}", [1024, 512], BF16) for t in range(NQT)]
    gat_kT = [dint(f"gat_kT{t}", [4096, 512], BF16) for t in range(NQT)]
    src_V = [dint(f"src_V{t}", [512, 1024], BF16) for t in range(NQT)]
    gat_V = [dint(f"gat_V{t}", [2048, 1024], BF16) for t in range(NQT)]
    src_ph = dint("src_ph", [128, 128], F32)
    gat_ph = dint("gat_ph", [512, 128], F32)
    scr_q = dint("scr_q", [1024, CH + 16], BF16)
    scr_mk = dint("scr_mk", [1024, 16], BF16)
    scr_mv = dint("scr_mv", [16, 1024], BF16)
    src_o = dint("src_o", [32, 128], F32)
    gat_o4 = dint("gat_o4", [128, 128], F32)
    gat_o8 = dint("gat_o8", [256, 128], F32)
    B = {n: Buf(n) for n in ("src_kT", "gat_kT", "src_V", "gat_V", "src_ph", "gat_ph", "scr_q", "scr_mk", "scr_mv",
                             "src_o", "gat_o4", "gat_o8", "out")}

    es = ExitStack()

    uniq = [0]

    def sb(name, shape, dt=F32, st=None):
        uniq[0] += 1
        return (st or es).enter_context(nc.sbuf_tensor(f"{name}_{uniq[0]}", list(shape), dt))

    class Rot:
        def __init__(self, name, shape, dt, n, st=None):
            self.t = [sb(f"{name}{i}", shape, dt, st) for i in range(n)]
            self.b = [Buf(f"{name}{i}") for i in range(n)]
            self.i = 0

        def get(self):
            k = self.i % len(self.t)
            self.i += 1
            return self.t[k], self.b[k]

    with es:
        sems = {}
        for nm in (["pe", "act", "dve", "pool"] + [f"sp{i}" for i in range(8)] + [f"pool{i}" for i in range(8)]
                   + [f"cc{i}" for i in range(8)]):
            sems[nm] = es.enter_context(nc.semaphore("s_" + nm))
        psum = [es.enter_context(nc.psum_tensor(f"ps{i}", [128, 512], F32)) for i in range(7)]
        psb = [Buf(f"ps{i}") for i in range(7)]
        psbf = es.enter_context(nc.psum_tensor("psbf", [128, 1024], BF16))
        b_psbf = Buf("psbf")
        pcount = [0]

        def next_ps():
            i = pcount[0] % 7
            pcount[0] += 1
            return psum[i], psb[i]

        SKIP = cfg.get("SKIP") or ""
        phase = [""]

        def dma(q, out, in_, reads, writes):
            if phase[0] == "A" and "d" in SKIP:
                return
            P.add(q, lambda e, o=out, i=in_: e.dma_start(out=o, in_=i), reads, writes, kind="d")

        def cc(groups, src, dst, reads, writes):
            P.add("pool", lambda e: e.collective_compute("AllGather", ALU.bypass, replica_groups=groups,
                                                          ins=[src], outs=[dst]), reads, writes, kind="cc")

        xT = sb("xT", [128, KC, NT])
        xTb = [[Buf(f"x{k}_{t}") for t in range(NTT)] for k in range(KC)]
        hT = sb("hT", [128, KC, NT], BF16)
        hTb = [[Buf(f"h{k}_{t}") for t in range(NTT)] for k in range(KC)]
        cv_sb = sb("cv_sb", [128, 72]); b_cv = Buf("cv")
        cf_sb = sb("cf_sb", [128, cfg["NCF"]]); b_cf = Buf("cf")
        cb_sb = sb("cb_sb", [128, cfg["NCB"]], BF16); b_cb = Buf("cb")
        gain_sb = sb("gain_sb", [128, 128]); b_gain = Buf("gain")
        pt_sb = sb("pt_sb", [128, 8], I32); b_pt = Buf("pt")
        lam_sb = sb("lam_sb", [128, 4]); b_lam = Buf("lam")
        sqr = Rot("sq", [128, 512], BF16, 3)
        rstd_r = Rot("rstd", [128, 512], F32, 2)

        def cf(name, w=1, o=0):
            return cf_sb[:, CO[name] + o:CO[name] + o + w]

        def cb(name, w, o=0):
            return cb_sb[:, CO[name] + o:CO[name] + o + w]

        ident_f = cf("ident", 128)
        ident_b = cb("identb", 128)
        ones_b = cb("onesb", 128)
        eps_col = cf("eps")

        dma("sp", cv_sb[:], cvec[:, :], [], [b_cv])
        dma("sp", cf_sb[:], cf32[:, :], [], [b_cf])
        dma("sp", cb_sb[:], cbf[:, :], [], [b_cb])
        dma("sp", gain_sb[:], gain_bc[:, :], [], [b_gain])
        dma("sp", pt_sb[:], ptab[:, :], [], [b_pt])

        evac_flip = [0]

        def evac(out, in_, reads, writes, scale=None, eng=None):
            evac_flip[0] ^= 1
            if eng is None:
                eng = "act" if evac_flip[0] else "dve"
            if eng == "act":
                if scale is None:
                    P.add("act", lambda e: e.activation(out=out, in_=in_, func=AF.Copy), reads, writes)
                else:
                    P.add("act", lambda e: e.activation(out=out, in_=in_, func=AF.Copy, scale=scale), reads, writes)
            else:
                if scale is None:
                    P.add("dve", lambda e: e.tensor_copy(out=out, in_=in_), reads, writes)
                else:
                    P.add("dve", lambda e: e.tensor_scalar(out=out, in0=in_, scalar1=scale, scalar2=None, op0=ALU.mult),
                          reads, writes)

        def tile_of(tok):
            for ti, (t0, n) in enumerate(TT):
                if t0 <= tok < t0 + n:
                    return ti
            raise ValueError

        with ExitStack() as st:
            xtok = Rot("xtok", [128, D], F32, 2, st)
            for tt in range((NT + 127) // 128):
                r0 = tt * 128
                nr = min(128, NT - r0)
                xt, xb = xtok.get()
                dma("sp", xt[0:nr, :], xin[r0:r0 + nr, :], [], [xb])
                ti = tile_of(r0)
                for k4 in range(KC // 4):
                    ps, pb = next_ps()
                    fns = []
                    for kk in range(4):
                        k = k4 * 4 + kk
                        fns.append(lambda e, k=k, kk=kk, ps=ps, xt=xt, nr=nr: e.transpose(
                            out=ps[:, kk * 128:kk * 128 + nr], in_=xt[0:nr, k * 128:(k + 1) * 128],
                            identity=ident_f[0:nr, 0:nr]))
                    P.add("pe", fns, [xb, b_cf], [pb])
                    evac(xT[:, k4 * 4:k4 * 4 + 4, r0:r0 + nr], ps[:].rearrange("p (a b) -> p a b", a=4)[:, :, 0:nr],
                         [pb], [xTb[k][ti] for k in range(k4 * 4, k4 * 4 + 4)])
            P.barrier()

        def rms_rstd(ti):
            t0, n = TT[ti]
            ps, pb = next_ps()
            for k in range(KC):
                sq, sqb = sqr.get()
                P.add("act", lambda e, sq=sq, k=k: e.activation(out=sq[:, 0:n], in_=xT[:, k, t0:t0 + n], func=AF.Square),
                      [xTb[k][ti]], [sqb])
                P.add("pe", lambda e, sq=sq, k=k, ps=ps: e.matmul(ps[:, 0:n], lhsT=ones_b, rhs=sq[:, 0:n],
                                                                 start=(k == 0), stop=(k == KC - 1)),
                      [sqb, b_cb] + ([pb] if k else []), [pb])
            rs, rsb = rstd_r.get()
            P.add("act", lambda e: e.activation(out=rs[:, 0:n], in_=ps[:, 0:n], func=AF.Sqrt, bias=eps_col, scale=1.0 / D),
                  [pb, b_cf], [rsb])
            P.add("dve", lambda e: e.reciprocal(out=rs[:, 0:n], in_=rs[:, 0:n]), [rsb], [rsb])
            return rs, rsb

        def rmsnorm_to_h(gbase):
            for ti, (t0, n) in enumerate(TT):
                rs, rsb = rms_rstd(ti)
                for k in range(KC):
                    P.add("dve", lambda e, k=k, rs=rs, t0=t0, n=n: e.scalar_tensor_tensor(
                        out=hT[:, k, t0:t0 + n], in0=xT[:, k, t0:t0 + n], scalar=cv_sb[:, gbase + k:gbase + k + 1],
                        in1=rs[:, 0:n], op0=ALU.mult, op1=ALU.mult),
                        [xTb[k][ti], rsb, b_cv], [hTb[k][ti]])

        def wload(dst, w2d, n0, wdt, nk):
            return lambda e: e.dma_start(out=dst, in_=w2d.rearrange("(kc p) n -> p kc n", p=128)[:, 0:nk, n0:n0 + wdt])

        FP = 8

        def ffn(gbase, wg, wu, wd):
            with ExitStack() as st:
                wgu = Rot("wgu", [128, KC, 256], BF16, 4, st)
                hid = sb("hid", [128, FP, NT], BF16, st)
                hidb = [[Buf(f"hid{f}_{t}") for t in range(NTT)] for f in range(FP)]
                wdn = Rot("wdn", [128, FP, 512], BF16, 2, st)
                sgr = Rot("sg", [128, 512], F32, 2, st)
                rmsnorm_to_h(gbase)
                f0 = 0
                while f0 < NF:
                    nfp = min(FP, NF - f0)
                    for fp in range(0, nfp, 2):
                        wgt, wgb = wgu.get()
                        P.add("pool", wload(wgt[:], wg, (f0 + fp) * 128, 256, KC), [], [wgb], kind="d")
                        wut, wub = wgu.get()
                        P.add("pool", wload(wut[:], wu, (f0 + fp) * 128, 256, KC), [], [wub], kind="d")
                        for ff in range(2):
                            fi = fp + ff
                            for ti, (t0, n) in enumerate(TT):
                                psA, pbA = next_ps()
                                P.add("pe", [lambda e, k=k, psA=psA, wgt=wgt, ff=ff, t0=t0, n=n: e.matmul(
                                    psA[:, 0:n], lhsT=wgt[:, k, ff * 128:(ff + 1) * 128], rhs=hT[:, k, t0:t0 + n],
                                    start=(k == 0), stop=(k == KC - 1)) for k in range(KC)],
                                    [wgb] + [hTb[k][ti] for k in range(KC)], [pbA])
                                psB, pbB = next_ps()
                                P.add("pe", [lambda e, k=k, psB=psB, wut=wut, ff=ff, t0=t0, n=n: e.matmul(
                                    psB[:, 0:n], lhsT=wut[:, k, ff * 128:(ff + 1) * 128], rhs=hT[:, k, t0:t0 + n],
                                    start=(k == 0), stop=(k == KC - 1)) for k in range(KC)],
                                    [wub] + [hTb[k][ti] for k in range(KC)], [pbB])
                                sg, sgb = sgr.get()
                                P.add("act", lambda e, sg=sg, psA=psA, n=n: e.activation(out=sg[:, 0:n], in_=psA[:, 0:n],
                                                                                        func=AF.Silu), [pbA], [sgb])
                                P.add("dve", lambda e, sg=sg, psB=psB, fi=fi, t0=t0, n=n: e.tensor_tensor(
                                    out=hid[:, fi, t0:t0 + n], in0=sg[:, 0:n], in1=psB[:, 0:n], op=ALU.mult),
                                    [sgb, pbB], [hidb[fi][ti]])
                    for o4 in range(4):
                        wdt_, wdb = wdn.get()
                        P.add("pool", lambda e, wdt_=wdt_, f0=f0, nfp=nfp, o4=o4: e.dma_start(
                            out=wdt_[:, 0:nfp, :],
                            in_=wd.rearrange("(fc p) n -> p fc n", p=128)[:, f0:f0 + nfp, o4 * 512:(o4 + 1) * 512]),
                            [], [wdb], kind="d")
                        for oo in range(4):
                            oc = o4 * 4 + oo
                            for ti, (t0, n) in enumerate(TT):
                                ps, pb = next_ps()
                                P.add("pe", [lambda e, f=f, ps=ps, wdt_=wdt_, oo=oo, t0=t0, n=n, nfp=nfp: e.matmul(
                                    ps[:, 0:n], lhsT=wdt_[:, f, oo * 128:(oo + 1) * 128], rhs=hid[:, f, t0:t0 + n],
                                    start=(f == 0), stop=(f == nfp - 1)) for f in range(nfp)],
                                    [wdb] + [hidb[f][ti] for f in range(nfp)], [pb])
                                P.add("dve", lambda e, ps=ps, oc=oc, t0=t0, n=n: e.scalar_tensor_tensor(
                                    out=xT[:, oc, t0:t0 + n], in0=ps[:, 0:n], scalar=0.5, in1=xT[:, oc, t0:t0 + n],
                                    op0=ALU.mult, op1=ALU.add), [pb, xTb[oc][ti]], [xTb[oc][ti]])
                    f0 += nfp
                P.barrier()

        def mixer(_):
            mst = ExitStack()
            Qblk = sb("Qblk_", [128, 8, 8], BF16); b_Q = Buf("Qblk")
            ksT = sb("ksT_", [128, 32], BF16); b_ksT = Buf("ksT")
            vsf = sb("vsf_", [128, 32], F32); b_vsf = Buf("vsf")
            Vnew = sb("Vnew_", [4, 8, 128], BF16); b_Vn = Buf("Vnew")
            gain08 = sb("gain08_", [128, 128], F32); b_g08 = Buf("g08")
            lv = sb("lv_", [1, 256], F32); b_lv = Buf("lv")
            lt = sb("lt_", [1, 8], F32); b_lt = Buf("lt")
            Cm = sb("Cm_", [8, 4], F32); b_Cm = Buf("Cm")
            pAll = sb("pAll", [128, 8, CH + 15], F32, mst); b_pA = [Buf(f"pA{k}") for k in range(8)]
            pM = sb("pM", [128, 8, 31], F32, mst); b_pM = Buf("pM")
            pS = sb("pS", [128, 8, 152], F32, mst); b_pS = Buf("pS")
            P.add("dve", lambda e: e.memset(pM[:], 0.0), [], [b_pM])
            P.add("dve", lambda e: e.memset(Qblk[:], 0.0), [], [b_Q])
            P.add("dve", lambda e: e.tensor_scalar(out=gain08[:], in0=gain_sb[:], scalar1=1.0 - LAM_INIT, scalar2=None,
                                                   op0=ALU.mult), [b_gain], [b_g08])
            dma("sp", lv[:], lamv[:, :], [], [b_lv])
            P.add("dve", lambda e: e.tensor_tensor(out=lv[0:1, 0:64], in0=lv[0:1, 0:64], in1=lv[0:1, 64:128], op=ALU.mult),
                  [b_lv], [b_lv])
            P.add("dve", lambda e: e.tensor_tensor(out=lv[0:1, 128:192], in0=lv[0:1, 128:192], in1=lv[0:1, 192:256],
                                                   op=ALU.mult), [b_lv], [b_lv])
            P.add("dve", lambda e: e.reduce_sum(out=lt[0:1, 0:1], in_=lv[0:1, 0:64], axis=mybir.AxisListType.X), [b_lv], [b_lt])
            P.add("dve", lambda e: e.reduce_sum(out=lt[0:1, 1:2], in_=lv[0:1, 128:192], axis=mybir.AxisListType.X),
                  [b_lv, b_lt], [b_lt])
            P.add("act", lambda e: e.activation(out=lt[0:1, 2:4], in_=lt[0:1, 0:2], func=AF.Exp), [b_lt], [b_lt])
            P.add("dve", lambda e: e.tensor_tensor(out=lt[0:1, 4:5], in0=lt[0:1, 2:3], in1=lt[0:1, 3:4], op=ALU.subtract),
                  [b_lt], [b_lt])
            P.add("dve", lambda e: e.tensor_scalar(out=lt[0:1, 5:6], in0=lt[0:1, 4:5], scalar1=LAM_INIT, scalar2=None,
                                                   op0=ALU.add), [b_lt], [b_lt])
            ps, pb = next_ps()
            P.add("pe", lambda e, ps=ps: e.matmul(ps[:, 0:1], lhsT=cf("onesf", 128)[0:1, :], rhs=lt[0:1, 5:6],
                                                  start=True, stop=True), [b_lt, b_cf], [pb])
            P.add("dve", lambda e, ps=ps: e.tensor_copy(out=lam_sb[:, 0:1], in_=ps[:, 0:1]), [pb], [b_lam])
            P.add("dve", lambda e, ps=ps: e.tensor_scalar(out=lam_sb[:, 1:2], in0=ps[:, 0:1], scalar1=-1.0, scalar2=None,
                                                          op0=ALU.mult), [pb, b_lam], [b_lam])
            P.add("dve", lambda e: e.scalar_tensor_tensor(out=Cm[:], in0=cf("cb1", 4)[0:8, :], scalar=lam_sb[0:8, 1:2],
                                                          in1=cf("cb0", 4)[0:8, :], op0=ALU.mult, op1=ALU.add),
                  [b_lam, b_cf], [b_Cm])

            if cfg.get("STOP") == "L":
                mst.close(); return
            rmsnorm_to_h(16)
            XT = NTT - 1
            with ExitStack() as st:
                win = Rot("win", [128, KC, 256], BF16, 3, st)
                kfr = Rot("kf", [128, 512], F32, 2, st)
                k16 = Rot("k16", [128, 512], BF16, 2, st)
                kst = Rot("kst", [128, 4, 128], F32, 2, st)
                vst = Rot("vst", [128, 4, 128], BF16, 2, st)
                for t_, b_ in zip(vst.t, vst.b):
                    P.add("dve", lambda e, t_=t_: e.memset(t_[:], 1.0), [], [b_])

                phase[0] = "A"

                def tok_major_out(ft, fb, n, t0, h, dst_o, want_bf):
                    if "t" in SKIP:
                        return
                    ps2, pb2 = next_ps()
                    nsub = (n + 127) // 128
                    rows = min(128, n)
                    P.add("pe", [lambda e, j=j, ps2=ps2: e.transpose(
                        out=ps2[0:min(128, n - j * 128), j * 128:(j + 1) * 128],
                        in_=ft[:, j * 128:j * 128 + min(128, n - j * 128)], identity=ident_f) for j in range(nsub)],
                        [fb, b_cf], [pb2])
                    kt_, ktb = kst.get()
                    src = ps2[0:rows, 0:nsub * 128].rearrange("p (j d) -> p j d", d=128)
                    evac(kt_[0:rows, 0:nsub, :], src, [pb2], [ktb], eng="act")
                    if n == 512:
                        dma("sp", dst_o[t0:t0 + 512, h * 128:(h + 1) * 128].rearrange("(j p) d -> p j d", p=128), kt_[:, :, :],
                            [ktb], [B["out"]])
                    else:
                        dma("sp", dst_o[t0:t0 + n, h * 128:(h + 1) * 128], kt_[0:n, 0, :], [ktb], [B["out"]])
                    if want_bf:
                        vt_, vtb = vst.get()
                        P.add("dve", lambda e, vt_=vt_, kt_=kt_: e.tensor_copy(out=vt_[0:rows, 0:nsub, 0:128], in_=kt_[0:rows, 0:nsub, :]),
                              [ktb], [vtb])
                        if n == 512:
                            dma("sp", src_V[t0 // 512][:, h * 128:(h + 1) * 128].rearrange("(j p) c -> p j c", p=128),
                                vt_[:, :, :], [vtb], [B["src_V"]])
                        else:
                            dma("sp", scr_mv[0:16, h * 128:(h + 1) * 128], vt_[0:16, 0, :], [vtb], [B["scr_mv"]])

                order = list(range(0, 8)) + list(range(16, 32)) + list(range(8, 16))
                for oi in range(0, 32, 2):
                    oc0 = order[oi]
                    wt, wb = win.get()
                    P.add("pool", wload(wt[:], wts["w_in"], oc0 * 128, 256, KC), [], [wb], kind="d")
                    for ff in range(2):
                        oc = oc0 + ff
                        for ti, (t0, n) in enumerate(TT):
                            ps, pb = next_ps()
                            P.add("pe", [lambda e, k=k, ps=ps, wt=wt, ff=ff, t0=t0, n=n: e.matmul(
                                ps[:, 0:n], lhsT=wt[:, k, ff * 128:(ff + 1) * 128], rhs=hT[:, k, t0:t0 + n],
                                start=(k == 0), stop=(k == KC - 1)) for k in range(KC)],
                                [wb] + [hTb[k][ti] for k in range(KC)], [pb])
                            if (oc < 8 and "P" in SKIP) or (8 <= oc < 16 and "Q" in SKIP) or (16 <= oc < 24 and "K" in SKIP) or (oc >= 24 and "V" in SKIP):
                                continue
                            if oc < 8:
                                if ti < XT:
                                    evac(pAll[:, oc, 15 + t0:15 + t0 + n], ps[:, 0:n], [pb], [b_pA[oc]])
                                else:
                                    evac(pM[:, oc, 15:31], ps[:, 0:16], [pb], [b_pM], eng="act")
                                    evac(pS[:, oc, :].rearrange("p (s c) -> p s c", c=19)[:, :, 15:19],
                                         ps[:, 16:48].rearrange("p (s q) -> p s q", q=4), [pb], [b_pS], eng="act")
                            elif oc < 16:
                                h = oc - 8
                                qt_, qb_ = k16.get()
                                evac(qt_[:, 0:n], ps[:, 0:n], [pb], [qb_], scale=0.125)
                                if ti < XT:
                                    dma("sp", scr_q[h * 128:(h + 1) * 128, t0:t0 + n], qt_[:, 0:n], [qb_], [B["scr_q"]])
                                else:
                                    dma("sp", scr_q[h * 128:(h + 1) * 128, CH:CH + 16], qt_[:, 0:16], [qb_], [B["scr_q"]])
                            elif oc < 24:
                                h = oc - 16
                                kf_, kfb = kfr.get()
                                evac(kf_[:, 0:n], ps[:, 0:n], [pb], [kfb], eng="act")
                                kb_, kbb = k16.get()
                                P.add("dve", lambda e, kb_=kb_, kf_=kf_, n=n: e.tensor_copy(out=kb_[:, 0:n], in_=kf_[:, 0:n]),
                                      [kfb], [kbb])
                                if ti < XT:
                                    dma("sp", src_kT[ti][h * 128:(h + 1) * 128, :], kb_[:, 0:n], [kbb], [B["src_kT"]])
                                else:
                                    dma("sp", scr_mk[h * 128:(h + 1) * 128, 0:16], kb_[:, 0:16], [kbb], [B["scr_mk"]])
                                tok_major_out(kf_, kfb, n, t0, h, k_o, False)
                            else:
                                h = oc - 24
                                vf_, vfb = kfr.get()
                                evac(vf_[:, 0:n], ps[:, 0:n], [pb], [vfb])
                                tok_major_out(vf_, vfb, n, t0, h, v_o, True)
                phase[0] = ""
                if cfg.get("STOP") == "A0":
                    P.barrier(); st.close(); mst.close(); return
                for part in range(3):
                    wt, wb = win.get()
                    P.add("pool", wload(wt[:, :, 0:128], wts["w_in_hd"], part * 128, 128, KC), [], [wb], kind="d")
                    ps, pb = next_ps()
                    P.add("pe", [lambda e, k=k, ps=ps, wt=wt: e.matmul(
                        ps[:, 0:32], lhsT=wt[:, k, 0:128], rhs=hT[:, k, CH + 16:CH + 48],
                        start=(k == 0), stop=(k == KC - 1)) for k in range(KC)],
                        [wb] + [hTb[k][XT] for k in range(KC)], [pb])
                    if part == 0:
                        for m_ in range(2):
                            P.add("act", lambda e, ps=ps, m_=m_: e.activation(
                                out=Qblk[64 * m_:64 * m_ + 64, :, 4 * m_:4 * m_ + 4],
                                in_=ps[64 * m_:64 * m_ + 64, 0:32].rearrange("p (s q) -> p s q", q=4),
                                func=AF.Copy, scale=0.125), [pb, b_Q], [b_Q])
                    elif part == 1:
                        evac(ksT[:, :], ps[:, 0:32], [pb], [b_ksT])
                    else:
                        evac(vsf[:, :], ps[:, 0:32], [pb], [b_vsf])
                        for half in range(2):
                            ps2, pb2 = next_ps()
                            P.add("pe", [lambda e, s4=s4, ps2=ps2, half=half: e.transpose(
                                out=ps2[0:4, s4 * 128:(s4 + 1) * 128], in_=vsf[:, (half * 4 + s4) * 4:(half * 4 + s4) * 4 + 4],
                                identity=ident_f) for s4 in range(4)], [b_vsf, b_cf], [pb2])
                            evac(Vnew[0:4, half * 4:half * 4 + 4, :], ps2[0:4, :].rearrange("p (s d) -> p s d", d=128),
                                 [pb2, b_Vn], [b_Vn])
                if cfg.get("STOP") == "A1":
                    P.barrier(); st.close(); mst.close(); return
                phst = kfr.t[0]; b_phst = kfr.b[0]
                P.add("dve", lambda e: e.memset(phst[:, 0:128], 0.0), [b_phst], [b_phst])
                P.add("dve", lambda e: e.tensor_copy(out=phst[:, 0:120].rearrange("p (k c) -> p k c", c=15), in_=pAll[:, :, CH:CH + 15]),
                      [b_phst] + b_pA, [b_phst])
                dma("sp", src_ph[:, :], phst[:, 0:128], [b_phst], [B["src_ph"]])
                dma("sp", pt_o.rearrange("p (k c) -> p k c", c=15), pAll[:, :, CH:CH + 15], b_pA, [B["out"]])
                for t in range(NQT):
                    cc(GROUPS4, src_kT[t][:, :], gat_kT[t][:, :], [B["src_kT"]], [B["gat_kT"]])
                    cc(GROUPS4, src_V[t][:, :], gat_V[t][:, :], [B["src_V"]], [B["gat_V"]])
                cc(GROUPS4, src_ph[:, :], gat_ph[:, :], [B["src_ph"]], [B["gat_ph"]])
                P.barrier()

            if cfg.get("STOP") == "A":
                mst.close(); return
            with ExitStack() as st:
                gph = sb("gph", [128, 4, 128], F32, st); b_gph = Buf("gph")
                spt = sb("spt", [120, 1024], F32, st); b_spt = Buf("spt")
                W1 = sb("W1", [128, 8, 271], F32, st); b_W1 = Buf("W1")
                W2 = sb("W2", [128, 8, 271], F32, st); b_W2 = Buf("W2")
                feat = sb("feat", [128, 8, 256], BF16, st); b_feat = Buf("feat")
                featS = sb("featS", [128, 8, 32], BF16, st); b_featS = Buf("featS")
                psc = sb("psc", [128, 8, 32], F32, st); b_psc = Buf("psc")
                wp_sb = sb("wp_sb", [128, 8, 256], BF16, st); b_wp = Buf("wp")
                P.add("pool", wload(wp_sb[:], wts["w_pool"], 0, 256, 8), [], [b_wp], kind="d")
                dma("sp", gph[:], gat_ph.rearrange("(r p) c -> p r c", p=128), [B["gat_ph"]], [b_gph])
                dma("sp", spt[:], spool[:, :], [], [b_spt])
                dma("sp", pso_o.rearrange("(s r) c -> s r c", r=11), spool.rearrange("(s r) c -> s r c", r=15)[:, 4:15, :],
                    [], [B["out"]])
                halo = pAll[:, :, 0:15]
                P.add("dve", lambda e: e.tensor_scalar(out=halo, in0=pM[:, :, 16:31], scalar1=cf("sel", 1, 4), scalar2=None,
                                                       op0=ALU.mult), [b_pM, b_cf] + b_pA, b_pA)
                for r in range(4):
                    P.add("dve", lambda e, r=r: e.scalar_tensor_tensor(
                        out=halo, in0=gph[:, r, 0:120].rearrange("p (k c) -> p k c", c=15), scalar=cf("sel", 1, r), in1=halo,
                        op0=ALU.mult, op1=ALU.add), [b_gph, b_cf] + b_pA, b_pA)
                for k4 in range(2):
                    ps, pb = next_ps()
                    P.add("pe", [lambda e, kk=kk, k4=k4, ps=ps: e.transpose(
                        out=ps[:, kk * 128:kk * 128 + 120], in_=spt[0:120, (k4 * 4 + kk) * 128:(k4 * 4 + kk + 1) * 128],
                        identity=ident_f[0:120, 0:120]) for kk in range(4)], [b_spt, b_cf], [pb])
                    for kk in range(4):
                        evac(pS[:, k4 * 4 + kk, :].rearrange("p (s c) -> p s c", c=19)[:, :, 0:15],
                             ps[:, kk * 128:kk * 128 + 120].rearrange("p (s c) -> p s c", c=15), [pb, b_pS], [b_pS], eng="act")
                for s in range(8):
                    P.add("dve", lambda e, s=s: e.tensor_copy(out=psc[:, :, s * 4:s * 4 + 4], in_=pS[:, :, s * 19 + 15:s * 19 + 19]),
                          [b_pS, b_psc], [b_psc])
                dma("sp", psn_o.rearrange("p (k t) -> p k t", k=8), psc[:, :, :], [b_psc], [B["out"]])

                def windows(X, L, xbufs):
                    TTa = mybir.AluOpType.add
                    P.add("dve", lambda e: e.tensor_tensor(out=W1[:, 0:8, 1:L], in0=X[:, 0:8, 1:L], in1=X[:, 0:8, 0:L - 1], op=TTa),
                          xbufs + [b_W1], [b_W1])
                    P.add("dve", lambda e: e.tensor_tensor(out=W2[:, 2:8, 3:L], in0=W1[:, 2:8, 3:L], in1=W1[:, 2:8, 1:L - 2], op=TTa),
                          [b_W1, b_W2], [b_W2])
                    P.add("dve", lambda e: e.tensor_tensor(out=W1[:, 4:8, 7:L], in0=W2[:, 4:8, 7:L], in1=W2[:, 4:8, 3:L - 4], op=TTa),
                          [b_W2, b_W1], [b_W1])
                    P.add("dve", lambda e: e.tensor_tensor(out=W2[:, 6:8, 15:L], in0=W1[:, 6:8, 15:L], in1=W1[:, 6:8, 7:L - 8], op=TTa),
                          [b_W1, b_W2], [b_W2])
                    return [W1, W2, W1, W2]

                def pool_mm(ft, fbuf, n, c0, ti):
                    for g in range(4):
                        for dc in range(2):
                            ps, pb = next_ps()
                            P.add("pe", [lambda e, cc_=cc_, ps=ps, g=g, dc=dc: e.matmul(
                                ps[:, 0:n], lhsT=wp_sb[:, 2 * g + cc_, dc * 128:(dc + 1) * 128], rhs=ft[:, 2 * g + cc_, 0:n],
                                start=(cc_ == 0), stop=(cc_ == 1)) for cc_ in range(2)], [b_wp, fbuf], [pb])
                            oc = 2 * g + dc
                            evac(hT[:, oc, c0:c0 + n], ps[:, 0:n], [pb, b_cv, hTb[oc][ti]], [hTb[oc][ti]],
                                 scale=cv_sb[:, 64 + oc:65 + oc])

                for t0 in range(0, CH, 256):
                    ti, n = t0 // 512, 256
                    X = pAll[:, :, t0:t0 + n + 15]
                    res = windows(X, n + 15, list(b_pA))
                    for g in range(4):
                        P.add("dve", lambda e, g=g, R=res[g], X=X, n=n: e.scalar_tensor_tensor(
                            out=feat[:, 2 * g:2 * g + 2, 0:n], in0=R[:, 2 * g:2 * g + 2, 15:15 + n], scalar=1.0 / WINS[g],
                            in1=X[:, 2 * g:2 * g + 2, 15:15 + n], op0=ALU.mult, op1=ALU.subtract),
                            [b_W1, b_W2, b_feat] + b_pA, [b_feat])
                    pool_mm(feat, b_feat, n, t0, ti)
                res = windows(pM, 31, [b_pM])
                for g in range(4):
                    P.add("dve", lambda e, g=g, R=res[g]: e.tensor_tensor(
                        out=R[:, 2 * g:2 * g + 2, 15:31], in0=R[:, 2 * g:2 * g + 2, 15:31],
                        in1=cf("invc", 128).rearrange("p (k t) -> p k t", t=16)[:, 2 * g:2 * g + 2, :], op=ALU.mult),
                        [b_W1, b_W2, b_cf], [b_W1, b_W2])
                    P.add("dve", lambda e, g=g, R=res[g]: e.tensor_tensor(
                        out=feat[:, 2 * g:2 * g + 2, 0:16], in0=R[:, 2 * g:2 * g + 2, 15:31], in1=pM[:, 2 * g:2 * g + 2, 15:31],
                        op=ALU.subtract), [b_W1, b_W2, b_pM, b_feat], [b_feat])
                pool_mm(feat, b_feat, 16, CH, XT)
                res = windows(pS, 152, [b_pS])
                for g in range(4):
                    P.add("dve", lambda e, g=g, R=res[g]: e.scalar_tensor_tensor(
                        out=R[:, 2 * g:2 * g + 2, 15:152], in0=R[:, 2 * g:2 * g + 2, 15:152], scalar=1.0 / WINS[g],
                        in1=pS[:, 2 * g:2 * g + 2, 15:152], op0=ALU.mult, op1=ALU.subtract),
                        [b_W1, b_W2, b_pS], [b_W1, b_W2])
                    for s in range(8):
                        P.add("dve", lambda e, g=g, R=res[g], s=s: e.tensor_copy(
                            out=featS[:, 2 * g:2 * g + 2, s * 4:s * 4 + 4], in_=R[:, 2 * g:2 * g + 2, s * 19 + 15:s * 19 + 19]),
                            [b_W1, b_W2, b_featS], [b_featS])
                pool_mm(featS, b_featS, 32, CH + 16, XT)
                P.barrier()
            mst.close()

            if cfg.get("STOP") == "B1":
                return
            with ExitStack() as st:
                NVT = 5 * NKT + 1
                kTg = sb("kTg", [128, 4, CH], BF16, st)
                kTo = sb("kTo", [128, CH], BF16, st)
                kTm = sb("kTm", [128, 16], BF16, st)
                Vh = sb("Vh", [128, NVT, 129], BF16, st)
                qh = sb("qh", [128, CH + 16], BF16, st)
                qbt = sb("qbt", [128, CH], F32, st)
                tmpr = Rot("tmp", [128, 512], F32, 3, st)
                Ar = Rot("A", [128, 512], BF16, 3, st)
                on1 = sb("on1", [128, 4, 128], F32, st); b_on1 = Buf("on1")
                fin = sb("fin", [128, 4, 128], F32, st); b_fin = Buf("fin")
                sqt = sb("sqt", [128, 128], F32, st); b_sqt = Buf("sqt")
                rc = sb("rc", [128, 16], F32, st); b_rc = Buf("rc")
                resb = sb("resb", [128, 4, 128], BF16, st); b_resb = Buf("resb")
                b_kTg, b_kTo, b_kTm, b_Vg, b_Vo, b_Vm, b_qh, b_qbt = (Buf(x) for x in ("kTg", "kTo", "kTm", "Vg", "Vo", "Vm", "qh", "qbt"))
                P.add("dve", lambda e: e.memset(Vh[:, :, 128:129], 1.0), [], [b_Vg, b_Vo, b_Vm])
                acc_ps = [psum[5], psum[6]]
                acc_b = [psb[5], psb[6]]
                c5 = [0]

                def next_ps5():
                    i = c5[0] % 5
                    c5[0] += 1
                    return psum[i], psb[i]

                def attend(h, q0, nq, subs, keytiles, outc0, ti_out):
                    for m_ in range(2):
                        p0 = 64 * m_
                        nkts = len(keytiles)
                        for ki, (kT_ap, kbuf, V_ap, vbuf, bias_ap, nk, mask_ap) in enumerate(keytiles):
                            ps, pb = next_ps5()
                            fns = [lambda e, ps=ps, kT_ap=kT_ap, nk=nk, mask_ap=mask_ap, p0=p0: e.matmul(
                                ps[0:nk, 0:nq], lhsT=kT_ap[p0:p0 + 64, 0:nk], rhs=qh[p0:p0 + 64, q0:q0 + nq],
                                start=True, stop=(mask_ap is None))]
                            if mask_ap is not None:
                                fns.append(lambda e, ps=ps, nk=nk, mask_ap=mask_ap: e.matmul(
                                    ps[0:nk, 0:nq], lhsT=ident_b[0:nk, 0:nk], rhs=mask_ap, start=False, stop=True))
                            P.add("pe", fns, [kbuf, b_qh, b_cb], [pb])
                            tm, tmb = tmpr.get()
                            P.add("dve", lambda e, tm=tm, ps=ps, nk=nk: e.tensor_tensor(
                                out=tm[0:nk, 0:nq], in0=ps[0:nk, 0:nq], in1=qbt[0:nk, 0:nq] if q0 >= CH else qbt[0:nk, q0:q0 + nq],
                                op=ALU.add), [pb, b_qbt], [tmb])
                            A_, Ab = Ar.get()
                            P.add("act", lambda e, A_=A_, tm=tm, nk=nk, bias_ap=bias_ap: e.activation(
                                out=A_[0:nk, 0:nq], in_=tm[0:nk, 0:nq], func=AF.Exp, bias=bias_ap, scale=1.0), [tmb, b_cf], [Ab])
                            fns = []
                            if ki == 0:
                                rows0 = subs[0][1]
                                nb = (len(subs) + 2) // 3
                                for bi in range(nb):
                                    ncol = 129 * min(3, len(subs) - 3 * bi)
                                    fns.append(lambda e, bi=bi, ncol=ncol, rows0=rows0: e.matmul(
                                        acc_ps[bi][0:rows0, 0:ncol], lhsT=cb("zerob", 128)[:, 0:rows0], rhs=cb("maskd", 512)[:, 0:ncol],
                                        start=True, stop=False))
                            for si, (so, rows) in enumerate(subs):
                                accp = acc_ps[si // 3]
                                c0 = (si % 3) * 129
                                last_in_bank = (si % 3 == 2) or (si == len(subs) - 1)
                                fns.append(lambda e, A_=A_, so=so, rows=rows, accp=accp, c0=c0, V_ap=V_ap, nk=nk, ki=ki, lib=last_in_bank: e.matmul(
                                    accp[0:rows, c0:c0 + 129], lhsT=A_[0:nk, so:so + rows], rhs=V_ap[0:nk, :],
                                    start=False, stop=(ki == nkts - 1 and lib)))
                            P.add("pe", fns, [Ab, vbuf, b_cb] + acc_b, acc_b)
                        for si, (so, rows) in enumerate(subs):
                            accp = acc_ps[si // 3]
                            c0 = (si % 3) * 129
                            P.add("dve", lambda e, accp=accp, c0=c0, rows=rows, si=si: e.reciprocal(
                                out=rc[0:rows, si:si + 1], in_=accp[0:rows, c0 + 128:c0 + 129]), acc_b + [b_rc], [b_rc])
                            if m_ == 0:
                                P.add("dve", lambda e, accp=accp, c0=c0, rows=rows, si=si: e.tensor_scalar(
                                    out=on1[0:rows, si, :], in0=accp[0:rows, c0:c0 + 128], scalar1=rc[0:rows, si:si + 1],
                                    scalar2=None, op0=ALU.mult), acc_b + [b_rc, b_on1], [b_on1])
                            else:
                                P.add("dve", lambda e, accp=accp, c0=c0, rows=rows, si=si: e.tensor_scalar(
                                    out=fin[0:rows, si, :], in0=accp[0:rows, c0:c0 + 128], scalar1=rc[0:rows, si:si + 1],
                                    scalar2=None, op0=ALU.mult), acc_b + [b_rc, b_fin], [b_fin])
                                P.add("dve", lambda e, rows=rows, si=si: e.scalar_tensor_tensor(
                                    out=fin[0:rows, si, :], in0=fin[0:rows, si, :], scalar=lam_sb[0:rows, 1:2], in1=on1[0:rows, si, :],
                                    op0=ALU.mult, op1=ALU.add), [b_fin, b_on1, b_lam], [b_fin])
                    for si, (so, rows) in enumerate(subs):
                        P.add("dve", lambda e, rows=rows, si=si: e.tensor_tensor(
                            out=sqt[0:rows, :], in0=fin[0:rows, si, :], in1=fin[0:rows, si, :], op=ALU.mult), [b_fin, b_sqt], [b_sqt])
                        P.add("dve", lambda e, rows=rows, si=si: e.reduce_sum(
                            out=rc[0:rows, 8 + si:9 + si], in_=sqt[0:rows, :], axis=mybir.AxisListType.X), [b_sqt, b_rc], [b_rc])
                        P.add("act", lambda e, rows=rows, si=si: e.activation(
                            out=rc[0:rows, 8 + si:9 + si], in_=rc[0:rows, 8 + si:9 + si], func=AF.Sqrt, bias=eps_col[0:rows, :],
                            scale=1.0 / 128), [b_rc, b_cf], [b_rc])
                        P.add("dve", lambda e, rows=rows, si=si: e.reciprocal(
                            out=rc[0:rows, 8 + si:9 + si], in_=rc[0:rows, 8 + si:9 + si]), [b_rc], [b_rc])
                        P.add("dve", lambda e, rows=rows, si=si: e.scalar_tensor_tensor(
                            out=resb[0:rows, si, :], in0=fin[0:rows, si, :], scalar=rc[0:rows, 8 + si:9 + si], in1=gain08[0:rows, :],
                            op0=ALU.mult, op1=ALU.mult), [b_fin, b_rc, b_g08, b_resb], [b_resb])
                        P.add("pe", lambda e, rows=rows, si=si: e.transpose(
                            out=psbf[:, si * 128:si * 128 + rows], in_=resb[0:rows, si, :], identity=ident_b[0:rows, 0:rows]),
                            [b_resb, b_cb, b_psbf], [b_psbf])
                        evac(hT[:, 8 + h, outc0 + so:outc0 + so + rows], psbf[:, si * 128:si * 128 + rows],
                             [b_psbf, hTb[8 + h][ti_out]], [hTb[8 + h][ti_out]])

                for h in range(NH):
                    for t in range(NQT):
                        dma("sp", kTg[:, :, t * 512:(t + 1) * 512], gat_kT[t].rearrange("(r hd) t -> hd r t", r=4)[h * 128:(h + 1) * 128, :, :], [B["gat_kT"]], [b_kTg])
                        dma("sp", kTo[:, t * 512:(t + 1) * 512], src_kT[t][h * 128:(h + 1) * 128, :], [B["src_kT"]], [b_kTo])
                    dma("sp", kTm[:], scr_mk[h * 128:(h + 1) * 128, :], [B["scr_mk"]], [b_kTm])
                    for t in range(NQT):
                        for r in range(4):
                            dma("sp", Vh[:, r * NKT + t * 4:r * NKT + t * 4 + 4, 0:128],
                                gat_V[t][r * 512:(r + 1) * 512, h * 128:(h + 1) * 128].rearrange("(k p) c -> p k c", p=128),
                                [B["gat_V"]], [b_Vg])
                        dma("sp", Vh[:, 4 * NKT + t * 4:4 * NKT + t * 4 + 4, 0:128],
                            src_V[t][:, h * 128:(h + 1) * 128].rearrange("(k p) c -> p k c", p=128), [B["src_V"]], [b_Vo])
                    dma("sp", Vh[0:16, 5 * NKT, 0:128], scr_mv[0:16, h * 128:(h + 1) * 128], [B["scr_mv"]], [b_Vm])
                    dma("sp", qh[:], scr_q[h * 128:(h + 1) * 128, :], [B["scr_q"]], [b_qh])
                    P.add("dve", lambda e, h=h: e.tensor_scalar(out=qbt[:], in0=cf("qrow", CH), scalar1=-SLOPES[h], scalar2=None,
                                                                op0=ALU.mult), [b_cf, b_qbt], [b_qbt])
                    meta_kt = (kTm, b_kTm, Vh[:, 5 * NKT, :], b_Vm, cf("biasm", 1, h)[0:16, :], 16, None)
                    for qt in range(NQT):
                        kts = []
                        for r in range(4):
                            for kt in range(NKT):
                                kts.append((kTg[:, r, kt * 128:(kt + 1) * 128], b_kTg, Vh[:, r * NKT + kt, :], b_Vg,
                                            cf("biasg", 1, (h * 4 + r) * NKT + kt), 128, None))
                        kts.append(meta_kt)
                        for kt in range(4 * qt + 4):
                            m = kt - 4 * qt
                            kts.append((kTo[:, kt * 128:(kt + 1) * 128], b_kTo, Vh[:, 4 * NKT + kt, :], b_Vo,
                                        cf("biaso", 1, h * NKT + kt), 128, cb("maskd", 512, m * 512) if m >= 0 else None))
                        attend(h, qt * 512, 512, [(i * 128, 128) for i in range(4)], kts, qt * 512, qt)
                    mm_kt = (kTm, b_kTm, Vh[:, 5 * NKT, :], b_Vm, cf("biasmm", 1, h)[0:16, :], 16, cb("maskmm", 16)[0:16, :])
                    attend(h, CH, 16, [(0, 16)], [mm_kt], CH, XT)
                P.barrier()

            if cfg.get("STOP") == "B2":
                return
            with ExitStack() as st:
                Ku = Rot("Ku", [128, 2048], BF16, 2, st)
                Vu = Rot("Vu", [128, 2048], BF16, 2, st)
                kTu = Rot("kTu", [128, 2048], BF16, 2, st)
                Eu = Rot("Eu", [128, 128], F32, 2, st)
                A_all = sb("A_all", [128, PAGE * 8], BF16, st); b_Aall = Buf("Aall")
                rsum = sb("rsum", [128, 8], F32, st); b_rsum = Buf("rsum")
                An32 = sb("An32", [4, 8], F32, st); b_An32 = Buf("An32")
                An = sb("An", [4, 8], BF16, st); b_An = Buf("An")
                osb = sb("osb", [8, 128], F32, st); b_osb = Buf("osb")
                orc = sb("orc", [8, 4], F32, st); b_orc = Buf("orc")
                f4 = sb("f4", [4, 128], F32, st); b_f4 = Buf("f4")
                sq4 = sb("sq4", [4, 128], F32, st); b_sq4 = Buf("sq4")
                r4 = sb("r4", [4, 2], F32, st); b_r4 = Buf("r4")
                res_all = sb("res_all", [4, 8, 128], F32, st); b_res = Buf("res_all")
                osamp = sb("osamp", [32, 8, 128], F32, st); b_osamp = Buf("osamp")
                o_ps, o_pb = psum[5], psb[5]
                s_ps, s_pb = psum[6], psb[6]
                for s in range(8):
                    for u in range(NSUB):
                        ku, kub = Ku.get()
                        P.add("pool", lambda e, ku=ku, u=u, s=s: e.indirect_dma_start(
                            out=ku[:], out_offset=None, in_=cks[u][:, :],
                            in_offset=bass.IndirectOffsetOnAxis(ap=pt_sb[:, s:s + 1], axis=0)), [b_pt], [kub], kind="d")
                        vu, vub = Vu.get()
                        P.add("pool", lambda e, vu=vu, u=u, s=s: e.indirect_dma_start(
                            out=vu[:], out_offset=None, in_=cvs[u][:, :],
                            in_offset=bass.IndirectOffsetOnAxis(ap=pt_sb[:, s:s + 1], axis=0)), [b_pt], [vub], kind="d")
                        ktu, ktub = kTu.get()
                        for half in range(2):
                            P.add("pe", [lambda e, tk=tk, ku=ku, half=half: e.transpose(
                                out=psbf[:, tk * 128:(tk + 1) * 128], in_=ku[:, (half * 8 + tk) * 128:(half * 8 + tk + 1) * 128],
                                identity=ident_b) for tk in range(8)], [kub, b_cb, b_psbf], [b_psbf])
                            evac(ktu[:, half * 1024:(half + 1) * 1024], psbf[:, :], [b_psbf, ktub], [ktub])
                        ps, pb = next_ps5()
                        P.add("pe", [lambda e, tk=tk, ps=ps, ktu=ktu, s=s: e.matmul(
                            ps[:, tk * 8:(tk + 1) * 8], lhsT=ktu[:, tk * 128:(tk + 1) * 128], rhs=Qblk[:, s, :],
                            start=True, stop=True) for tk in range(16)], [ktub, b_Q], [pb])
                        eu, eub = Eu.get()
                        P.add("act", lambda e, eu=eu, ps=ps: e.activation(out=eu[:], in_=ps[:, 0:128], func=AF.Exp,
                                                                         bias=cf("sbias"), scale=1.0), [pb, b_cf], [eub])
                        P.add("dve", lambda e, eu=eu, u=u: e.tensor_tensor(
                            out=A_all[:, u * 128:(u + 1) * 128], in0=eu[:], in1=cf("wtok", 128, u * 128), op=ALU.mult),
                            [eub, b_cf, b_Aall], [b_Aall])
                        P.add("pe", [lambda e, tk=tk, vu=vu, u=u: e.matmul(
                            o_ps[0:8, 0:128], lhsT=A_all[:, (u * 16 + tk) * 8:(u * 16 + tk) * 8 + 8], rhs=vu[:, tk * 128:(tk + 1) * 128],
                            start=(u == 0 and tk == 0), stop=False) for tk in range(16)], [b_Aall, vub, o_pb], [o_pb])
                    ps, pb = next_ps5()
                    P.add("pe", lambda e, ps=ps, s=s: e.matmul(ps[0:4, 0:8], lhsT=ksT[:, 4 * s:4 * s + 4], rhs=Qblk[:, s, :],
                                                               start=True, stop=True), [b_ksT, b_Q], [pb])
                    P.add("act", lambda e, ps=ps: e.activation(out=An32[:], in_=ps[0:4, 0:8], func=AF.Exp,
                                                               bias=cf("nbias")[0:4, :], scale=1.0), [pb, b_cf, b_An32], [b_An32])
                    P.add("dve", lambda e: e.tensor_tensor(out=An32[:], in0=An32[:], in1=cf("nmask", 8)[0:4, :], op=ALU.mult),
                          [b_An32, b_cf], [b_An32])
                    P.add("dve", lambda e: e.tensor_copy(out=An[:], in_=An32[:]), [b_An32, b_An], [b_An])
                    P.add("pe", lambda e, s=s: e.matmul(o_ps[0:8, 0:128], lhsT=An[0:4, :], rhs=Vnew[0:4, s, :], start=False, stop=True),
                          [b_An, b_Vn, o_pb], [o_pb])
                    P.add("dve", lambda e: e.reduce_sum(out=rsum[:], in_=A_all[:].rearrange("p (t m) -> p m t", m=8),
                                                        axis=mybir.AxisListType.X), [b_Aall, b_rsum], [b_rsum])
                    P.add("pe", [lambda e: e.matmul(s_ps[0:8, 0:1], lhsT=rsum[:, :], rhs=cf("onesf", 1), start=True, stop=False),
                                 lambda e: e.matmul(s_ps[0:8, 0:1], lhsT=An32[0:4, :], rhs=cf("onesf", 1)[0:4, :], start=False, stop=True)],
                          [b_rsum, b_An32, b_cf, s_pb], [s_pb])
                    P.add("dve", lambda e: e.reciprocal(out=orc[:, 0:1], in_=s_ps[0:8, 0:1]), [s_pb, b_orc], [b_orc])
                    P.add("dve", lambda e: e.tensor_scalar(out=osb[:], in0=o_ps[0:8, 0:128], scalar1=orc[:, 0:1], scalar2=None,
                                                           op0=ALU.mult), [o_pb, b_orc, b_osb], [b_osb])
                    ps, pb = next_ps5()
                    P.add("pe", lambda e, ps=ps: e.matmul(ps[0:4, 0:128], lhsT=Cm[0:8, :], rhs=osb[0:8, :], start=True, stop=True),
                          [b_Cm, b_osb], [pb])
                    P.add("dve", lambda e, ps=ps: e.tensor_copy(out=f4[:], in_=ps[0:4, 0:128]), [pb, b_f4], [b_f4])
                    P.add("dve", lambda e: e.tensor_tensor(out=sq4[:], in0=f4[:], in1=f4[:], op=ALU.mult), [b_f4, b_sq4], [b_sq4])
                    P.add("dve", lambda e: e.reduce_sum(out=r4[:, 0:1], in_=sq4[:], axis=mybir.AxisListType.X), [b_sq4, b_r4], [b_r4])
                    P.add("act", lambda e: e.activation(out=r4[:, 0:1], in_=r4[:, 0:1], func=AF.Sqrt, bias=eps_col[0:4, :],
                                                        scale=1.0 / 128), [b_r4, b_cf], [b_r4])
                    P.add("dve", lambda e: e.reciprocal(out=r4[:, 0:1], in_=r4[:, 0:1]), [b_r4], [b_r4])
                    P.add("dve", lambda e, s=s: e.scalar_tensor_tensor(
                        out=res_all[0:4, s, :], in0=f4[:], scalar=r4[:, 0:1], in1=gain08[0:4, :], op0=ALU.mult, op1=ALU.mult),
                        [b_f4, b_r4, b_g08, b_res], [b_res])
                dma("sp", src_o.rearrange("(s q) d -> q s d", q=4), res_all[0:4, :, :], [b_res], [B["src_o"]])
                cc(GROUPS4, src_o[:, :], gat_o4[:, :], [B["src_o"]], [B["gat_o4"]])
                cc(PAIRS, gat_o4[:, :], gat_o8[:, :], [B["gat_o4"]], [B["gat_o8"]])
                dma("sp", osamp[:], gat_o8.rearrange("(h t) d -> t h d", t=32), [B["gat_o8"]], [b_osamp])
                ps, pb = next_ps5()
                P.add("pe", [lambda e, h=h, ps=ps: e.transpose(out=ps[:, h * 32:(h + 1) * 32], in_=osamp[0:32, h, :],
                                                              identity=ident_f[0:32, 0:32]) for h in range(8)],
                      [b_osamp, b_cf], [pb])
                evac(hT[:, 8:16, CH + 16:CH + 48], ps[:, 0:256].rearrange("p (h t) -> p h t", t=32),
                     [pb] + [hTb[8 + h][XT] for h in range(8)], [hTb[8 + h][XT] for h in range(8)])
                P.barrier()

            if cfg.get("STOP") == "B3":
                return
            with ExitStack() as st:
                wo = Rot("wo", [128, KC, 256], BF16, 3, st)
                for oc0 in range(0, KC, 2):
                    wt, wb = wo.get()
                    P.add("pool", wload(wt[:], wts["w_out"], oc0 * 128, 256, KC), [], [wb], kind="d")
                    for ff in range(2):
                        oc = oc0 + ff
                        for ti, (t0, n) in enumerate(TT):
                            ps, pb = next_ps()
                            P.add("pe", [lambda e, k=k, ps=ps, wt=wt, ff=ff, t0=t0, n=n: e.matmul(
                                ps[:, 0:n], lhsT=wt[:, k, ff * 128:(ff + 1) * 128], rhs=hT[:, k, t0:t0 + n],
                                start=(k == 0), stop=(k == KC - 1)) for k in range(KC)],
                                [wb] + [hTb[k][ti] for k in range(KC)], [pb])
                            P.add("dve", lambda e, ps=ps, oc=oc, t0=t0, n=n: e.tensor_tensor(
                                out=xT[:, oc, t0:t0 + n], in0=ps[:, 0:n], in1=xT[:, oc, t0:t0 + n], op=ALU.add),
                                [pb, xTb[oc][ti]], [xTb[oc][ti]])
                P.barrier()


        ffn(0, wts["w_gate1"], wts["w_up1"], wts["w_down1"])
        if cfg.get("MIXER", True):
            mixer(locals())
        ffn(32, wts["w_gate2"], wts["w_up2"], wts["w_down2"])

        with ExitStack() as st:
            ytok = Rot("ytok", [128, D], F32, 2, st)
            yf = sb("yf", [128, KC, 128], F32, st)
            b_yf = Buf("yf")
            gF = 48
            for ti, (t0, n) in enumerate(TT):
                rs, rsb = rms_rstd(ti)
                for s0 in range(0, n, 128):
                    ns = min(128, n - s0)
                    for k in range(KC):
                        P.add("dve", lambda e, k=k, rs=rs, t0=t0, s0=s0, ns=ns: e.scalar_tensor_tensor(
                            out=yf[:, k, 0:ns], in0=xT[:, k, t0 + s0:t0 + s0 + ns], scalar=cv_sb[:, gF + k:gF + k + 1],
                            in1=rs[:, s0:s0 + ns], op0=ALU.mult, op1=ALU.mult), [xTb[k][ti], rsb, b_cv, b_yf], [b_yf])
                    yt, ytb = ytok.get()
                    for k4 in range(KC // 4):
                        ps, pb = next_ps()
                        P.add("pe", [lambda e, kk=kk, k4=k4, ps=ps, ns=ns: e.transpose(
                            out=ps[0:ns, kk * 128:(kk + 1) * 128], in_=yf[:, k4 * 4 + kk, 0:ns], identity=ident_f)
                            for kk in range(4)], [b_yf, b_cf], [pb])
                        evac(yt[0:ns, k4 * 512:(k4 + 1) * 512], ps[0:ns, :], [pb], [ytb])
                    dma("sp", y_o[t0 + s0:t0 + s0 + ns, :], yt[0:ns, :], [ytb], [B["out"]])
            P.barrier()

        streams = P.emit(nc, sems, None)
        with nc.Block() as block:
            @block.tensor
            def _(e):
                run_streams({"pe": streams["pe"]}, sems, {"pe": e})

            @block.scalar
            def _(e):
                run_streams({"act": streams["act"]}, sems, {"act": e})

            @block.vector
            def _(e):
                run_streams({"dve": streams["dve"]}, sems, {"dve": e})

            @block.gpsimd
            def _(e):
                run_streams({"pool": streams["pool"]}, sems, {"pool": e})

            @block.sync
            def _(e):
                run_streams({"sp": streams["sp"]}, sems, {"sp": e})
    return nc


def host_consts(cfg, core):
    import ml_dtypes
    CH, PAGE = cfg["CH"], cfg["PAGE"]
    NKT = CH // 128
    j = core % 4
    off = {}
    cols = []
    i = np.arange(128, dtype=np.float64)[:, None]

    def addf(name, arr):
        off[name] = sum(a.shape[1] for a in cols)
        a = np.zeros((128, np.asarray(arr).shape[1]), np.float32)
        a[:np.asarray(arr).shape[0]] = np.asarray(arr, np.float32)
        cols.append(a)

    addf("ident", np.eye(128))
    addf("onesf", np.ones((128, 128)))
    addf("eps", np.full((128, 1), EPS))
    sel = np.zeros((128, 5));
    if j == 0:
        sel[:, 4] = 1.0
    else:
        sel[:, j - 1] = 1.0
    addf("sel", sel)
    bg = np.zeros((128, NH * 4 * NKT))
    bo = np.zeros((128, NH * NKT))
    bm = np.zeros((128, NH)); bmm = np.zeros((128, NH))
    for h in range(NH):
        sl = SLOPES[h]
        for r in range(4):
            for kt in range(NKT):
                bg[:, (h * 4 + r) * NKT + kt] = sl * (i[:, 0] + 128 * kt + CH * (r - j)) + (NEG if r >= j else 0.0)
        for kt in range(NKT):
            bo[:, h * NKT + kt] = sl * (i[:, 0] + 128 * kt)
        bm[:, h] = sl * (i[:, 0] - 16 - j * CH)
        bmm[:, h] = sl * i[:, 0]
    addf("biasg", bg); addf("biaso", bo); addf("biasm", bm); addf("biasmm", bmm)
    addf("qrow", np.broadcast_to(np.arange(CH, dtype=np.float64)[None, :], (128, CH)))
    slc = SLOPES[core]
    addf("sbias", slc * PAGE * (i - 127))
    tok = np.repeat(np.arange(PAGE), 8)[None, :]
    addf("wtok", np.broadcast_to(np.exp(slc * (tok - (PAGE - 1))), (128, PAGE * 8)))
    addf("nbias", slc * (1 + i))
    nm = np.zeros((128, 8))
    for jn in range(4):
        for m in range(2):
            for q in range(4):
                nm[jn, m * 4 + q] = 1.0 if jn <= q else 0.0
    addf("nmask", nm)
    invc = np.zeros((128, 128))
    for kc in range(8):
        for t in range(16):
            invc[:, kc * 16 + t] = 1.0 / min(t + 1, WINS[kc // 2])
    addf("invc", invc)
    c0 = np.zeros((128, 4)); c1 = np.zeros((128, 4))
    for q in range(4):
        c0[q, q] = 1.0; c1[4 + q, q] = 1.0
    addf("cb0", c0); addf("cb1", c1)
    cf = np.concatenate(cols, 1)
    colsb = []

    def addb(name, arr):
        off[name] = sum(a.shape[1] for a in colsb)
        colsb.append(np.asarray(arr, np.float32).astype(ml_dtypes.bfloat16))

    addb("identb", np.eye(128))
    addb("onesb", np.ones((128, 128)))
    addb("zerob", np.zeros((128, 128)))
    jq = np.arange(512)[None, :]
    md = np.concatenate([np.where(i + 128 * m <= jq, 0.0, NEG) for m in range(4)], 1)
    addb("maskd", md)
    addb("maskmm", np.where(i <= np.arange(16)[None, :], 0.0, NEG))
    cb = np.concatenate(colsb, 1)
    return off, cf, cb


def fm(v, n):
    return np.ascontiguousarray(np.asarray(v, np.float32).reshape(n, 128).T)


def prep_inputs(cfg, inp):
    CH, NSUB = cfg["CH"], cfg["NSUB"]
    maps = []
    x_prompt = np.asarray(inp["x_prompt"]); x_sample = np.asarray(inp["x_sample"])
    meta = np.asarray(inp["meta_tokens"])
    cvec = np.concatenate([fm(inp["norm_ffn1"][0], 16), fm(inp["norm_mix"][0], 16), fm(inp["norm_ffn2"][0], 16),
                           fm(inp["norm_final"], 16), fm(inp["pool_scale"][0], 8)], 1)
    gain_bc = np.ascontiguousarray(np.broadcast_to(np.asarray(inp["subln_gain"][0], np.float32)[None, :], (128, 128)))
    lamv = np.concatenate([np.asarray(inp[k][0], np.float32) for k in ("lambda_q1", "lambda_k1", "lambda_q2", "lambda_k2")])[None, :]
    w_in = np.asarray(inp["w_in"][0])
    shared = dict(cvec=cvec, gain_bc=gain_bc, lamv=np.ascontiguousarray(lamv),
                  w_gate1=np.asarray(inp["w_gate1"][0]), w_up1=np.asarray(inp["w_up1"][0]), w_down1=np.asarray(inp["w_down1"][0]),
                  w_in=w_in, w_out=np.asarray(inp["w_out"][0]),
                  w_gate2=np.asarray(inp["w_gate2"][0]), w_up2=np.asarray(inp["w_up2"][0]), w_down2=np.asarray(inp["w_down2"][0]),
                  w_pool=np.ascontiguousarray(np.asarray(inp["w_pool"][0]).reshape(1024, 256)),
                  spool=np.ascontiguousarray(np.asarray(inp["state_pool"][0]).reshape(120, 1024)),
                  ptab=np.ascontiguousarray(np.asarray(inp["page_table"]).T.astype(np.int32)))
    ckf = np.asarray(inp["cache_k"][0]); cvf = np.asarray(inp["cache_v"][0])
    npool = ckf.shape[0]
    for c in range(8):
        b, j = c // 4, c % 4
        m = dict(shared)
        m["xin"] = np.ascontiguousarray(np.concatenate([x_prompt[b, j * CH:(j + 1) * CH], meta, x_sample.reshape(32, D)], 0))
        m["w_in_hd"] = np.ascontiguousarray(np.concatenate([w_in[:, 1024 + 128 * c:1152 + 128 * c],
                                                            w_in[:, 2048 + 128 * c:2176 + 128 * c],
                                                            w_in[:, 3072 + 128 * c:3200 + 128 * c]], 1))
        for u in range(NSUB):
            m[f"ck{u}"] = np.ascontiguousarray(ckf[:, 16 * u:16 * u + 16, c, :].reshape(npool, 2048))
            m[f"cv{u}"] = np.ascontiguousarray(cvf[:, 16 * u:16 * u + 16, c, :].reshape(npool, 2048))
        off, cf, cb = host_consts(cfg, c)
        m["cf32"], m["cbf"] = cf, cb
        maps.append(m)
    return maps


def finish_cfg(cfg):
    off, cf, cb = host_consts(cfg, 0)
    cfg["OFF"] = off; cfg["NCF"] = cf.shape[1]; cfg["NCB"] = cb.shape[1]
    return cfg


def assemble(cfg, res):
    CH, SEQ = cfg["CH"], cfg["SEQ"]
    y_prompt = np.zeros((2, SEQ, D), np.float32)
    k_prompt = np.zeros((1, 2, 16 + SEQ, 8, 128), np.float32)
    v_prompt = np.zeros((1, 2, 16 + SEQ, 8, 128), np.float32)
    pool_prompt = np.zeros((1, 2, 15, 1024), np.float32)
    for c in range(8):
        b, j = c // 4, c % 4
        r = res[c]
        y_prompt[b, j * CH:(j + 1) * CH] = r["y"][0:CH]
        k_prompt[0, b, 16 + j * CH:16 + (j + 1) * CH] = r["ko"][0:CH].reshape(CH, 8, 128)
        v_prompt[0, b, 16 + j * CH:16 + (j + 1) * CH] = r["vo"][0:CH].reshape(CH, 8, 128)
        if j == 0:
            k_prompt[0, b, 0:16] = r["ko"][CH:CH + 16].reshape(16, 8, 128)
            v_prompt[0, b, 0:16] = r["vo"][CH:CH + 16].reshape(16, 8, 128)
        if j == 3:
            pool_prompt[0, b] = r["ptail"].reshape(128, 8, 15).transpose(2, 1, 0).reshape(15, 1024)
    r0 = res[0]
    y_sample = np.ascontiguousarray(r0["y"][CH + 16:CH + 48].reshape(8, 4, D))
    k_sample = np.ascontiguousarray(r0["ko"][CH + 16:CH + 48].reshape(1, 8, 4, 8, 128))
    v_sample = np.ascontiguousarray(r0["vo"][CH + 16:CH + 48].reshape(1, 8, 4, 8, 128))
    new = r0["psnew"].reshape(128, 8, 8, 4).transpose(2, 3, 1, 0).reshape(8, 4, 1024)
    old = r0["psold"].reshape(8, 11, 1024)
    pool_sample = np.ascontiguousarray(np.concatenate([old, new], 1)[None])
    return (y_prompt, y_sample, k_prompt, v_prompt, pool_prompt, k_sample, v_sample, pool_sample)


_CACHE = {}


def kernel(**inputs):
    cfg = finish_cfg(make_cfg())
    if "nc" not in _CACHE:
        _CACHE["nc"] = build(cfg)
    nc = _CACHE["nc"]
    maps = prep_inputs(cfg, inputs)
    res = run_bass_kernel_spmd(nc, maps, core_ids=list(range(8)))
    return assemble(cfg, res.results)
```

```python
import numpy as np
import concourse.bass as bass
import concourse.mybir as mybir
from concourse.bass_utils import run_bass_kernel_spmd

F32 = mybir.dt.float32
BF16 = mybir.dt.bfloat16
I32 = mybir.dt.int32
ALU = mybir.AluOpType
AF = mybir.ActivationFunctionType

D = 2048
KC = 16
NH = 8
EPS = 1e-6
NEG = -30000.0
SLOPES = [2.0 ** (-8.0 * (h + 1) / NH) for h in range(NH)]
LAM_INIT = 0.2
WINS = (2, 4, 8, 16)


class Buf:
    __slots__ = ("name", "w", "r")

    def __init__(self, name):
        self.name = name
        self.w = None
        self.r = []


class Prog:
    def __init__(self):
        self.recs = []
        self.nslot = {"sp": 8, "pool": 8}

    def barrier(self):
        self.recs.append(("sp", [], [], [], "bar"))

    def add(self, eng, fns, reads=(), writes=(), kind="c"):
        if not isinstance(fns, (list, tuple)):
            fns = [fns]
        self.recs.append((eng, list(fns), list(reads), list(writes), kind))

    def emit(self, nc, sems, block_engines):
        cnt = {e: 0 for e in ("pe", "act", "dve", "pool")}
        slot_use = {q: [0] * n for q, n in self.nslot.items()}
        slot_next = {q: 0 for q in self.nslot}
        ncc = 0
        waited = {}
        streams = {e: [] for e in ("pe", "act", "dve", "pool", "sp")}

        def need(E, tok):
            if tok is None:
                return
            key = (E, tok[0])
            if waited.get(key, 0) >= tok[1]:
                return
            waited[key] = tok[1]
            streams[E].append(("w", tok[0], tok[1]))

        cc_done = []
        for eng, fns, reads, writes, kind in self.recs:
            E = eng
            if kind == "bar":
                alltok = [(e, cnt[e]) for e in cnt if cnt[e]]
                for q, n in self.nslot.items():
                    for s in range(n):
                        if slot_use[q][s]:
                            alltok.append((f"{q}{s}", 16 * slot_use[q][s]))
                alltok += cc_done
                for E2 in streams:
                    for t in alltok:
                        if t[0] == E2:
                            continue
                        need(E2, t)
                continue
            toks = []
            for b in reads:
                toks.append(b.w)
            for b in writes:
                toks.append(b.w)
                toks.extend(b.r)
            if kind == "c":
                cnt[E] += 1
                tok = (E, cnt[E])
                inc = 1
            elif kind == "d":
                q = E
                s = slot_next[q]
                slot_next[q] = (s + 1) % self.nslot[q]
                prev = slot_use[q][s]
                if prev:
                    toks.append((f"{q}{s}", 16 * prev))
                slot_use[q][s] = prev + 1
                tok = (f"{q}{s}", 16 * (prev + 1))
                inc = 16
            else:
                tok = (f"cc{ncc}", 1)
                cc_done.append(tok)
                ncc += 1
                inc = 1
            for t in toks:
                if t is None:
                    continue
                if t[0] == E and E == "pe":
                    continue
                need(E, t)
            streams[E].append(("i", fns, tok[0], inc))
            for b in writes:
                b.w = tok
                b.r = []
            for b in reads:
                if b not in writes:
                    b.r.append(tok)
        final = []
        for q, n in self.nslot.items():
            for s in range(n):
                if slot_use[q][s]:
                    final.append((f"{q}{s}", 16 * slot_use[q][s]))
        for t in final:
            need("sp", t)
        for e in ("pe", "act", "dve", "pool"):
            if cnt[e]:
                need("sp", (e, cnt[e]))
        return streams


def run_streams(streams, sems, engs):
    for e, lst in streams.items():
        eng = engs[e]
        for it in lst:
            if it[0] == "w":
                eng.wait_ge(sems[it[1]], it[2])
            else:
                _, fns, semname, inc = it
                last = None
                for f in fns:
                    last = f(eng)
                last.then_inc(sems[semname], inc)


def make_cfg(seq=4096, dff=5632, page=128, npool=1280):
    ch = seq // 4
    return dict(SEQ=seq, CH=ch, DFF=dff, PAGE=page, NPOOL=npool, NT=ch + 48, NSUB=page // 16)


def token_tiles(cfg):
    ch = cfg["CH"]
    tl = [(i * 512, 512) for i in range(ch // 512)]
    tl.append((ch, 48))
    return tl


GROUPS4 = [[0, 1, 2, 3], [4, 5, 6, 7]]
PAIRS = [[0, 4], [1, 5], [2, 6], [3, 7]]


def build(cfg):
    from contextlib import ExitStack
    CH, DFF, PAGE, NT, NPOOL, NSUB = cfg["CH"], cfg["DFF"], cfg["PAGE"], cfg["NT"], cfg["NPOOL"], cfg["NSUB"]
    NF = DFF // 128
    TT = token_tiles(cfg)
    NTT = len(TT)
    NKT = CH // 128
    NQT = CH // 512
    nc = bass.Bass("TRN2", target_bir_lowering=False, num_devices=8)
    P = Prog()
    CO = cfg["OFF"]

    def din(name, shape, dt=F32):
        return nc.dram_tensor(name, list(shape), dt, kind="ExternalInput").ap()

    def dout(name, shape, dt=F32):
        return nc.dram_tensor(name, list(shape), dt, kind="ExternalOutput").ap()

    def dint(name, shape, dt):
        return nc.dram_tensor(name, list(shape), dt, kind="Internal").ap()

    xin = din("xin", [NT, D])
    wts = {}
    for nm, shp in (("w_gate1", [D, DFF]), ("w_up1", [D, DFF]), ("w_down1", [DFF, D]), ("w_in", [D, 4096]),
                    ("w_out", [D, D]), ("w_gate2", [D, DFF]), ("w_up2", [D, DFF]), ("w_down2", [DFF, D]),
                    ("w_pool", [1024, 256]), ("w_in_hd", [D, 384])):
        wts[nm] = din(nm, shp)
    cvec = din("cvec", [128, 72])
    gain_bc = din("gain_bc", [128, 128])
    lamv = din("lamv", [1, 256])
    cks = [din(f"ck{u}", [NPOOL, 2048]) for u in range(NSUB)]
    cvs = [din(f"cv{u}", [NPOOL, 2048]) for u in range(NSUB)]
    ptab = din("ptab", [128, 8], I32)
    spool = din("spool", [120, 1024])
    cf32 = din("cf32", [128, cfg["NCF"]])
    cbf = din("cbf", [128, cfg["NCB"]], BF16)

    y_o = dout("y", [NT, D])
    k_o = dout("ko", [NT, 1024])
    v_o = dout("vo", [NT, 1024])
    pt_o = dout("ptail", [128, 120])
    psn_o = dout("psnew", [128, 256])
    pso_o = dout("psold", [88, 1024])

    src_kT = [dint(f"src_kT{t}", [1024, 512], BF16) for t in range(NQT)]
    gat_kT = [dint(f"gat_kT{t}", [4096, 512], BF16) for t in range(NQT)]
    src_V = [dint(f"src_V{t}", [512, 1024], BF16) for t in range(NQT)]
    gat_V = [dint(f"gat_V{t}", [2048, 1024], BF16) for t in range(NQT)]
    src_ph = dint("src_ph", [128, 128], F32)
    gat_ph = dint("gat_ph", [512, 128], F32)
    scr_q = dint("scr_q", [1024, CH + 16], BF16)
    scr_mk = dint("scr_mk", [1024, 16], BF16)
    scr_mv = dint("scr_mv", [16, 1024], BF16)
    src_o = dint("src_o", [32, 128], F32)
    gat_o4 = dint("gat_o4", [128, 128], F32)
    gat_o8 = dint("gat_o8", [256, 128], F32)
    B = {n: Buf(n) for n in ("src_kT", "gat_kT", "src_V", "gat_V", "src_ph", "gat_ph", "scr_q", "scr_mk", "scr_mv",
                             "src_o", "gat_o4", "gat_o8", "out")}

    es = ExitStack()

    uniq = [0]

    def sb(name, shape, dt=F32, st=None):
        uniq[0] += 1
        return (st or es).enter_context(nc.sbuf_tensor(f"{name}_{uniq[0]}", list(shape), dt))

    class Rot:
        def __init__(self, name, shape, dt, n, st=None):
            self.t = [sb(f"{name}{i}", shape, dt, st) for i in range(n)]
            self.b = [Buf(f"{name}{i}") for i in range(n)]
            self.i = 0

        def get(self):
            k = self.i % len(self.t)
            self.i += 1
            return self.t[k], self.b[k]

    with es:
        sems = {}
        for nm in (["pe", "act", "dve", "pool"] + [f"sp{i}" for i in range(8)] + [f"pool{i}" for i in range(8)]
                   + [f"cc{i}" for i in range(8)]):
            sems[nm] = es.enter_context(nc.semaphore("s_" + nm))
        psum = [es.enter_context(nc.psum_tensor(f"ps{i}", [128, 512], F32)) for i in range(7)]
        psb = [Buf(f"ps{i}") for i in range(7)]
        psbf = es.enter_context(nc.psum_tensor("psbf", [128, 1024], BF16))
        b_psbf = Buf("psbf")
        pcount = [0]

        def next_ps():
            i = pcount[0] % 7
            pcount[0] += 1
            return psum[i], psb[i]

        SKIP = cfg.get("SKIP") or ""
        phase = [""]

        def dma(q, out, in_, reads, writes):
            if phase[0] == "A" and "d" in SKIP:
                return
            P.add(q, lambda e, o=out, i=in_: e.dma_start(out=o, in_=i), reads, writes, kind="d")

        def cc(groups, src, dst, reads, writes):
            P.add("pool", lambda e: e.collective_compute("AllGather", ALU.bypass, replica_groups=groups,
                                                          ins=[src], outs=[dst]), reads, writes, kind="cc")

        xT = sb("xT", [128, KC, NT])
        xTb = [[Buf(f"x{k}_{t}") for t in range(NTT)] for k in range(KC)]
        hT = sb("hT", [128, KC, NT], BF16)
        hTb = [[Buf(f"h{k}_{t}") for t in range(NTT)] for k in range(KC)]
        cv_sb = sb("cv_sb", [128, 72]); b_cv = Buf("cv")
        cf_sb = sb("cf_sb", [128, cfg["NCF"]]); b_cf = Buf("cf")
        cb_sb = sb("cb_sb", [128, cfg["NCB"]], BF16); b_cb = Buf("cb")
        gain_sb = sb("gain_sb", [128, 128]); b_gain = Buf("gain")
        pt_sb = sb("pt_sb", [128, 8], I32); b_pt = Buf("pt")
        lam_sb = sb("lam_sb", [128, 4]); b_lam = Buf("lam")
        sqr = Rot("sq", [128, 512], BF16, 3)
        rstd_r = Rot("rstd", [128, 512], F32, 2)

        def cf(name, w=1, o=0):
            return cf_sb[:, CO[name] + o:CO[name] + o + w]

        def cb(name, w, o=0):
            return cb_sb[:, CO[name] + o:CO[name] + o + w]

        ident_f = cf("ident", 128)
        ident_b = cb("identb", 128)
        ones_b = cb("onesb", 128)
        eps_col = cf("eps")

        dma("sp", cv_sb[:], cvec[:, :], [], [b_cv])
        dma("sp", cf_sb[:], cf32[:, :], [], [b_cf])
        dma("sp", cb_sb[:], cbf[:, :], [], [b_cb])
        dma("sp", gain_sb[:], gain_bc[:, :], [], [b_gain])
        dma("sp", pt_sb[:], ptab[:, :], [], [b_pt])

        evac_flip = [0]

        def evac(out, in_, reads, writes, scale=None, eng=None):
            evac_flip[0] ^= 1
            if eng is None:
                eng = "act" if evac_flip[0] else "dve"
            if eng == "act":
                if scale is None:
                    P.add("act", lambda e: e.activation(out=out, in_=in_, func=AF.Copy), reads, writes)
                else:
                    P.add("act", lambda e: e.activation(out=out, in_=in_, func=AF.Copy, scale=scale), reads, writes)
            else:
                if scale is None:
                    P.add("dve", lambda e: e.tensor_copy(out=out, in_=in_), reads, writes)
                else:
                    P.add("dve", lambda e: e.tensor_scalar(out=out, in0=in_, scalar1=scale, scalar2=None, op0=ALU.mult),
                          reads, writes)

        def tile_of(tok):
            for ti, (t0, n) in enumerate(TT):
                if t0 <= tok < t0 + n:
                    return ti
            raise ValueError

        with ExitStack() as st:
            xtok = Rot("xtok", [128, D], F32, 2, st)
            for tt in range((NT + 127) // 128):
                r0 = tt * 128
                nr = min(128, NT - r0)
                xt, xb = xtok.get()
                dma("sp", xt[0:nr, :], xin[r0:r0 + nr, :], [], [xb])
                ti = tile_of(r0)
                for k4 in range(KC // 4):
                    ps, pb = next_ps()
                    fns = []
                    for kk in range(4):
                        k = k4 * 4 + kk
                        fns.append(lambda e, k=k, kk=kk, ps=ps, xt=xt, nr=nr: e.transpose(
                            out=ps[:, kk * 128:kk * 128 + nr], in_=xt[0:nr, k * 128:(k + 1) * 128],
                            identity=ident_f[0:nr, 0:nr]))
                    P.add("pe", fns, [xb, b_cf], [pb])
                    evac(xT[:, k4 * 4:k4 * 4 + 4, r0:r0 + nr], ps[:].rearrange("p (a b) -> p a b", a=4)[:, :, 0:nr],
                         [pb], [xTb[k][ti] for k in range(k4 * 4, k4 * 4 + 4)])
            P.barrier()

        def rms_rstd(ti):
            t0, n = TT[ti]
            ps, pb = next_ps()
            for k in range(KC):
                sq, sqb = sqr.get()
                P.add("act", lambda e, sq=sq, k=k: e.activation(out=sq[:, 0:n], in_=xT[:, k, t0:t0 + n], func=AF.Square),
                      [xTb[k][ti]], [sqb])
                P.add("pe", lambda e, sq=sq, k=k, ps=ps: e.matmul(ps[:, 0:n], lhsT=ones_b, rhs=sq[:, 0:n],
                                                                 start=(k == 0), stop=(k == KC - 1)),
                      [sqb, b_cb] + ([pb] if k else []), [pb])
            rs, rsb = rstd_r.get()
            P.add("act", lambda e: e.activation(out=rs[:, 0:n], in_=ps[:, 0:n], func=AF.Sqrt, bias=eps_col, scale=1.0 / D),
                  [pb, b_cf], [rsb])
            P.add("dve", lambda e: e.reciprocal(out=rs[:, 0:n], in_=rs[:, 0:n]), [rsb], [rsb])
            return rs, rsb

        def rmsnorm_to_h(gbase):
            for ti, (t0, n) in enumerate(TT):
                rs, rsb = rms_rstd(ti)
                for k in range(KC):
                    P.add("dve", lambda e, k=k, rs=rs, t0=t0, n=n: e.scalar_tensor_tensor(
                        out=hT[:, k, t0:t0 + n], in0=xT[:, k, t0:t0 + n], scalar=cv_sb[:, gbase + k:gbase + k + 1],
                        in1=rs[:, 0:n], op0=ALU.mult, op1=ALU.mult),
                        [xTb[k][ti], rsb, b_cv], [hTb[k][ti]])

        def wload(dst, w2d, n0, wdt, nk):
            return lambda e: e.dma_start(out=dst, in_=w2d.rearrange("(kc p) n -> p kc n", p=128)[:, 0:nk, n0:n0 + wdt])

        FP = 8

        def ffn(gbase, wg, wu, wd):
            with ExitStack() as st:
                wgu = Rot("wgu", [128, KC, 256], BF16, 4, st)
                hid = sb("hid", [128, FP, NT], BF16, st)
                hidb = [[Buf(f"hid{f}_{t}") for t in range(NTT)] for f in range(FP)]
                wdn = Rot("wdn", [128, FP, 512], BF16, 2, st)
                sgr = Rot("sg", [128, 512], F32, 2, st)
                rmsnorm_to_h(gbase)
                f0 = 0
                while f0 < NF:
                    nfp = min(FP, NF - f0)
                    for fp in range(0, nfp, 2):
                        wgt, wgb = wgu.get()
                        P.add("pool", wload(wgt[:], wg, (f0 + fp) * 128, 256, KC), [], [wgb], kind="d")
                        wut, wub = wgu.get()
                        P.add("pool", wload(wut[:], wu, (f0 + fp) * 128, 256, KC), [], [wub], kind="d")
                        for ff in range(2):
                            fi = fp + ff
                            for ti, (t0, n) in enumerate(TT):
                                psA, pbA = next_ps()
                                P.add("pe", [lambda e, k=k, psA=psA, wgt=wgt, ff=ff, t0=t0, n=n: e.matmul(
                                    psA[:, 0:n], lhsT=wgt[:, k, ff * 128:(ff + 1) * 128], rhs=hT[:, k, t0:t0 + n],
                                    start=(k == 0), stop=(k == KC - 1)) for k in range(KC)],
                                    [wgb] + [hTb[k][ti] for k in range(KC)], [pbA])
                                psB, pbB = next_ps()
                                P.add("pe", [lambda e, k=k, psB=psB, wut=wut, ff=ff, t0=t0, n=n: e.matmul(
                                    psB[:, 0:n], lhsT=wut[:, k, ff * 128:(ff + 1) * 128], rhs=hT[:, k, t0:t0 + n],
                                    start=(k == 0), stop=(k == KC - 1)) for k in range(KC)],
                                    [wub] + [hTb[k][ti] for k in range(KC)], [pbB])
                                sg, sgb = sgr.get()
                                P.add("act", lambda e, sg=sg, psA=psA, n=n: e.activation(out=sg[:, 0:n], in_=psA[:, 0:n],
                                                                                        func=AF.Silu), [pbA], [sgb])
                                P.add("dve", lambda e, sg=sg, psB=psB, fi=fi, t0=t0, n=n: e.tensor_tensor(
                                    out=hid[:, fi, t0:t0 + n], in0=sg[:, 0:n], in1=psB[:, 0:n], op=ALU.mult),
                                    [sgb, pbB], [hidb[fi][ti]])
                    for o4 in range(4):
                        wdt_, wdb = wdn.get()
                        P.add("pool", lambda e, wdt_=wdt_, f0=f0, nfp=nfp, o4=o4: e.dma_start(
                            out=wdt_[:, 0:nfp, :],
                            in_=wd.rearrange("(fc p) n -> p fc n", p=128)[:, f0:f0 + nfp, o4 * 512:(o4 + 1) * 512]),
                            [], [wdb], kind="d")
                        for oo in range(4):
                            oc = o4 * 4 + oo
                            for ti, (t0, n) in enumerate(TT):
                                ps, pb = next_ps()
                                P.add("pe", [lambda e, f=f, ps=ps, wdt_=wdt_, oo=oo, t0=t0, n=n, nfp=nfp: e.matmul(
                                    ps[:, 0:n], lhsT=wdt_[:, f, oo * 128:(oo + 1) * 128], rhs=hid[:, f, t0:t0 + n],
                                    start=(f == 0), stop=(f == nfp - 1)) for f in range(nfp)],
                                    [wdb] + [hidb[f][ti] for f in range(nfp)], [pb])
                                P.add("dve", lambda e, ps=ps, oc=oc, t0=t0, n=n: e.scalar_tensor_tensor(
                                    out=xT[:, oc, t0:t0 + n], in0=ps[:, 0:n], scalar=0.5, in1=xT[:, oc, t0:t0 + n],
                                    op0=ALU.mult, op1=ALU.add), [pb, xTb[oc][ti]], [xTb[oc][ti]])
                    f0 += nfp
                P.barrier()

        def mixer(_):
            mst = ExitStack()
            Qblk = sb("Qblk_", [128, 8, 8], BF16); b_Q = Buf("Qblk")
            ksT = sb("ksT_", [128, 32], BF16); b_ksT = Buf("ksT")
            vsf = sb("vsf_", [128, 32], F32); b_vsf = Buf("vsf")
            Vnew = sb("Vnew_", [4, 8, 128], BF16); b_Vn = Buf("Vnew")
            gain08 = sb("gain08_", [128, 128], F32); b_g08 = Buf("g08")
            lv = sb("lv_", [1, 256], F32); b_lv = Buf("lv")
            lt = sb("lt_", [1, 8], F32); b_lt = Buf("lt")
            Cm = sb("Cm_", [8, 4], F32); b_Cm = Buf("Cm")
            pAll = sb("pAll", [128, 8, CH + 15], F32, mst); b_pA = [Buf(f"pA{k}") for k in range(8)]
            pM = sb("pM", [128, 8, 31], F32, mst); b_pM = Buf("pM")
            pS = sb("pS", [128, 8, 152], F32, mst); b_pS = Buf("pS")
            P.add("dve", lambda e: e.memset(pM[:], 0.0), [], [b_pM])
            P.add("dve", lambda e: e.memset(Qblk[:], 0.0), [], [b_Q])
            P.add("dve", lambda e: e.tensor_scalar(out=gain08[:], in0=gain_sb[:], scalar1=1.0 - LAM_INIT, scalar2=None,
                                                   op0=ALU.mult), [b_gain], [b_g08])
            dma("sp", lv[:], lamv[:, :], [], [b_lv])
            P.add("dve", lambda e: e.tensor_tensor(out=lv[0:1, 0:64], in0=lv[0:1, 0:64], in1=lv[0:1, 64:128], op=ALU.mult),
                  [b_lv], [b_lv])
            P.add("dve", lambda e: e.tensor_tensor(out=lv[0:1, 128:192], in0=lv[0:1, 128:192], in1=lv[0:1, 192:256],
                                                   op=ALU.mult), [b_lv], [b_lv])
            P.add("dve", lambda e: e.reduce_sum(out=lt[0:1, 0:1], in_=lv[0:1, 0:64], axis=mybir.AxisListType.X), [b_lv], [b_lt])
            P.add("dve", lambda e: e.reduce_sum(out=lt[0:1, 1:2], in_=lv[0:1, 128:192], axis=mybir.AxisListType.X),
                  [b_lv, b_lt], [b_lt])
            P.add("act", lambda e: e.activation(out=lt[0:1, 2:4], in_=lt[0:1, 0:2], func=AF.Exp), [b_lt], [b_lt])
            P.add("dve", lambda e: e.tensor_tensor(out=lt[0:1, 4:5], in0=lt[0:1, 2:3], in1=lt[0:1, 3:4], op=ALU.subtract),
                  [b_lt], [b_lt])
            P.add("dve", lambda e: e.tensor_scalar(out=lt[0:1, 5:6], in0=lt[0:1, 4:5], scalar1=LAM_INIT, scalar2=None,
                                                   op0=ALU.add), [b_lt], [b_lt])
            ps, pb = next_ps()
            P.add("pe", lambda e, ps=ps: e.matmul(ps[:, 0:1], lhsT=cf("onesf", 128)[0:1, :], rhs=lt[0:1, 5:6],
                                                  start=True, stop=True), [b_lt, b_cf], [pb])
            P.add("dve", lambda e, ps=ps: e.tensor_copy(out=lam_sb[:, 0:1], in_=ps[:, 0:1]), [pb], [b_lam])
            P.add("dve", lambda e, ps=ps: e.tensor_scalar(out=lam_sb[:, 1:2], in0=ps[:, 0:1], scalar1=-1.0, scalar2=None,
                                                          op0=ALU.mult), [pb, b_lam], [b_lam])
            P.add("dve", lambda e: e.scalar_tensor_tensor(out=Cm[:], in0=cf("cb1", 4)[0:8, :], scalar=lam_sb[0:8, 1:2],
                                                          in1=cf("cb0", 4)[0:8, :], op0=ALU.mult, op1=ALU.add),
                  [b_lam, b_cf], [b_Cm])

            if cfg.get("STOP") == "L":
                mst.close(); return
            rmsnorm_to_h(16)
            XT = NTT - 1
            with ExitStack() as st:
                win = Rot("win", [128, KC, 256], BF16, 3, st)
                kfr = Rot("kf", [128, 512], F32, 2, st)
                k16 = Rot("k16", [128, 512], BF16, 2, st)
                kst = Rot("kst", [128, 4, 128], F32, 2, st)
                vst = Rot("vst", [128, 4, 128], BF16, 2, st)
                for t_, b_ in zip(vst.t, vst.b):
                    P.add("dve", lambda e, t_=t_: e.memset(t_[:], 1.0), [], [b_])

                phase[0] = "A"

                def tok_major_out(ft, fb, n, t0, h, dst_o, want_bf):
                    if "t" in SKIP:
                        return
                    ps2, pb2 = next_ps()
                    nsub = (n + 127) // 128
                    rows = min(128, n)
                    P.add("pe", [lambda e, j=j, ps2=ps2: e.transpose(
                        out=ps2[0:min(128, n - j * 128), j * 128:(j + 1) * 128],
                        in_=ft[:, j * 128:j * 128 + min(128, n - j * 128)], identity=ident_f) for j in range(nsub)],
                        [fb, b_cf], [pb2])
                    kt_, ktb = kst.get()
                    src = ps2[0:rows, 0:nsub * 128].rearrange("p (j d) -> p j d", d=128)
                    evac(kt_[0:rows, 0:nsub, :], src, [pb2], [ktb], eng="act")
                    if n == 512:
                        dma("sp", dst_o[t0:t0 + 512, h * 128:(h + 1) * 128].rearrange("(j p) d -> p j d", p=128), kt_[:, :, :],
                            [ktb], [B["out"]])
                    else:
                        dma("sp", dst_o[t0:t0 + n, h * 128:(h + 1) * 128], kt_[0:n, 0, :], [ktb], [B["out"]])
                    if want_bf:
                        vt_, vtb = vst.get()
                        P.add("dve", lambda e, vt_=vt_, kt_=kt_: e.tensor_copy(out=vt_[0:rows, 0:nsub, 0:128], in_=kt_[0:rows, 0:nsub, :]),
                              [ktb], [vtb])
                        if n == 512:
                            dma("sp", src_V[t0 // 512][:, h * 128:(h + 1) * 128].rearrange("(j p) c -> p j c", p=128),
                                vt_[:, :, :], [vtb], [B["src_V"]])
                        else:
                            dma("sp", scr_mv[0:16, h * 128:(h + 1) * 128], vt_[0:16, 0, :], [vtb], [B["scr_mv"]])

                order = list(range(0, 8)) + list(range(16, 32)) + list(range(8, 16))
                for oi in range(0, 32, 2):
                    oc0 = order[oi]
                    wt, wb = win.get()
                    P.add("pool", wload(wt[:], wts["w_in"], oc0 * 128, 256, KC), [], [wb], kind="d")
                    for ff in range(2):
                        oc = oc0 + ff
                        for ti, (t0, n) in enumerate(TT):
                            ps, pb = next_ps()
                            P.add("pe", [lambda e, k=k, ps=ps, wt=wt, ff=ff, t0=t0, n=n: e.matmul(
                                ps[:, 0:n], lhsT=wt[:, k, ff * 128:(ff + 1) * 128], rhs=hT[:, k, t0:t0 + n],
                                start=(k == 0), stop=(k == KC - 1)) for k in range(KC)],
                                [wb] + [hTb[k][ti] for k in range(KC)], [pb])
                            if (oc < 8 and "P" in SKIP) or (8 <= oc < 16 and "Q" in SKIP) or (16 <= oc < 24 and "K" in SKIP) or (oc >= 24 and "V" in SKIP):
                                continue
                            if oc < 8:
                                if ti < XT:
                                    evac(pAll[:, oc, 15 + t0:15 + t0 + n], ps[:, 0:n], [pb], [b_pA[oc]])
                                else:
                                    evac(pM[:, oc, 15:31], ps[:, 0:16], [pb], [b_pM], eng="act")
                                    evac(pS[:, oc, :].rearrange("p (s c) -> p s c", c=19)[:, :, 15:19],
                                         ps[:, 16:48].rearrange("p (s q) -> p s q", q=4), [pb], [b_pS], eng="act")
                            elif oc < 16:
                                h = oc - 8
                                qt_, qb_ = k16.get()
                                evac(qt_[:, 0:n], ps[:, 0:n], [pb], [qb_], scale=0.125)
                                if ti < XT:
                                    dma("sp", scr_q[h * 128:(h + 1) * 128, t0:t0 + n], qt_[:, 0:n], [qb_], [B["scr_q"]])
                                else:
                                    dma("sp", scr_q[h * 128:(h + 1) * 128, CH:CH + 16], qt_[:, 0:16], [qb_], [B["scr_q"]])
                            elif oc < 24:
                                h = oc - 16
                                kf_, kfb = kfr.get()
                                evac(kf_[:, 0:n], ps[:, 0:n], [pb], [kfb], eng="act")
                                kb_, kbb = k16.get()
                                P.add("dve", lambda e, kb_=kb_, kf_=kf_, n=n: e.tensor_copy(out=kb_[:, 0:n], in_=kf_[:, 0:n]),
                                      [kfb], [kbb])
                                if ti < XT:
                                    dma("sp", src_kT[ti][h * 128:(h + 1) * 128, :], kb_[:, 0:n], [kbb], [B["src_kT"]])
                                else:
                                    dma("sp", scr_mk[h * 128:(h + 1) * 128, 0:16], kb_[:, 0:16], [kbb], [B["scr_mk"]])
                                tok_major_out(kf_, kfb, n, t0, h, k_o, False)
                            else:
                                h = oc - 24
                                vf_, vfb = kfr.get()
                                evac(vf_[:, 0:n], ps[:, 0:n], [pb], [vfb])
                                tok_major_out(vf_, vfb, n, t0, h, v_o, True)
                phase[0] = ""
                if cfg.get("STOP") == "A0":
                    P.barrier(); st.close(); mst.close(); return
                for part in range(3):
                    wt, wb = win.get()
                    P.add("pool", wload(wt[:, :, 0:128], wts["w_in_hd"], part * 128, 128, KC), [], [wb], kind="d")
                    ps, pb = next_ps()
                    P.add("pe", [lambda e, k=k, ps=ps, wt=wt: e.matmul(
                        ps[:, 0:32], lhsT=wt[:, k, 0:128], rhs=hT[:, k, CH + 16:CH + 48],
                        start=(k == 0), stop=(k == KC - 1)) for k in range(KC)],
                        [wb] + [hTb[k][XT] for k in range(KC)], [pb])
                    if part == 0:
                        for m_ in range(2):
                            P.add("act", lambda e, ps=ps, m_=m_: e.activation(
                                out=Qblk[64 * m_:64 * m_ + 64, :, 4 * m_:4 * m_ + 4],
                                in_=ps[64 * m_:64 * m_ + 64, 0:32].rearrange("p (s q) -> p s q", q=4),
                                func=AF.Copy, scale=0.125), [pb, b_Q], [b_Q])
                    elif part == 1:
                        evac(ksT[:, :], ps[:, 0:32], [pb], [b_ksT])
                    else:
                        evac(vsf[:, :], ps[:, 0:32], [pb], [b_vsf])
                        for half in range(2):
                            ps2, pb2 = next_ps()
                            P.add("pe", [lambda e, s4=s4, ps2=ps2, half=half: e.transpose(
                                out=ps2[0:4, s4 * 128:(s4 + 1) * 128], in_=vsf[:, (half * 4 + s4) * 4:(half * 4 + s4) * 4 + 4],
                                identity=ident_f) for s4 in range(4)], [b_vsf, b_cf], [pb2])
                            evac(Vnew[0:4, half * 4:half * 4 + 4, :], ps2[0:4, :].rearrange("p (s d) -> p s d", d=128),
                                 [pb2, b_Vn], [b_Vn])
                if cfg.get("STOP") == "A1":
                    P.barrier(); st.close(); mst.close(); return
                phst = kfr.t[0]; b_phst = kfr.b[0]
                P.add("dve", lambda e: e.memset(phst[:, 0:128], 0.0), [b_phst], [b_phst])
                P.add("dve", lambda e: e.tensor_copy(out=phst[:, 0:120].rearrange("p (k c) -> p k c", c=15), in_=pAll[:, :, CH:CH + 15]),
                      [b_phst] + b_pA, [b_phst])
                dma("sp", src_ph[:, :], phst[:, 0:128], [b_phst], [B["src_ph"]])
                dma("sp", pt_o.rearrange("p (k c) -> p k c", c=15), pAll[:, :, CH:CH + 15], b_pA, [B["out"]])
                for t in range(NQT):
                    cc(GROUPS4, src_kT[t][:, :], gat_kT[t][:, :], [B["src_kT"]], [B["gat_kT"]])
                    cc(GROUPS4, src_V[t][:, :], gat_V[t][:, :], [B["src_V"]], [B["gat_V"]])
                cc(GROUPS4, src_ph[:, :], gat_ph[:, :], [B["src_ph"]], [B["gat_ph"]])
                P.barrier()

            if cfg.get("STOP") == "A":
                mst.close(); return
            with ExitStack() as st:
                gph = sb("gph", [128, 4, 128], F32, st); b_gph = Buf("gph")
                spt = sb("spt", [120, 1024], F32, st); b_spt = Buf("spt")
                W1 = sb("W1", [128, 8, 271], F32, st); b_W1 = Buf("W1")
                W2 = sb("W2", [128, 8, 271], F32, st); b_W2 = Buf("W2")
                feat = sb("feat", [128, 8, 256], BF16, st); b_feat = Buf("feat")
                featS = sb("featS", [128, 8, 32], BF16, st); b_featS = Buf("featS")
                psc = sb("psc", [128, 8, 32], F32, st); b_psc = Buf("psc")
                wp_sb = sb("wp_sb", [128, 8, 256], BF16, st); b_wp = Buf("wp")
                P.add("pool", wload(wp_sb[:], wts["w_pool"], 0, 256, 8), [], [b_wp], kind="d")
                dma("sp", gph[:], gat_ph.rearrange("(r p) c -> p r c", p=128), [B["gat_ph"]], [b_gph])
                dma("sp", spt[:], spool[:, :], [], [b_spt])
                dma("sp", pso_o.rearrange("(s r) c -> s r c", r=11), spool.rearrange("(s r) c -> s r c", r=15)[:, 4:15, :],
                    [], [B["out"]])
                halo = pAll[:, :, 0:15]
                P.add("dve", lambda e: e.tensor_scalar(out=halo, in0=pM[:, :, 16:31], scalar1=cf("sel", 1, 4), scalar2=None,
                                                       op0=ALU.mult), [b_pM, b_cf] + b_pA, b_pA)
                for r in range(4):
                    P.add("dve", lambda e, r=r: e.scalar_tensor_tensor(
                        out=halo, in0=gph[:, r, 0:120].rearrange("p (k c) -> p k c", c=15), scalar=cf("sel", 1, r), in1=halo,
                        op0=ALU.mult, op1=ALU.add), [b_gph, b_cf] + b_pA, b_pA)
                for k4 in range(2):
                    ps, pb = next_ps()
                    P.add("pe", [lambda e, kk=kk, k4=k4, ps=ps: e.transpose(
                        out=ps[:, kk * 128:kk * 128 + 120], in_=spt[0:120, (k4 * 4 + kk) * 128:(k4 * 4 + kk + 1) * 128],
                        identity=ident_f[0:120, 0:120]) for kk in range(4)], [b_spt, b_cf], [pb])
                    for kk in range(4):
                        evac(pS[:, k4 * 4 + kk, :].rearrange("p (s c) -> p s c", c=19)[:, :, 0:15],
                             ps[:, kk * 128:kk * 128 + 120].rearrange("p (s c) -> p s c", c=15), [pb, b_pS], [b_pS], eng="act")
                for s in range(8):
                    P.add("dve", lambda e, s=s: e.tensor_copy(out=psc[:, :, s * 4:s * 4 + 4], in_=pS[:, :, s * 19 + 15:s * 19 + 19]),
                          [b_pS, b_psc], [b_psc])
                dma("sp", psn_o.rearrange("p (k t) -> p k t", k=8), psc[:, :, :], [b_psc], [B["out"]])

                def windows(X, L, xbufs):
                    TTa = mybir.AluOpType.add
                    P.add("dve", lambda e: e.tensor_tensor(out=W1[:, 0:8, 1:L], in0=X[:, 0:8, 1:L], in1=X[:, 0:8, 0:L - 1], op=TTa),
                          xbufs + [b_W1], [b_W1])
                    P.add("dve", lambda e: e.tensor_tensor(out=W2[:, 2:8, 3:L], in0=W1[:, 2:8, 3:L], in1=W1[:, 2:8, 1:L - 2], op=TTa),
                          [b_W1, b_W2], [b_W2])
                    P.add("dve", lambda e: e.tensor_tensor(out=W1[:, 4:8, 7:L], in0=W2[:, 4:8, 7:L], in1=W2[:, 4:8, 3:L - 4], op=TTa),
                          [b_W2, b_W1], [b_W1])
                    P.add("dve", lambda e: e.tensor_tensor(out=W2[:, 6:8, 15:L], in0=W1[:, 6:8, 15:L], in1=W1[:, 6:8, 7:L - 8], op=TTa),
                          [b_W1, b_W2], [b_W2])
                    return [W1, W2, W1, W2]

                def pool_mm(ft, fbuf, n, c0, ti):
                    for g in range(4):
                        for dc in range(2):
                            ps, pb = next_ps()
                            P.add("pe", [lambda e, cc_=cc_, ps=ps, g=g, dc=dc: e.matmul(
                                ps[:, 0:n], lhsT=wp_sb[:, 2 * g + cc_, dc * 128:(dc + 1) * 128], rhs=ft[:, 2 * g + cc_, 0:n],
                                start=(cc_ == 0), stop=(cc_ == 1)) for cc_ in range(2)], [b_wp, fbuf], [pb])
                            oc = 2 * g + dc
                            evac(hT[:, oc, c0:c0 + n], ps[:, 0:n], [pb, b_cv, hTb[oc][ti]], [hTb[oc][ti]],
                                 scale=cv_sb[:, 64 + oc:65 + oc])

                for t0 in range(0, CH, 256):
                    ti, n = t0 // 512, 256
                    X = pAll[:, :, t0:t0 + n + 15]
                    res = windows(X, n + 15, list(b_pA))
                    for g in range(4):
                        P.add("dve", lambda e, g=g, R=res[g], X=X, n=n: e.scalar_tensor_tensor(
                            out=feat[:, 2 * g:2 * g + 2, 0:n], in0=R[:, 2 * g:2 * g + 2, 15:15 + n], scalar=1.0 / WINS[g],
                            in1=X[:, 2 * g:2 * g + 2, 15:15 + n], op0=ALU.mult, op1=ALU.subtract),
                            [b_W1, b_W2, b_feat] + b_pA, [b_feat])
                    pool_mm(feat, b_feat, n, t0, ti)
                res = windows(pM, 31, [b_pM])
                for g in range(4):
                    P.add("dve", lambda e, g=g, R=res[g]: e.tensor_tensor(
                        out=R[:, 2 * g:2 * g + 2, 15:31], in0=R[:, 2 * g:2 * g + 2, 15:31],
                        in1=cf("invc", 128).rearrange("p (k t) -> p k t", t=16)[:, 2 * g:2 * g + 2, :], op=ALU.mult),
                        [b_W1, b_W2, b_cf], [b_W1, b_W2])
                    P.add("dve", lambda e, g=g, R=res[g]: e.tensor_tensor(
                        out=feat[:, 2 * g:2 * g + 2, 0:16], in0=R[:, 2 * g:2 * g + 2, 15:31], in1=pM[:, 2 * g:2 * g + 2, 15:31],
                        op=ALU.subtract), [b_W1, b_W2, b_pM, b_feat], [b_feat])
                pool_mm(feat, b_feat, 16, CH, XT)
                res = windows(pS, 152, [b_pS])
                for g in range(4):
                    P.add("dve", lambda e, g=g, R=res[g]: e.scalar_tensor_tensor(
                        out=R[:, 2 * g:2 * g + 2, 15:152], in0=R[:, 2 * g:2 * g + 2, 15:152], scalar=1.0 / WINS[g],
                        in1=pS[:, 2 * g:2 * g + 2, 15:152], op0=ALU.mult, op1=ALU.subtract),
                        [b_W1, b_W2, b_pS], [b_W1, b_W2])
                    for s in range(8):
                        P.add("dve", lambda e, g=g, R=res[g], s=s: e.tensor_copy(
                            out=featS[:, 2 * g:2 * g + 2, s * 4:s * 4 + 4], in_=R[:, 2 * g:2 * g + 2, s * 19 + 15:s * 19 + 19]),
                            [b_W1, b_W2, b_featS], [b_featS])
                pool_mm(featS, b_featS, 32, CH + 16, XT)
                P.barrier()
            mst.close()

            if cfg.get("STOP") == "B1":
                return
            with ExitStack() as st:
                NVT = 5 * NKT + 1
                kTg = sb("kTg", [128, 4, CH], BF16, st)
                kTo = sb("kTo", [128, CH], BF16, st)
                kTm = sb("kTm", [128, 16], BF16, st)
                Vh = sb("Vh", [128, NVT, 129], BF16, st)
                qh = sb("qh", [128, CH + 16], BF16, st)
                qbt = sb("qbt", [128, CH], F32, st)
                tmpr = Rot("tmp", [128, 512], F32, 3, st)
                Ar = Rot("A", [128, 512], BF16, 4, st)
                on1 = sb("on1", [128, 4, 128], F32, st); b_on1 = Buf("on1")
                fin = sb("fin", [128, 4, 128], F32, st); b_fin = Buf("fin")
                sqt = sb("sqt", [128, 128], F32, st); b_sqt = Buf("sqt")
                rc = sb("rc", [128, 16], F32, st); b_rc = Buf("rc")
                resb = sb("resb", [128, 4, 128], BF16, st); b_resb = Buf("resb")
                b_kTg, b_kTo, b_kTm, b_Vg, b_Vo, b_Vm, b_qh, b_qbt = (Buf(x) for x in ("kTg", "kTo", "kTm", "Vg", "Vo", "Vm", "qh", "qbt"))
                P.add("dve", lambda e: e.memset(Vh[:, :, 128:129], 1.0), [], [b_Vg, b_Vo, b_Vm])
                acc_ps = [psum[5], psum[6]]
                acc_b = [psb[5], psb[6]]
                c5 = [0]

                def next_ps5():
                    i = c5[0] % 5
                    c5[0] += 1
                    return psum[i], psb[i]

                def attend(h, q0, nq, subs, keytiles, outc0, ti_out):
                    for m_ in range(2):
                        p0 = 64 * m_
                        nkts = len(keytiles)
                        pend = []
                        LA = 2
                        def emit_av(item):
                            A_, Ab, V_ap, vbuf, nk, ki = item
                            fns = []
                            if ki == 0:
                                rows0 = subs[0][1]
                                nb = (len(subs) + 2) // 3
                                for bi in range(nb):
                                    ncol = 129 * min(3, len(subs) - 3 * bi)
                                    fns.append(lambda e, bi=bi, ncol=ncol, rows0=rows0: e.matmul(
                                        acc_ps[bi][0:rows0, 0:ncol], lhsT=cb("zerob", 128)[:, 0:rows0], rhs=cb("maskd", 512)[:, 0:ncol],
                                        start=True, stop=False))
                            for si, (so, rows) in enumerate(subs):
                                accp = acc_ps[si // 3]
                                c0 = (si % 3) * 129
                                last_in_bank = (si % 3 == 2) or (si == len(subs) - 1)
                                fns.append(lambda e, A_=A_, so=so, rows=rows, accp=accp, c0=c0, V_ap=V_ap, nk=nk, ki=ki, lib=last_in_bank: e.matmul(
                                    accp[0:rows, c0:c0 + 129], lhsT=A_[0:nk, so:so + rows], rhs=V_ap[0:nk, :],
                                    start=False, stop=(ki == nkts - 1 and lib)))
                            P.add("pe", fns, [Ab, vbuf, b_cb] + acc_b, acc_b)

                        for ki, (kT_ap, kbuf, V_ap, vbuf, bias_ap, nk, mask_ap) in enumerate(keytiles):
                            ps, pb = next_ps5()
                            fns = [lambda e, ps=ps, kT_ap=kT_ap, nk=nk, mask_ap=mask_ap, p0=p0: e.matmul(
                                ps[0:nk, 0:nq], lhsT=kT_ap[p0:p0 + 64, 0:nk], rhs=qh[p0:p0 + 64, q0:q0 + nq],
                                start=True, stop=(mask_ap is None))]
                            if mask_ap is not None:
                                fns.append(lambda e, ps=ps, nk=nk, mask_ap=mask_ap: e.matmul(
                                    ps[0:nk, 0:nq], lhsT=ident_b[0:nk, 0:nk], rhs=mask_ap, start=False, stop=True))
                            P.add("pe", fns, [kbuf, b_qh, b_cb], [pb])
                            tm, tmb = tmpr.get()
                            P.add("dve", lambda e, tm=tm, ps=ps, nk=nk: e.tensor_tensor(
                                out=tm[0:nk, 0:nq], in0=ps[0:nk, 0:nq], in1=qbt[0:nk, 0:nq] if q0 >= CH else qbt[0:nk, q0:q0 + nq],
                                op=ALU.add), [pb, b_qbt], [tmb])
                            A_, Ab = Ar.get()
                            P.add("act", lambda e, A_=A_, tm=tm, nk=nk, bias_ap=bias_ap: e.activation(
                                out=A_[0:nk, 0:nq], in_=tm[0:nk, 0:nq], func=AF.Exp, bias=bias_ap, scale=1.0), [tmb, b_cf], [Ab])
                            pend.append((A_, Ab, V_ap, vbuf, nk, ki))
                            if len(pend) > LA:
                                emit_av(pend.pop(0))
                        while pend:
                            emit_av(pend.pop(0))
                        for si, (so, rows) in enumerate(subs):
                            accp = acc_ps[si // 3]
                            c0 = (si % 3) * 129
                            P.add("dve", lambda e, accp=accp, c0=c0, rows=rows, si=si: e.reciprocal(
                                out=rc[0:rows, si:si + 1], in_=accp[0:rows, c0 + 128:c0 + 129]), acc_b + [b_rc], [b_rc])
                            if m_ == 0:
                                P.add("dve", lambda e, accp=accp, c0=c0, rows=rows, si=si: e.tensor_scalar(
                                    out=on1[0:rows, si, :], in0=accp[0:rows, c0:c0 + 128], scalar1=rc[0:rows, si:si + 1],
                                    scalar2=None, op0=ALU.mult), acc_b + [b_rc, b_on1], [b_on1])
                            else:
                                P.add("dve", lambda e, accp=accp, c0=c0, rows=rows, si=si: e.tensor_scalar(
                                    out=fin[0:rows, si, :], in0=accp[0:rows, c0:c0 + 128], scalar1=rc[0:rows, si:si + 1],
                                    scalar2=None, op0=ALU.mult), acc_b + [b_rc, b_fin], [b_fin])
                                P.add("dve", lambda e, rows=rows, si=si: e.scalar_tensor_tensor(
                                    out=fin[0:rows, si, :], in0=fin[0:rows, si, :], scalar=lam_sb[0:rows, 1:2], in1=on1[0:rows, si, :],
                                    op0=ALU.mult, op1=ALU.add), [b_fin, b_on1, b_lam], [b_fin])
                    for si, (so, rows) in enumerate(subs):
                        P.add("dve", lambda e, rows=rows, si=si: e.tensor_tensor(
                            out=sqt[0:rows, :], in0=fin[0:rows, si, :], in1=fin[0:rows, si, :], op=ALU.mult), [b_fin, b_sqt], [b_sqt])
                        P.add("dve", lambda e, rows=rows, si=si: e.reduce_sum(
                            out=rc[0:rows, 8 + si:9 + si], in_=sqt[0:rows, :], axis=mybir.AxisListType.X), [b_sqt, b_rc], [b_rc])
                        P.add("act", lambda e, rows=rows, si=si: e.activation(
                            out=rc[0:rows, 8 + si:9 + si], in_=rc[0:rows, 8 + si:9 + si], func=AF.Sqrt, bias=eps_col[0:rows, :],
                            scale=1.0 / 128), [b_rc, b_cf], [b_rc])
                        P.add("dve", lambda e, rows=rows, si=si: e.reciprocal(
                            out=rc[0:rows, 8 + si:9 + si], in_=rc[0:rows, 8 + si:9 + si]), [b_rc], [b_rc])
                        P.add("dve", lambda e, rows=rows, si=si: e.scalar_tensor_tensor(
                            out=resb[0:rows, si, :], in0=fin[0:rows, si, :], scalar=rc[0:rows, 8 + si:9 + si], in1=gain08[0:rows, :],
                            op0=ALU.mult, op1=ALU.mult), [b_fin, b_rc, b_g08, b_resb], [b_resb])
                        P.add("pe", lambda e, rows=rows, si=si: e.transpose(
                            out=psbf[:, si * 128:si * 128 + rows], in_=resb[0:rows, si, :], identity=ident_b[0:rows, 0:rows]),
                            [b_resb, b_cb, b_psbf], [b_psbf])
                        evac(hT[:, 8 + h, outc0 + so:outc0 + so + rows], psbf[:, si * 128:si * 128 + rows],
                             [b_psbf, hTb[8 + h][ti_out]], [hTb[8 + h][ti_out]])

                for h in range(NH):
                    for t in range(NQT):
                        dma("sp", kTg[:, :, t * 512:(t + 1) * 512], gat_kT[t].rearrange("(r hd) t -> hd r t", r=4)[h * 128:(h + 1) * 128, :, :], [B["gat_kT"]], [b_kTg])
                        dma("sp", kTo[:, t * 512:(t + 1) * 512], src_kT[t][h * 128:(h + 1) * 128, :], [B["src_kT"]], [b_kTo])
                    dma("sp", kTm[:], scr_mk[h * 128:(h + 1) * 128, :], [B["scr_mk"]], [b_kTm])
                    for t in range(NQT):
                        for r in range(4):
                            dma("sp", Vh[:, r * NKT + t * 4:r * NKT + t * 4 + 4, 0:128],
                                gat_V[t][r * 512:(r + 1) * 512, h * 128:(h + 1) * 128].rearrange("(k p) c -> p k c", p=128),
                                [B["gat_V"]], [b_Vg])
                        dma("sp", Vh[:, 4 * NKT + t * 4:4 * NKT + t * 4 + 4, 0:128],
                            src_V[t][:, h * 128:(h + 1) * 128].rearrange("(k p) c -> p k c", p=128), [B["src_V"]], [b_Vo])
                    dma("sp", Vh[0:16, 5 * NKT, 0:128], scr_mv[0:16, h * 128:(h + 1) * 128], [B["scr_mv"]], [b_Vm])
                    dma("sp", qh[:], scr_q[h * 128:(h + 1) * 128, :], [B["scr_q"]], [b_qh])
                    P.add("dve", lambda e, h=h: e.tensor_scalar(out=qbt[:], in0=cf("qrow", CH), scalar1=-SLOPES[h], scalar2=None,
                                                                op0=ALU.mult), [b_cf, b_qbt], [b_qbt])
                    meta_kt = (kTm, b_kTm, Vh[:, 5 * NKT, :], b_Vm, cf("biasm", 1, h)[0:16, :], 16, None)
                    for qt in range(NQT):
                        kts = []
                        for r in range(4):
                            for kt in range(NKT):
                                kts.append((kTg[:, r, kt * 128:(kt + 1) * 128], b_kTg, Vh[:, r * NKT + kt, :], b_Vg,
                                            cf("biasg", 1, (h * 4 + r) * NKT + kt), 128, None))
                        kts.append(meta_kt)
                        for kt in range(4 * qt + 4):
                            m = kt - 4 * qt
                            kts.append((kTo[:, kt * 128:(kt + 1) * 128], b_kTo, Vh[:, 4 * NKT + kt, :], b_Vo,
                                        cf("biaso", 1, h * NKT + kt), 128, cb("maskd", 512, m * 512) if m >= 0 else None))
                        attend(h, qt * 512, 512, [(i * 128, 128) for i in range(4)], kts, qt * 512, qt)
                    mm_kt = (kTm, b_kTm, Vh[:, 5 * NKT, :], b_Vm, cf("biasmm", 1, h)[0:16, :], 16, cb("maskmm", 16)[0:16, :])
                    attend(h, CH, 16, [(0, 16)], [mm_kt], CH, XT)
                P.barrier()

            if cfg.get("STOP") == "B2":
                return
            with ExitStack() as st:
                Ku = Rot("Ku", [128, 2048], BF16, 2, st)
                Vu = Rot("Vu", [128, 2048], BF16, 2, st)
                kTu = Rot("kTu", [128, 2048], BF16, 2, st)
                Eu = Rot("Eu", [128, 128], F32, 2, st)
                A_all = sb("A_all", [128, PAGE * 8], BF16, st); b_Aall = Buf("Aall")
                rsum = sb("rsum", [128, 8], F32, st); b_rsum = Buf("rsum")
                An32 = sb("An32", [4, 8], F32, st); b_An32 = Buf("An32")
                An = sb("An", [4, 8], BF16, st); b_An = Buf("An")
                osb = sb("osb", [8, 128], F32, st); b_osb = Buf("osb")
                orc = sb("orc", [8, 4], F32, st); b_orc = Buf("orc")
                f4 = sb("f4", [4, 128], F32, st); b_f4 = Buf("f4")
                sq4 = sb("sq4", [4, 128], F32, st); b_sq4 = Buf("sq4")
                r4 = sb("r4", [4, 2], F32, st); b_r4 = Buf("r4")
                res_all = sb("res_all", [4, 8, 128], F32, st); b_res = Buf("res_all")
                osamp = sb("osamp", [32, 8, 128], F32, st); b_osamp = Buf("osamp")
                o_ps, o_pb = psum[5], psb[5]
                s_ps, s_pb = psum[6], psb[6]
                for s in range(8):
                    for u in range(NSUB):
                        ku, kub = Ku.get()
                        P.add("pool", lambda e, ku=ku, u=u, s=s: e.indirect_dma_start(
                            out=ku[:], out_offset=None, in_=cks[u][:, :],
                            in_offset=bass.IndirectOffsetOnAxis(ap=pt_sb[:, s:s + 1], axis=0)), [b_pt], [kub], kind="d")
                        vu, vub = Vu.get()
                        P.add("pool", lambda e, vu=vu, u=u, s=s: e.indirect_dma_start(
                            out=vu[:], out_offset=None, in_=cvs[u][:, :],
                            in_offset=bass.IndirectOffsetOnAxis(ap=pt_sb[:, s:s + 1], axis=0)), [b_pt], [vub], kind="d")
                        ktu, ktub = kTu.get()
                        for half in range(2):
                            P.add("pe", [lambda e, tk=tk, ku=ku, half=half: e.transpose(
                                out=psbf[:, tk * 128:(tk + 1) * 128], in_=ku[:, (half * 8 + tk) * 128:(half * 8 + tk + 1) * 128],
                                identity=ident_b) for tk in range(8)], [kub, b_cb, b_psbf], [b_psbf])
                            evac(ktu[:, half * 1024:(half + 1) * 1024], psbf[:, :], [b_psbf, ktub], [ktub])
                        ps, pb = next_ps5()
                        P.add("pe", [lambda e, tk=tk, ps=ps, ktu=ktu, s=s: e.matmul(
                            ps[:, tk * 8:(tk + 1) * 8], lhsT=ktu[:, tk * 128:(tk + 1) * 128], rhs=Qblk[:, s, :],
                            start=True, stop=True) for tk in range(16)], [ktub, b_Q], [pb])
                        eu, eub = Eu.get()
                        P.add("act", lambda e, eu=eu, ps=ps: e.activation(out=eu[:], in_=ps[:, 0:128], func=AF.Exp,
                                                                         bias=cf("sbias"), scale=1.0), [pb, b_cf], [eub])
                        P.add("dve", lambda e, eu=eu, u=u: e.tensor_tensor(
                            out=A_all[:, u * 128:(u + 1) * 128], in0=eu[:], in1=cf("wtok", 128, u * 128), op=ALU.mult),
                            [eub, b_cf, b_Aall], [b_Aall])
                        P.add("pe", [lambda e, tk=tk, vu=vu, u=u: e.matmul(
                            o_ps[0:8, 0:128], lhsT=A_all[:, (u * 16 + tk) * 8:(u * 16 + tk) * 8 + 8], rhs=vu[:, tk * 128:(tk + 1) * 128],
                            start=(u == 0 and tk == 0), stop=False) for tk in range(16)], [b_Aall, vub, o_pb], [o_pb])
                    ps, pb = next_ps5()
                    P.add("pe", lambda e, ps=ps, s=s: e.matmul(ps[0:4, 0:8], lhsT=ksT[:, 4 * s:4 * s + 4], rhs=Qblk[:, s, :],
                                                               start=True, stop=True), [b_ksT, b_Q], [pb])
                    P.add("act", lambda e, ps=ps: e.activation(out=An32[:], in_=ps[0:4, 0:8], func=AF.Exp,
                                                               bias=cf("nbias")[0:4, :], scale=1.0), [pb, b_cf, b_An32], [b_An32])
                    P.add("dve", lambda e: e.tensor_tensor(out=An32[:], in0=An32[:], in1=cf("nmask", 8)[0:4, :], op=ALU.mult),
                          [b_An32, b_cf], [b_An32])
                    P.add("dve", lambda e: e.tensor_copy(out=An[:], in_=An32[:]), [b_An32, b_An], [b_An])
                    P.add("pe", lambda e, s=s: e.matmul(o_ps[0:8, 0:128], lhsT=An[0:4, :], rhs=Vnew[0:4, s, :], start=False, stop=True),
                          [b_An, b_Vn, o_pb], [o_pb])
                    P.add("dve", lambda e: e.reduce_sum(out=rsum[:], in_=A_all[:].rearrange("p (t m) -> p m t", m=8),
                                                        axis=mybir.AxisListType.X), [b_Aall, b_rsum], [b_rsum])
                    P.add("pe", [lambda e: e.matmul(s_ps[0:8, 0:1], lhsT=rsum[:, :], rhs=cf("onesf", 1), start=True, stop=False),
                                 lambda e: e.matmul(s_ps[0:8, 0:1], lhsT=An32[0:4, :], rhs=cf("onesf", 1)[0:4, :], start=False, stop=True)],
                          [b_rsum, b_An32, b_cf, s_pb], [s_pb])
                    P.add("dve", lambda e: e.reciprocal(out=orc[:, 0:1], in_=s_ps[0:8, 0:1]), [s_pb, b_orc], [b_orc])
                    P.add("dve", lambda e: e.tensor_scalar(out=osb[:], in0=o_ps[0:8, 0:128], scalar1=orc[:, 0:1], scalar2=None,
                                                           op0=ALU.mult), [o_pb, b_orc, b_osb], [b_osb])
                    ps, pb = next_ps5()
                    P.add("pe", lambda e, ps=ps: e.matmul(ps[0:4, 0:128], lhsT=Cm[0:8, :], rhs=osb[0:8, :], start=True, stop=True),
                          [b_Cm, b_osb], [pb])
                    P.add("dve", lambda e, ps=ps: e.tensor_copy(out=f4[:], in_=ps[0:4, 0:128]), [pb, b_f4], [b_f4])
                    P.add("dve", lambda e: e.tensor_tensor(out=sq4[:], in0=f4[:], in1=f4[:], op=ALU.mult), [b_f4, b_sq4], [b_sq4])
                    P.add("dve", lambda e: e.reduce_sum(out=r4[:, 0:1], in_=sq4[:], axis=mybir.AxisListType.X), [b_sq4, b_r4], [b_r4])
                    P.add("act", lambda e: e.activation(out=r4[:, 0:1], in_=r4[:, 0:1], func=AF.Sqrt, bias=eps_col[0:4, :],
                                                        scale=1.0 / 128), [b_r4, b_cf], [b_r4])
                    P.add("dve", lambda e: e.reciprocal(out=r4[:, 0:1], in_=r4[:, 0:1]), [b_r4], [b_r4])
                    P.add("dve", lambda e, s=s: e.scalar_tensor_tensor(
                        out=res_all[0:4, s, :], in0=f4[:], scalar=r4[:, 0:1], in1=gain08[0:4, :], op0=ALU.mult, op1=ALU.mult),
                        [b_f4, b_r4, b_g08, b_res], [b_res])
                dma("sp", src_o.rearrange("(s q) d -> q s d", q=4), res_all[0:4, :, :], [b_res], [B["src_o"]])
                cc(GROUPS4, src_o[:, :], gat_o4[:, :], [B["src_o"]], [B["gat_o4"]])
                cc(PAIRS, gat_o4[:, :], gat_o8[:, :], [B["gat_o4"]], [B["gat_o8"]])
                dma("sp", osamp[:], gat_o8.rearrange("(h t) d -> t h d", t=32), [B["gat_o8"]], [b_osamp])
                ps, pb = next_ps5()
                P.add("pe", [lambda e, h=h, ps=ps: e.transpose(out=ps[:, h * 32:(h + 1) * 32], in_=osamp[0:32, h, :],
                                                              identity=ident_f[0:32, 0:32]) for h in range(8)],
                      [b_osamp, b_cf], [pb])
                evac(hT[:, 8:16, CH + 16:CH + 48], ps[:, 0:256].rearrange("p (h t) -> p h t", t=32),
                     [pb] + [hTb[8 + h][XT] for h in range(8)], [hTb[8 + h][XT] for h in range(8)])
                P.barrier()

            if cfg.get("STOP") == "B3":
                return
            with ExitStack() as st:
                wo = Rot("wo", [128, KC, 256], BF16, 3, st)
                for oc0 in range(0, KC, 2):
                    wt, wb = wo.get()
                    P.add("pool", wload(wt[:], wts["w_out"], oc0 * 128, 256, KC), [], [wb], kind="d")
                    for ff in range(2):
                        oc = oc0 + ff
                        for ti, (t0, n) in enumerate(TT):
                            ps, pb = next_ps()
                            P.add("pe", [lambda e, k=k, ps=ps, wt=wt, ff=ff, t0=t0, n=n: e.matmul(
                                ps[:, 0:n], lhsT=wt[:, k, ff * 128:(ff + 1) * 128], rhs=hT[:, k, t0:t0 + n],
                                start=(k == 0), stop=(k == KC - 1)) for k in range(KC)],
                                [wb] + [hTb[k][ti] for k in range(KC)], [pb])
                            P.add("dve", lambda e, ps=ps, oc=oc, t0=t0, n=n: e.tensor_tensor(
                                out=xT[:, oc, t0:t0 + n], in0=ps[:, 0:n], in1=xT[:, oc, t0:t0 + n], op=ALU.add),
                                [pb, xTb[oc][ti]], [xTb[oc][ti]])
                P.barrier()


        ffn(0, wts["w_gate1"], wts["w_up1"], wts["w_down1"])
        if cfg.get("MIXER", True):
            mixer(locals())
        ffn(32, wts["w_gate2"], wts["w_up2"], wts["w_down2"])

        with ExitStack() as st:
            ytok = Rot("ytok", [128, D], F32, 2, st)
            yf = sb("yf", [128, KC, 128], F32, st)
            b_yf = Buf("yf")
            gF = 48
            for ti, (t0, n) in enumerate(TT):
                rs, rsb = rms_rstd(ti)
                for s0 in range(0, n, 128):
                    ns = min(128, n - s0)
                    for k in range(KC):
                        P.add("dve", lambda e, k=k, rs=rs, t0=t0, s0=s0, ns=ns: e.scalar_tensor_tensor(
                            out=yf[:, k, 0:ns], in0=xT[:, k, t0 + s0:t0 + s0 + ns], scalar=cv_sb[:, gF + k:gF + k + 1],
                            in1=rs[:, s0:s0 + ns], op0=ALU.mult, op1=ALU.mult), [xTb[k][ti], rsb, b_cv, b_yf], [b_yf])
                    yt, ytb = ytok.get()
                    for k4 in range(KC // 4):
                        ps, pb = next_ps()
                        P.add("pe", [lambda e, kk=kk, k4=k4, ps=ps, ns=ns: e.transpose(
                            out=ps[0:ns, kk * 128:(kk + 1) * 128], in_=yf[:, k4 * 4 + kk, 0:ns], identity=ident_f)
                            for kk in range(4)], [b_yf, b_cf], [pb])
                        evac(yt[0:ns, k4 * 512:(k4 + 1) * 512], ps[0:ns, :], [pb], [ytb])
                    dma("sp", y_o[t0 + s0:t0 + s0 + ns, :], yt[0:ns, :], [ytb], [B["out"]])
            P.barrier()

        streams = P.emit(nc, sems, None)
        with nc.Block() as block:
            @block.tensor
            def _(e):
                run_streams({"pe": streams["pe"]}, sems, {"pe": e})

            @block.scalar
            def _(e):
                run_streams({"act": streams["act"]}, sems, {"act": e})

            @block.vector
            def _(e):
                run_streams({"dve": streams["dve"]}, sems, {"dve": e})

            @block.gpsimd
            def _(e):
                run_streams({"pool": streams["pool"]}, sems, {"pool": e})

            @block.sync
            def _(e):
                run_streams({"sp": streams["sp"]}, sems, {"sp": e})
    return nc


def host_consts(cfg, core):
    import ml_dtypes
    CH, PAGE = cfg["CH"], cfg["PAGE"]
    NKT = CH // 128
    j = core % 4
    off = {}
    cols = []
    i = np.arange(128, dtype=np.float64)[:, None]

    def addf(name, arr):
        off[name] = sum(a.shape[1] for a in cols)
        a = np.zeros((128, np.asarray(arr).shape[1]), np.float32)
        a[:np.asarray(arr).shape[0]] = np.asarray(arr, np.float32)
        cols.append(a)

    addf("ident", np.eye(128))
    addf("onesf", np.ones((128, 128)))
    addf("eps", np.full((128, 1), EPS))
    sel = np.zeros((128, 5));
    if j == 0:
        sel[:, 4] = 1.0
    else:
        sel[:, j - 1] = 1.0
    addf("sel", sel)
    bg = np.zeros((128, NH * 4 * NKT))
    bo = np.zeros((128, NH * NKT))
    bm = np.zeros((128, NH)); bmm = np.zeros((128, NH))
    for h in range(NH):
        sl = SLOPES[h]
        for r in range(4):
            for kt in range(NKT):
                bg[:, (h * 4 + r) * NKT + kt] = sl * (i[:, 0] + 128 * kt + CH * (r - j)) + (NEG if r >= j else 0.0)
        for kt in range(NKT):
            bo[:, h * NKT + kt] = sl * (i[:, 0] + 128 * kt)
        bm[:, h] = sl * (i[:, 0] - 16 - j * CH)
        bmm[:, h] = sl * i[:, 0]
    addf("biasg", bg); addf("biaso", bo); addf("biasm", bm); addf("biasmm", bmm)
    addf("qrow", np.broadcast_to(np.arange(CH, dtype=np.float64)[None, :], (128, CH)))
    slc = SLOPES[core]
    addf("sbias", slc * PAGE * (i - 127))
    tok = np.repeat(np.arange(PAGE), 8)[None, :]
    addf("wtok", np.broadcast_to(np.exp(slc * (tok - (PAGE - 1))), (128, PAGE * 8)))
    addf("nbias", slc * (1 + i))
    nm = np.zeros((128, 8))
    for jn in range(4):
        for m in range(2):
            for q in range(4):
                nm[jn, m * 4 + q] = 1.0 if jn <= q else 0.0
    addf("nmask", nm)
    invc = np.zeros((128, 128))
    for kc in range(8):
        for t in range(16):
            invc[:, kc * 16 + t] = 1.0 / min(t + 1, WINS[kc // 2])
    addf("invc", invc)
    c0 = np.zeros((128, 4)); c1 = np.zeros((128, 4))
    for q in range(4):
        c0[q, q] = 1.0; c1[4 + q, q] = 1.0
    addf("cb0", c0); addf("cb1", c1)
    cf = np.concatenate(cols, 1)
    colsb = []

    def addb(name, arr):
        off[name] = sum(a.shape[1] for a in colsb)
        colsb.append(np.asarray(arr, np.float32).astype(ml_dtypes.bfloat16))

    addb("identb", np.eye(128))
    addb("onesb", np.ones((128, 128)))
    addb("zerob", np.zeros((128, 128)))
    jq = np.arange(512)[None, :]
    md = np.concatenate([np.where(i + 128 * m <= jq, 0.0, NEG) for m in range(4)], 1)
    addb("maskd", md)
    addb("maskmm", np.where(i <= np.arange(16)[None, :], 0.0, NEG))
    cb = np.concatenate(colsb, 1)
    return off, cf, cb


def fm(v, n):
    return np.ascontiguousarray(np.asarray(v, np.float32).reshape(n, 128).T)


def prep_inputs(cfg, inp):
    CH, NSUB = cfg["CH"], cfg["NSUB"]
    maps = []
    x_prompt = np.asarray(inp["x_prompt"]); x_sample = np.asarray(inp["x_sample"])
    meta = np.asarray(inp["meta_tokens"])
    cvec = np.concatenate([fm(inp["norm_ffn1"][0], 16), fm(inp["norm_mix"][0], 16), fm(inp["norm_ffn2"][0], 16),
                           fm(inp["norm_final"], 16), fm(inp["pool_scale"][0], 8)], 1)
    gain_bc = np.ascontiguousarray(np.broadcast_to(np.asarray(inp["subln_gain"][0], np.float32)[None, :], (128, 128)))
    lamv = np.concatenate([np.asarray(inp[k][0], np.float32) for k in ("lambda_q1", "lambda_k1", "lambda_q2", "lambda_k2")])[None, :]
    w_in = np.asarray(inp["w_in"][0])
    shared = dict(cvec=cvec, gain_bc=gain_bc, lamv=np.ascontiguousarray(lamv),
                  w_gate1=np.asarray(inp["w_gate1"][0]), w_up1=np.asarray(inp["w_up1"][0]), w_down1=np.asarray(inp["w_down1"][0]),
                  w_in=w_in, w_out=np.asarray(inp["w_out"][0]),
                  w_gate2=np.asarray(inp["w_gate2"][0]), w_up2=np.asarray(inp["w_up2"][0]), w_down2=np.asarray(inp["w_down2"][0]),
                  w_pool=np.ascontiguousarray(np.asarray(inp["w_pool"][0]).reshape(1024, 256)),
                  spool=np.ascontiguousarray(np.asarray(inp["state_pool"][0]).reshape(120, 1024)),
                  ptab=np.ascontiguousarray(np.asarray(inp["page_table"]).T.astype(np.int32)))
    ckf = np.asarray(inp["cache_k"][0]); cvf = np.asarray(inp["cache_v"][0])
    npool = ckf.shape[0]
    for c in range(8):
        b, j = c // 4, c % 4
        m = dict(shared)
        m["xin"] = np.ascontiguousarray(np.concatenate([x_prompt[b, j * CH:(j + 1) * CH], meta, x_sample.reshape(32, D)], 0))
        m["w_in_hd"] = np.ascontiguousarray(np.concatenate([w_in[:, 1024 + 128 * c:1152 + 128 * c],
                                                            w_in[:, 2048 + 128 * c:2176 + 128 * c],
                                                            w_in[:, 3072 + 128 * c:3200 + 128 * c]], 1))
        for u in range(NSUB):
            m[f"ck{u}"] = np.ascontiguousarray(ckf[:, 16 * u:16 * u + 16, c, :].reshape(npool, 2048))
            m[f"cv{u}"] = np.ascontiguousarray(cvf[:, 16 * u:16 * u + 16, c, :].reshape(npool, 2048))
        off, cf, cb = host_consts(cfg, c)
        m["cf32"], m["cbf"] = cf, cb
        maps.append(m)
    return maps


def finish_cfg(cfg):
    off, cf, cb = host_consts(cfg, 0)
    cfg["OFF"] = off; cfg["NCF"] = cf.shape[1]; cfg["NCB"] = cb.shape[1]
    return cfg


def assemble(cfg, res):
    CH, SEQ = cfg["CH"], cfg["SEQ"]
    y_prompt = np.zeros((2, SEQ, D), np.float32)
    k_prompt = np.zeros((1, 2, 16 + SEQ, 8, 128), np.float32)
    v_prompt = np.zeros((1, 2, 16 + SEQ, 8, 128), np.float32)
    pool_prompt = np.zeros((1, 2, 15, 1024), np.float32)
    for c in range(8):
        b, j = c // 4, c % 4
        r = res[c]
        y_prompt[b, j * CH:(j + 1) * CH] = r["y"][0:CH]
        k_prompt[0, b, 16 + j * CH:16 + (j + 1) * CH] = r["ko"][0:CH].reshape(CH, 8, 128)
        v_prompt[0, b, 16 + j * CH:16 + (j + 1) * CH] = r["vo"][0:CH].reshape(CH, 8, 128)
        if j == 0:
            k_prompt[0, b, 0:16] = r["ko"][CH:CH + 16].reshape(16, 8, 128)
            v_prompt[0, b, 0:16] = r["vo"][CH:CH + 16].reshape(16, 8, 128)
        if j == 3:
            pool_prompt[0, b] = r["ptail"].reshape(128, 8, 15).transpose(2, 1, 0).reshape(15, 1024)
    r0 = res[0]
    y_sample = np.ascontiguousarray(r0["y"][CH + 16:CH + 48].reshape(8, 4, D))
    k_sample = np.ascontiguousarray(r0["ko"][CH + 16:CH + 48].reshape(1, 8, 4, 8, 128))
    v_sample = np.ascontiguousarray(r0["vo"][CH + 16:CH + 48].reshape(1, 8, 4, 8, 128))
    new = r0["psnew"].reshape(128, 8, 8, 4).transpose(2, 3, 1, 0).reshape(8, 4, 1024)
    old = r0["psold"].reshape(8, 11, 1024)
    pool_sample = np.ascontiguousarray(np.concatenate([old, new], 1)[None])
    return (y_prompt, y_sample, k_prompt, v_prompt, pool_prompt, k_sample, v_sample, pool_sample)


_CACHE = {}


def kernel(**inputs):
    cfg = finish_cfg(make_cfg())
    if "nc" not in _CACHE:
        _CACHE["nc"] = build(cfg)
    nc = _CACHE["nc"]
    maps = prep_inputs(cfg, inputs)
    res = run_bass_kernel_spmd(nc, maps, core_ids=list(range(8)))
    return assemble(cfg, res.results)
```

```python
import numpy as np
import concourse.bass as bass
import concourse.mybir as mybir
from concourse.bass_utils import run_bass_kernel_spmd

F32 = mybir.dt.float32
BF16 = mybir.dt.bfloat16
I32 = mybir.dt.int32
ALU = mybir.AluOpType
AF = mybir.ActivationFunctionType

D = 2048
KC = 16
NH = 8
EPS = 1e-6
NEG = -30000.0
SLOPES = [2.0 ** (-8.0 * (h + 1) / NH) for h in range(NH)]
LAM_INIT = 0.2
WINS = (2, 4, 8, 16)


class Buf:
    __slots__ = ("name", "w", "r")

    def __init__(self, name):
        self.name = name
        self.w = None
        self.r = []


class Prog:
    def __init__(self):
        self.recs = []
        self.nslot = {"sp": 8, "pool": 8}

    def barrier(self):
        self.recs.append(("sp", [], [], [], "bar"))

    def add(self, eng, fns, reads=(), writes=(), kind="c"):
        if not isinstance(fns, (list, tuple)):
            fns = [fns]
        self.recs.append((eng, list(fns), list(reads), list(writes), kind))

    def emit(self, nc, sems, block_engines):
        cnt = {e: 0 for e in ("pe", "act", "dve", "pool")}
        slot_use = {q: [0] * n for q, n in self.nslot.items()}
        slot_next = {q: 0 for q in self.nslot}
        ncc = 0
        waited = {}
        streams = {e: [] for e in ("pe", "act", "dve", "pool", "sp")}

        def need(E, tok):
            if tok is None:
                return
            key = (E, tok[0])
            if waited.get(key, 0) >= tok[1]:
                return
            waited[key] = tok[1]
            streams[E].append(("w", tok[0], tok[1]))

        cc_done = []
        for eng, fns, reads, writes, kind in self.recs:
            E = eng
            if kind == "bar":
                alltok = [(e, cnt[e]) for e in cnt if cnt[e]]
                for q, n in self.nslot.items():
                    for s in range(n):
                        if slot_use[q][s]:
                            alltok.append((f"{q}{s}", 16 * slot_use[q][s]))
                alltok += cc_done
                for E2 in streams:
                    for t in alltok:
                        if t[0] == E2:
                            continue
                        need(E2, t)
                continue
            toks = []
            for b in reads:
                toks.append(b.w)
            for b in writes:
                toks.append(b.w)
                toks.extend(b.r)
            if kind == "c":
                cnt[E] += 1
                tok = (E, cnt[E])
                inc = 1
            elif kind == "d":
                q = E
                s = slot_next[q]
                slot_next[q] = (s + 1) % self.nslot[q]
                prev = slot_use[q][s]
                if prev:
                    toks.append((f"{q}{s}", 16 * prev))
                slot_use[q][s] = prev + 1
                tok = (f"{q}{s}", 16 * (prev + 1))
                inc = 16
            else:
                tok = (f"cc{ncc}", 1)
                cc_done.append(tok)
                ncc += 1
                inc = 1
            for t in toks:
                if t is None:
                    continue
                if t[0] == E and E == "pe":
                    continue
                need(E, t)
            streams[E].append(("i", fns, tok[0], inc))
            for b in writes:
                b.w = tok
                b.r = []
            for b in reads:
                if b not in writes:
                    b.r.append(tok)
        final = []
        for q, n in self.nslot.items():
            for s in range(n):
                if slot_use[q][s]:
                    final.append((f"{q}{s}", 16 * slot_use[q][s]))
        for t in final:
            need("sp", t)
        for e in ("pe", "act", "dve", "pool"):
            if cnt[e]:
                need("sp", (e, cnt[e]))
        return streams


def run_streams(streams, sems, engs):
    for e, lst in streams.items():
        eng = engs[e]
        for it in lst:
            if it[0] == "w":
                eng.wait_ge(sems[it[1]], it[2])
            else:
                _, fns, semname, inc = it
                last = None
                for f in fns:
                    last = f(eng)
                last.then_inc(sems[semname], inc)


def make_cfg(seq=4096, dff=5632, page=128, npool=1280):
    ch = seq // 4
    return dict(SEQ=seq, CH=ch, DFF=dff, PAGE=page, NPOOL=npool, NT=ch + 48, NSUB=page // 16)


def token_tiles(cfg):
    ch = cfg["CH"]
    tl = [(i * 512, 512) for i in range(ch // 512)]
    tl.append((ch, 48))
    return tl


GROUPS4 = [[0, 1, 2, 3], [4, 5, 6, 7]]
PAIRS = [[0, 4], [1, 5], [2, 6], [3, 7]]


def build(cfg):
    from contextlib import ExitStack
    CH, DFF, PAGE, NT, NPOOL, NSUB = cfg["CH"], cfg["DFF"], cfg["PAGE"], cfg["NT"], cfg["NPOOL"], cfg["NSUB"]
    NF = DFF // 128
    TT = token_tiles(cfg)
    NTT = len(TT)
    NKT = CH // 128
    NQT = CH // 512
    nc = bass.Bass("TRN2", target_bir_lowering=False, num_devices=8)
    P = Prog()
    CO = cfg["OFF"]

    def din(name, shape, dt=F32):
        return nc.dram_tensor(name, list(shape), dt, kind="ExternalInput").ap()

    def dout(name, shape, dt=F32):
        return nc.dram_tensor(name, list(shape), dt, kind="ExternalOutput").ap()

    def dint(name, shape, dt):
        return nc.dram_tensor(name, list(shape), dt, kind="Internal").ap()

    xin = din("xin", [NT, D])
    wts = {}
    for nm, shp in (("w_gate1", [D, DFF]), ("w_up1", [D, DFF]), ("w_down1", [DFF, D]), ("w_in", [D, 4096]),
                    ("w_out", [D, D]), ("w_gate2", [D, DFF]), ("w_up2", [D, DFF]), ("w_down2", [DFF, D]),
                    ("w_pool", [1024, 256]), ("w_in_hd", [D, 384])):
        wts[nm] = din(nm, shp)
    cvec = din("cvec", [128, 72])
    gain_bc = din("gain_bc", [128, 128])
    lamv = din("lamv", [1, 256])
    cks = [din(f"ck{u}", [NPOOL, 2048]) for u in range(NSUB)]
    cvs = [din(f"cv{u}", [NPOOL, 2048]) for u in range(NSUB)]
    ptab = din("ptab", [128, 8], I32)
    spool = din("spool", [120, 1024])
    cf32 = din("cf32", [128, cfg["NCF"]])
    cbf = din("cbf", [128, cfg["NCB"]], BF16)

    y_o = dout("y", [NT, D])
    k_o = dout("ko", [NT, 1024])
    v_o = dout("vo", [NT, 1024])
    pt_o = dout("ptail", [128, 120])
    psn_o = dout("psnew", [128, 256])
    pso_o = dout("psold", [88, 1024])

    src_kT = [dint(f"src_kT{t}", [1024, 512], BF16) for t in range(NQT)]
    gat_kT = [dint(f"gat_kT{t}", [4096, 512], BF16) for t in range(NQT)]
    src_V = [dint(f"src_V{t}", [512, 1024], BF16) for t in range(NQT)]
    gat_V = [dint(f"gat_V{t}", [2048, 1024], BF16) for t in range(NQT)]
    src_ph = dint("src_ph", [128, 128], F32)
    gat_ph = dint("gat_ph", [512, 128], F32)
    scr_q = dint("scr_q", [1024, CH + 16], BF16)
    scr_mk = dint("scr_mk", [1024, 16], BF16)
    scr_mv = dint("scr_mv", [16, 1024], BF16)
    src_o = dint("src_o", [32, 128], F32)
    gat_o4 = dint("gat_o4", [128, 128], F32)
    gat_o8 = dint("gat_o8", [256, 128], F32)
    B = {n: Buf(n) for n in ("src_kT", "gat_kT", "src_V", "gat_V", "src_ph", "gat_ph", "scr_q", "scr_mk", "scr_mv",
                             "src_o", "gat_o4", "gat_o8", "out")}

    es = ExitStack()

    uniq = [0]

    def sb(name, shape, dt=F32, st=None):
        uniq[0] += 1
        return (st or es).enter_context(nc.sbuf_tensor(f"{name}_{uniq[0]}", list(shape), dt))

    class Rot:
        def __init__(self, name, shape, dt, n, st=None):
            self.t = [sb(f"{name}{i}", shape, dt, st) for i in range(n)]
            self.b = [Buf(f"{name}{i}") for i in range(n)]
            self.i = 0

        def get(self):
            k = self.i % len(self.t)
            self.i += 1
            return self.t[k], self.b[k]

    with es:
        sems = {}
        for nm in (["pe", "act", "dve", "pool"] + [f"sp{i}" for i in range(8)] + [f"pool{i}" for i in range(8)]
                   + [f"cc{i}" for i in range(8)]):
            sems[nm] = es.enter_context(nc.semaphore("s_" + nm))
        psum = [es.enter_context(nc.psum_tensor(f"ps{i}", [128, 512], F32)) for i in range(7)]
        psb = [Buf(f"ps{i}") for i in range(7)]
        psbf = es.enter_context(nc.psum_tensor("psbf", [128, 1024], BF16))
        b_psbf = Buf("psbf")
        pcount = [0]

        def next_ps():
            i = pcount[0] % 7
            pcount[0] += 1
            return psum[i], psb[i]

        SKIP = cfg.get("SKIP") or ""
        phase = [""]

        def dma(q, out, in_, reads, writes):
            if phase[0] == "A" and "d" in SKIP:
                return
            P.add(q, lambda e, o=out, i=in_: e.dma_start(out=o, in_=i), reads, writes, kind="d")

        def cc(groups, src, dst, reads, writes):
            P.add("pool", lambda e: e.collective_compute("AllGather", ALU.bypass, replica_groups=groups,
                                                          ins=[src], outs=[dst]), reads, writes, kind="cc")

        xT = sb("xT", [128, KC, NT])
        xTb = [[Buf(f"x{k}_{t}") for t in range(NTT)] for k in range(KC)]
        hT = sb("hT", [128, KC, NT], BF16)
        hTb = [[Buf(f"h{k}_{t}") for t in range(NTT)] for k in range(KC)]
        cv_sb = sb("cv_sb", [128, 72]); b_cv = Buf("cv")
        cf_sb = sb("cf_sb", [128, cfg["NCF"]]); b_cf = Buf("cf")
        cb_sb = sb("cb_sb", [128, cfg["NCB"]], BF16); b_cb = Buf("cb")
        gain_sb = sb("gain_sb", [128, 128]); b_gain = Buf("gain")
        pt_sb = sb("pt_sb", [128, 8], I32); b_pt = Buf("pt")
        lam_sb = sb("lam_sb", [128, 4]); b_lam = Buf("lam")
        sqr = Rot("sq", [128, 512], BF16, 3)
        rstd_r = Rot("rstd", [128, 512], F32, 2)

        def cf(name, w=1, o=0):
            return cf_sb[:, CO[name] + o:CO[name] + o + w]

        def cb(name, w, o=0):
            return cb_sb[:, CO[name] + o:CO[name] + o + w]

        ident_f = cf("ident", 128)
        ident_b = cb("identb", 128)
        ones_b = cb("onesb", 128)
        eps_col = cf("eps")

        dma("sp", cv_sb[:], cvec[:, :], [], [b_cv])
        dma("sp", cf_sb[:], cf32[:, :], [], [b_cf])
        dma("sp", cb_sb[:], cbf[:, :], [], [b_cb])
        dma("sp", gain_sb[:], gain_bc[:, :], [], [b_gain])
        dma("sp", pt_sb[:], ptab[:, :], [], [b_pt])

        evac_flip = [0]

        def evac(out, in_, reads, writes, scale=None, eng=None):
            evac_flip[0] ^= 1
            if eng is None:
                eng = "act" if evac_flip[0] else "dve"
            if eng == "act":
                if scale is None:
                    P.add("act", lambda e: e.activation(out=out, in_=in_, func=AF.Copy), reads, writes)
                else:
                    P.add("act", lambda e: e.activation(out=out, in_=in_, func=AF.Copy, scale=scale), reads, writes)
            else:
                if scale is None:
                    P.add("dve", lambda e: e.tensor_copy(out=out, in_=in_), reads, writes)
                else:
                    P.add("dve", lambda e: e.tensor_scalar(out=out, in0=in_, scalar1=scale, scalar2=None, op0=ALU.mult),
                          reads, writes)

        def tile_of(tok):
            for ti, (t0, n) in enumerate(TT):
                if t0 <= tok < t0 + n:
                    return ti
            raise ValueError

        with ExitStack() as st:
            xtok = Rot("xtok", [128, D], F32, 2, st)
            for tt in range((NT + 127) // 128):
                r0 = tt * 128
                nr = min(128, NT - r0)
                xt, xb = xtok.get()
                dma("sp", xt[0:nr, :], xin[r0:r0 + nr, :], [], [xb])
                ti = tile_of(r0)
                for k4 in range(KC // 4):
                    ps, pb = next_ps()
                    fns = []
                    for kk in range(4):
                        k = k4 * 4 + kk
                        fns.append(lambda e, k=k, kk=kk, ps=ps, xt=xt, nr=nr: e.transpose(
                            out=ps[:, kk * 128:kk * 128 + nr], in_=xt[0:nr, k * 128:(k + 1) * 128],
                            identity=ident_f[0:nr, 0:nr]))
                    P.add("pe", fns, [xb, b_cf], [pb])
                    evac(xT[:, k4 * 4:k4 * 4 + 4, r0:r0 + nr], ps[:].rearrange("p (a b) -> p a b", a=4)[:, :, 0:nr],
                         [pb], [xTb[k][ti] for k in range(k4 * 4, k4 * 4 + 4)])
            P.barrier()

        def rms_rstd(ti):
            t0, n = TT[ti]
            ps, pb = next_ps()
            for k in range(KC):
                sq, sqb = sqr.get()
                P.add("act", lambda e, sq=sq, k=k: e.activation(out=sq[:, 0:n], in_=xT[:, k, t0:t0 + n], func=AF.Square),
                      [xTb[k][ti]], [sqb])
                P.add("pe", lambda e, sq=sq, k=k, ps=ps: e.matmul(ps[:, 0:n], lhsT=ones_b, rhs=sq[:, 0:n],
                                                                 start=(k == 0), stop=(k == KC - 1)),
                      [sqb, b_cb] + ([pb] if k else []), [pb])
            rs, rsb = rstd_r.get()
            P.add("act", lambda e: e.activation(out=rs[:, 0:n], in_=ps[:, 0:n], func=AF.Sqrt, bias=eps_col, scale=1.0 / D),
                  [pb, b_cf], [rsb])
            P.add("dve", lambda e: e.reciprocal(out=rs[:, 0:n], in_=rs[:, 0:n]), [rsb], [rsb])
            return rs, rsb

        def rmsnorm_to_h(gbase):
            for ti, (t0, n) in enumerate(TT):
                rs, rsb = rms_rstd(ti)
                for k in range(KC):
                    P.add("dve", lambda e, k=k, rs=rs, t0=t0, n=n: e.scalar_tensor_tensor(
                        out=hT[:, k, t0:t0 + n], in0=xT[:, k, t0:t0 + n], scalar=cv_sb[:, gbase + k:gbase + k + 1],
                        in1=rs[:, 0:n], op0=ALU.mult, op1=ALU.mult),
                        [xTb[k][ti], rsb, b_cv], [hTb[k][ti]])

        def wload(dst, w2d, n0, wdt, nk):
            return lambda e: e.dma_start(out=dst, in_=w2d.rearrange("(kc p) n -> p kc n", p=128)[:, 0:nk, n0:n0 + wdt])

        FP = 8

        def ffn(gbase, wg, wu, wd):
            with ExitStack() as st:
                wgu = Rot("wgu", [128, KC, 256], BF16, 4, st)
                hid = sb("hid", [128, FP, NT], BF16, st)
                hidb = [[Buf(f"hid{f}_{t}") for t in range(NTT)] for f in range(FP)]
                wdn = Rot("wdn", [128, FP, 512], BF16, 2, st)
                sgr = Rot("sg", [128, 512], F32, 2, st)
                rmsnorm_to_h(gbase)
                f0 = 0
                while f0 < NF:
                    nfp = min(FP, NF - f0)
                    for fp in range(0, nfp, 2):
                        wgt, wgb = wgu.get()
                        P.add("pool", wload(wgt[:], wg, (f0 + fp) * 128, 256, KC), [], [wgb], kind="d")
                        wut, wub = wgu.get()
                        P.add("pool", wload(wut[:], wu, (f0 + fp) * 128, 256, KC), [], [wub], kind="d")
                        for ff in range(2):
                            fi = fp + ff
                            for ti, (t0, n) in enumerate(TT):
                                psA, pbA = next_ps()
                                P.add("pe", [lambda e, k=k, psA=psA, wgt=wgt, ff=ff, t0=t0, n=n: e.matmul(
                                    psA[:, 0:n], lhsT=wgt[:, k, ff * 128:(ff + 1) * 128], rhs=hT[:, k, t0:t0 + n],
                                    start=(k == 0), stop=(k == KC - 1)) for k in range(KC)],
                                    [wgb] + [hTb[k][ti] for k in range(KC)], [pbA])
                                psB, pbB = next_ps()
                                P.add("pe", [lambda e, k=k, psB=psB, wut=wut, ff=ff, t0=t0, n=n: e.matmul(
                                    psB[:, 0:n], lhsT=wut[:, k, ff * 128:(ff + 1) * 128], rhs=hT[:, k, t0:t0 + n],
                                    start=(k == 0), stop=(k == KC - 1)) for k in range(KC)],
                                    [wub] + [hTb[k][ti] for k in range(KC)], [pbB])
                                sg, sgb = sgr.get()
                                P.add("act", lambda e, sg=sg, psA=psA, n=n: e.activation(out=sg[:, 0:n], in_=psA[:, 0:n],
                                                                                        func=AF.Silu), [pbA], [sgb])
                                P.add("dve", lambda e, sg=sg, psB=psB, fi=fi, t0=t0, n=n: e.tensor_tensor(
                                    out=hid[:, fi, t0:t0 + n], in0=sg[:, 0:n], in1=psB[:, 0:n], op=ALU.mult),
                                    [sgb, pbB], [hidb[fi][ti]])
                    for o4 in range(4):
                        wdt_, wdb = wdn.get()
                        P.add("pool", lambda e, wdt_=wdt_, f0=f0, nfp=nfp, o4=o4: e.dma_start(
                            out=wdt_[:, 0:nfp, :],
                            in_=wd.rearrange("(fc p) n -> p fc n", p=128)[:, f0:f0 + nfp, o4 * 512:(o4 + 1) * 512]),
                            [], [wdb], kind="d")
                        for oo in range(4):
                            oc = o4 * 4 + oo
                            for ti, (t0, n) in enumerate(TT):
                                ps, pb = next_ps()
                                P.add("pe", [lambda e, f=f, ps=ps, wdt_=wdt_, oo=oo, t0=t0, n=n, nfp=nfp: e.matmul(
                                    ps[:, 0:n], lhsT=wdt_[:, f, oo * 128:(oo + 1) * 128], rhs=hid[:, f, t0:t0 + n],
                                    start=(f == 0), stop=(f == nfp - 1)) for f in range(nfp)],
                                    [wdb] + [hidb[f][ti] for f in range(nfp)], [pb])
                                P.add("dve", lambda e, ps=ps, oc=oc, t0=t0, n=n: e.scalar_tensor_tensor(
                                    out=xT[:, oc, t0:t0 + n], in0=ps[:, 0:n], scalar=0.5, in1=xT[:, oc, t0:t0 + n],
                                    op0=ALU.mult, op1=ALU.add), [pb, xTb[oc][ti]], [xTb[oc][ti]])
                    f0 += nfp
                P.barrier()

        def mixer(_):
            mst = ExitStack()
            Qblk = sb("Qblk_", [128, 8, 8], BF16); b_Q = Buf("Qblk")
            ksT = sb("ksT_", [128, 32], BF16); b_ksT = Buf("ksT")
            vsf = sb("vsf_", [128, 32], F32); b_vsf = Buf("vsf")
            Vnew = sb("Vnew_", [4, 8, 128], BF16); b_Vn = Buf("Vnew")
            gain08 = sb("gain08_", [128, 128], F32); b_g08 = Buf("g08")
            lv = sb("lv_", [1, 256], F32); b_lv = Buf("lv")
            lt = sb("lt_", [1, 8], F32); b_lt = Buf("lt")
            Cm = sb("Cm_", [8, 4], F32); b_Cm = Buf("Cm")
            pAll = sb("pAll", [128, 8, CH + 15], F32, mst); b_pA = [Buf(f"pA{k}") for k in range(8)]
            pM = sb("pM", [128, 8, 31], F32, mst); b_pM = Buf("pM")
            pS = sb("pS", [128, 8, 152], F32, mst); b_pS = Buf("pS")
            P.add("dve", lambda e: e.memset(pM[:], 0.0), [], [b_pM])
            P.add("dve", lambda e: e.memset(Qblk[:], 0.0), [], [b_Q])
            P.add("dve", lambda e: e.tensor_scalar(out=gain08[:], in0=gain_sb[:], scalar1=1.0 - LAM_INIT, scalar2=None,
                                                   op0=ALU.mult), [b_gain], [b_g08])
            dma("sp", lv[:], lamv[:, :], [], [b_lv])
            P.add("dve", lambda e: e.tensor_tensor(out=lv[0:1, 0:64], in0=lv[0:1, 0:64], in1=lv[0:1, 64:128], op=ALU.mult),
                  [b_lv], [b_lv])
            P.add("dve", lambda e: e.tensor_tensor(out=lv[0:1, 128:192], in0=lv[0:1, 128:192], in1=lv[0:1, 192:256],
                                                   op=ALU.mult), [b_lv], [b_lv])
            P.add("dve", lambda e: e.reduce_sum(out=lt[0:1, 0:1], in_=lv[0:1, 0:64], axis=mybir.AxisListType.X), [b_lv], [b_lt])
            P.add("dve", lambda e: e.reduce_sum(out=lt[0:1, 1:2], in_=lv[0:1, 128:192], axis=mybir.AxisListType.X),
                  [b_lv, b_lt], [b_lt])
            P.add("act", lambda e: e.activation(out=lt[0:1, 2:4], in_=lt[0:1, 0:2], func=AF.Exp), [b_lt], [b_lt])
            P.add("dve", lambda e: e.tensor_tensor(out=lt[0:1, 4:5], in0=lt[0:1, 2:3], in1=lt[0:1, 3:4], op=ALU.subtract),
                  [b_lt], [b_lt])
            P.add("dve", lambda e: e.tensor_scalar(out=lt[0:1, 5:6], in0=lt[0:1, 4:5], scalar1=LAM_INIT, scalar2=None,
                                                   op0=ALU.add), [b_lt], [b_lt])
            ps, pb = next_ps()
            P.add("pe", lambda e, ps=ps: e.matmul(ps[:, 0:1], lhsT=cf("onesf", 128)[0:1, :], rhs=lt[0:1, 5:6],
                                                  start=True, stop=True), [b_lt, b_cf], [pb])
            P.add("dve", lambda e, ps=ps: e.tensor_copy(out=lam_sb[:, 0:1], in_=ps[:, 0:1]), [pb], [b_lam])
            P.add("dve", lambda e, ps=ps: e.tensor_scalar(out=lam_sb[:, 1:2], in0=ps[:, 0:1], scalar1=-1.0, scalar2=None,
                                                          op0=ALU.mult), [pb, b_lam], [b_lam])
            P.add("dve", lambda e: e.scalar_tensor_tensor(out=Cm[:], in0=cf("cb1", 4)[0:8, :], scalar=lam_sb[0:8, 1:2],
                                                          in1=cf("cb0", 4)[0:8, :], op0=ALU.mult, op1=ALU.add),
                  [b_lam, b_cf], [b_Cm])

            if cfg.get("STOP") == "L":
                mst.close(); return
            rmsnorm_to_h(16)
            XT = NTT - 1
            with ExitStack() as st:
                win = Rot("win", [128, KC, 256], BF16, 3, st)
                kfr = Rot("kf", [128, 512], F32, 2, st)
                k16 = Rot("k16", [128, 512], BF16, 2, st)
                kst = Rot("kst", [128, 4, 128], F32, 2, st)
                vst = Rot("vst", [128, 4, 128], BF16, 2, st)
                for t_, b_ in zip(vst.t, vst.b):
                    P.add("dve", lambda e, t_=t_: e.memset(t_[:], 1.0), [], [b_])

                phase[0] = "A"

                def tok_major_out(ft, fb, n, t0, h, dst_o, want_bf):
                    if "t" in SKIP:
                        return
                    ps2, pb2 = next_ps()
                    nsub = (n + 127) // 128
                    rows = min(128, n)
                    P.add("pe", [lambda e, j=j, ps2=ps2: e.transpose(
                        out=ps2[0:min(128, n - j * 128), j * 128:(j + 1) * 128],
                        in_=ft[:, j * 128:j * 128 + min(128, n - j * 128)], identity=ident_f) for j in range(nsub)],
                        [fb, b_cf], [pb2])
                    kt_, ktb = kst.get()
                    src = ps2[0:rows, 0:nsub * 128].rearrange("p (j d) -> p j d", d=128)
                    evac(kt_[0:rows, 0:nsub, :], src, [pb2], [ktb], eng="act")
                    if n == 512:
                        dma("sp", dst_o[t0:t0 + 512, h * 128:(h + 1) * 128].rearrange("(j p) d -> p j d", p=128), kt_[:, :, :],
                            [ktb], [B["out"]])
                    else:
                        dma("sp", dst_o[t0:t0 + n, h * 128:(h + 1) * 128], kt_[0:n, 0, :], [ktb], [B["out"]])
                    if want_bf:
                        vt_, vtb = vst.get()
                        P.add("dve", lambda e, vt_=vt_, kt_=kt_: e.tensor_copy(out=vt_[0:rows, 0:nsub, 0:128], in_=kt_[0:rows, 0:nsub, :]),
                              [ktb], [vtb])
                        if n == 512:
                            dma("sp", src_V[t0 // 512][:, h * 128:(h + 1) * 128].rearrange("(j p) c -> p j c", p=128),
                                vt_[:, :, :], [vtb], [B["src_V"]])
                        else:
                            dma("sp", scr_mv[0:16, h * 128:(h + 1) * 128], vt_[0:16, 0, :], [vtb], [B["scr_mv"]])

                order = list(range(0, 8)) + list(range(16, 32)) + list(range(8, 16))
                for oi in range(0, 32, 2):
                    oc0 = order[oi]
                    wt, wb = win.get()
                    P.add("pool", wload(wt[:], wts["w_in"], oc0 * 128, 256, KC), [], [wb], kind="d")
                    for ff in range(2):
                        oc = oc0 + ff
                        for ti, (t0, n) in enumerate(TT):
                            ps, pb = next_ps()
                            P.add("pe", [lambda e, k=k, ps=ps, wt=wt, ff=ff, t0=t0, n=n: e.matmul(
                                ps[:, 0:n], lhsT=wt[:, k, ff * 128:(ff + 1) * 128], rhs=hT[:, k, t0:t0 + n],
                                start=(k == 0), stop=(k == KC - 1)) for k in range(KC)],
                                [wb] + [hTb[k][ti] for k in range(KC)], [pb])
                            if (oc < 8 and "P" in SKIP) or (8 <= oc < 16 and "Q" in SKIP) or (16 <= oc < 24 and "K" in SKIP) or (oc >= 24 and "V" in SKIP):
                                continue
                            if oc < 8:
                                if ti < XT:
                                    evac(pAll[:, oc, 15 + t0:15 + t0 + n], ps[:, 0:n], [pb], [b_pA[oc]])
                                else:
                                    evac(pM[:, oc, 15:31], ps[:, 0:16], [pb], [b_pM], eng="act")
                                    evac(pS[:, oc, :].rearrange("p (s c) -> p s c", c=19)[:, :, 15:19],
                                         ps[:, 16:48].rearrange("p (s q) -> p s q", q=4), [pb], [b_pS], eng="act")
                            elif oc < 16:
                                h = oc - 8
                                qt_, qb_ = k16.get()
                                evac(qt_[:, 0:n], ps[:, 0:n], [pb], [qb_], scale=0.125)
                                if ti < XT:
                                    dma("sp", scr_q[h * 128:(h + 1) * 128, t0:t0 + n], qt_[:, 0:n], [qb_], [B["scr_q"]])
                                else:
                                    dma("sp", scr_q[h * 128:(h + 1) * 128, CH:CH + 16], qt_[:, 0:16], [qb_], [B["scr_q"]])
                            elif oc < 24:
                                h = oc - 16
                                kf_, kfb = kfr.get()
                                evac(kf_[:, 0:n], ps[:, 0:n], [pb], [kfb], eng="act")
                                kb_, kbb = k16.get()
                                P.add("dve", lambda e, kb_=kb_, kf_=kf_, n=n: e.tensor_copy(out=kb_[:, 0:n], in_=kf_[:, 0:n]),
                                      [kfb], [kbb])
                                if ti < XT:
                                    dma("sp", src_kT[ti][h * 128:(h + 1) * 128, :], kb_[:, 0:n], [kbb], [B["src_kT"]])
                                else:
                                    dma("sp", scr_mk[h * 128:(h + 1) * 128, 0:16], kb_[:, 0:16], [kbb], [B["scr_mk"]])
                                tok_major_out(kf_, kfb, n, t0, h, k_o, False)
                            else:
                                h = oc - 24
                                vf_, vfb = kfr.get()
                                evac(vf_[:, 0:n], ps[:, 0:n], [pb], [vfb])
                                tok_major_out(vf_, vfb, n, t0, h, v_o, True)
                phase[0] = ""
                if cfg.get("STOP") == "A0":
                    P.barrier(); st.close(); mst.close(); return
                for part in range(3):
                    wt, wb = win.get()
                    P.add("pool", wload(wt[:, :, 0:128], wts["w_in_hd"], part * 128, 128, KC), [], [wb], kind="d")
                    ps, pb = next_ps()
                    P.add("pe", [lambda e, k=k, ps=ps, wt=wt: e.matmul(
                        ps[:, 0:32], lhsT=wt[:, k, 0:128], rhs=hT[:, k, CH + 16:CH + 48],
                        start=(k == 0), stop=(k == KC - 1)) for k in range(KC)],
                        [wb] + [hTb[k][XT] for k in range(KC)], [pb])
                    if part == 0:
                        for m_ in range(2):
                            P.add("act", lambda e, ps=ps, m_=m_: e.activation(
                                out=Qblk[64 * m_:64 * m_ + 64, :, 4 * m_:4 * m_ + 4],
                                in_=ps[64 * m_:64 * m_ + 64, 0:32].rearrange("p (s q) -> p s q", q=4),
                                func=AF.Copy, scale=0.125), [pb, b_Q], [b_Q])
                    elif part == 1:
                        evac(ksT[:, :], ps[:, 0:32], [pb], [b_ksT])
                    else:
                        evac(vsf[:, :], ps[:, 0:32], [pb], [b_vsf])
                        for half in range(2):
                            ps2, pb2 = next_ps()
                            P.add("pe", [lambda e, s4=s4, ps2=ps2, half=half: e.transpose(
                                out=ps2[0:4, s4 * 128:(s4 + 1) * 128], in_=vsf[:, (half * 4 + s4) * 4:(half * 4 + s4) * 4 + 4],
                                identity=ident_f) for s4 in range(4)], [b_vsf, b_cf], [pb2])
                            evac(Vnew[0:4, half * 4:half * 4 + 4, :], ps2[0:4, :].rearrange("p (s d) -> p s d", d=128),
                                 [pb2, b_Vn], [b_Vn])
                if cfg.get("STOP") == "A1":
                    P.barrier(); st.close(); mst.close(); return
                phst = kfr.t[0]; b_phst = kfr.b[0]
                P.add("dve", lambda e: e.memset(phst[:, 0:128], 0.0), [b_phst], [b_phst])
                P.add("dve", lambda e: e.tensor_copy(out=phst[:, 0:120].rearrange("p (k c) -> p k c", c=15), in_=pAll[:, :, CH:CH + 15]),
                      [b_phst] + b_pA, [b_phst])
                dma("sp", src_ph[:, :], phst[:, 0:128], [b_phst], [B["src_ph"]])
                dma("sp", pt_o.rearrange("p (k c) -> p k c", c=15), pAll[:, :, CH:CH + 15], b_pA, [B["out"]])
                for t in range(NQT):
                    cc(GROUPS4, src_kT[t][:, :], gat_kT[t][:, :], [B["src_kT"]], [B["gat_kT"]])
                    cc(GROUPS4, src_V[t][:, :], gat_V[t][:, :], [B["src_V"]], [B["gat_V"]])
                cc(GROUPS4, src_ph[:, :], gat_ph[:, :], [B["src_ph"]], [B["gat_ph"]])
                P.barrier()

            if cfg.get("STOP") == "A":
                mst.close(); return
            with ExitStack() as st:
                gph = sb("gph", [128, 4, 128], F32, st); b_gph = Buf("gph")
                spt = sb("spt", [120, 1024], F32, st); b_spt = Buf("spt")
                W1 = sb("W1", [128, 8, 271], F32, st); b_W1 = Buf("W1")
                W2 = sb("W2", [128, 8, 271], F32, st); b_W2 = Buf("W2")
                feat = sb("feat", [128, 8, 256], BF16, st); b_feat = Buf("feat")
                featS = sb("featS", [128, 8, 32], BF16, st); b_featS = Buf("featS")
                psc = sb("psc", [128, 8, 32], F32, st); b_psc = Buf("psc")
                wp_sb = sb("wp_sb", [128, 8, 256], BF16, st); b_wp = Buf("wp")
                P.add("pool", wload(wp_sb[:], wts["w_pool"], 0, 256, 8), [], [b_wp], kind="d")
                dma("sp", gph[:], gat_ph.rearrange("(r p) c -> p r c", p=128), [B["gat_ph"]], [b_gph])
                dma("sp", spt[:], spool[:, :], [], [b_spt])
                dma("sp", pso_o.rearrange("(s r) c -> s r c", r=11), spool.rearrange("(s r) c -> s r c", r=15)[:, 4:15, :],
                    [], [B["out"]])
                halo = pAll[:, :, 0:15]
                P.add("dve", lambda e: e.tensor_scalar(out=halo, in0=pM[:, :, 16:31], scalar1=cf("sel", 1, 4), scalar2=None,
                                                       op0=ALU.mult), [b_pM, b_cf] + b_pA, b_pA)
                for r in range(4):
                    P.add("dve", lambda e, r=r: e.scalar_tensor_tensor(
                        out=halo, in0=gph[:, r, 0:120].rearrange("p (k c) -> p k c", c=15), scalar=cf("sel", 1, r), in1=halo,
                        op0=ALU.mult, op1=ALU.add), [b_gph, b_cf] + b_pA, b_pA)
                for k4 in range(2):
                    ps, pb = next_ps()
                    P.add("pe", [lambda e, kk=kk, k4=k4, ps=ps: e.transpose(
                        out=ps[:, kk * 128:kk * 128 + 120], in_=spt[0:120, (k4 * 4 + kk) * 128:(k4 * 4 + kk + 1) * 128],
                        identity=ident_f[0:120, 0:120]) for kk in range(4)], [b_spt, b_cf], [pb])
                    for kk in range(4):
                        evac(pS[:, k4 * 4 + kk, :].rearrange("p (s c) -> p s c", c=19)[:, :, 0:15],
                             ps[:, kk * 128:kk * 128 + 120].rearrange("p (s c) -> p s c", c=15), [pb, b_pS], [b_pS], eng="act")
                for s in range(8):
                    P.add("dve", lambda e, s=s: e.tensor_copy(out=psc[:, :, s * 4:s * 4 + 4], in_=pS[:, :, s * 19 + 15:s * 19 + 19]),
                          [b_pS, b_psc], [b_psc])
                dma("sp", psn_o.rearrange("p (k t) -> p k t", k=8), psc[:, :, :], [b_psc], [B["out"]])

                def windows(X, L, xbufs):
                    TTa = mybir.AluOpType.add
                    P.add("dve", lambda e: e.tensor_tensor(out=W1[:, 0:8, 1:L], in0=X[:, 0:8, 1:L], in1=X[:, 0:8, 0:L - 1], op=TTa),
                          xbufs + [b_W1], [b_W1])
                    P.add("dve", lambda e: e.tensor_tensor(out=W2[:, 2:8, 3:L], in0=W1[:, 2:8, 3:L], in1=W1[:, 2:8, 1:L - 2], op=TTa),
                          [b_W1, b_W2], [b_W2])
                    P.add("dve", lambda e: e.tensor_tensor(out=W1[:, 4:8, 7:L], in0=W2[:, 4:8, 7:L], in1=W2[:, 4:8, 3:L - 4], op=TTa),
                          [b_W2, b_W1], [b_W1])
                    P.add("dve", lambda e: e.tensor_tensor(out=W2[:, 6:8, 15:L], in0=W1[:, 6:8, 15:L], in1=W1[:, 6:8, 7:L - 8], op=TTa),
                          [b_W1, b_W2], [b_W2])
                    return [W1, W2, W1, W2]

                def pool_mm(ft, fbuf, n, c0, ti):
                    for g in range(4):
                        for dc in range(2):
                            ps, pb = next_ps()
                            P.add("pe", [lambda e, cc_=cc_, ps=ps, g=g, dc=dc: e.matmul(
                                ps[:, 0:n], lhsT=wp_sb[:, 2 * g + cc_, dc * 128:(dc + 1) * 128], rhs=ft[:, 2 * g + cc_, 0:n],
                                start=(cc_ == 0), stop=(cc_ == 1)) for cc_ in range(2)], [b_wp, fbuf], [pb])
                            oc = 2 * g + dc
                            evac(hT[:, oc, c0:c0 + n], ps[:, 0:n], [pb, b_cv, hTb[oc][ti]], [hTb[oc][ti]],
                                 scale=cv_sb[:, 64 + oc:65 + oc])

                for t0 in range(0, CH, 256):
                    ti, n = t0 // 512, 256
                    X = pAll[:, :, t0:t0 + n + 15]
                    res = windows(X, n + 15, list(b_pA))
                    for g in range(4):
                        P.add("dve", lambda e, g=g, R=res[g], X=X, n=n: e.scalar_tensor_tensor(
                            out=feat[:, 2 * g:2 * g + 2, 0:n], in0=R[:, 2 * g:2 * g + 2, 15:15 + n], scalar=1.0 / WINS[g],
                            in1=X[:, 2 * g:2 * g + 2, 15:15 + n], op0=ALU.mult, op1=ALU.subtract),
                            [b_W1, b_W2, b_feat] + b_pA, [b_feat])
                    pool_mm(feat, b_feat, n, t0, ti)
                res = windows(pM, 31, [b_pM])
                for g in range(4):
                    P.add("dve", lambda e, g=g, R=res[g]: e.tensor_tensor(
                        out=R[:, 2 * g:2 * g + 2, 15:31], in0=R[:, 2 * g:2 * g + 2, 15:31],
                        in1=cf("invc", 128).rearrange("p (k t) -> p k t", t=16)[:, 2 * g:2 * g + 2, :], op=ALU.mult),
                        [b_W1, b_W2, b_cf], [b_W1, b_W2])
                    P.add("dve", lambda e, g=g, R=res[g]: e.tensor_tensor(
                        out=feat[:, 2 * g:2 * g + 2, 0:16], in0=R[:, 2 * g:2 * g + 2, 15:31], in1=pM[:, 2 * g:2 * g + 2, 15:31],
                        op=ALU.subtract), [b_W1, b_W2, b_pM, b_feat], [b_feat])
                pool_mm(feat, b_feat, 16, CH, XT)
                res = windows(pS, 152, [b_pS])
                for g in range(4):
                    P.add("dve", lambda e, g=g, R=res[g]: e.scalar_tensor_tensor(
                        out=R[:, 2 * g:2 * g + 2, 15:152], in0=R[:, 2 * g:2 * g + 2, 15:152], scalar=1.0 / WINS[g],
                        in1=pS[:, 2 * g:2 * g + 2, 15:152], op0=ALU.mult, op1=ALU.subtract),
                        [b_W1, b_W2, b_pS], [b_W1, b_W2])
                    for s in range(8):
                        P.add("dve", lambda e, g=g, R=res[g], s=s: e.tensor_copy(
                            out=featS[:, 2 * g:2 * g + 2, s * 4:s * 4 + 4], in_=R[:, 2 * g:2 * g + 2, s * 19 + 15:s * 19 + 19]),
                            [b_W1, b_W2, b_featS], [b_featS])
                pool_mm(featS, b_featS, 32, CH + 16, XT)
                P.barrier()
            mst.close()

            if cfg.get("STOP") == "B1":
                return
            with ExitStack() as st:
                NVT = 5 * NKT + 1
                kTg = sb("kTg", [128, 4, CH], BF16, st)
                kTo = sb("kTo", [128, CH], BF16, st)
                kTm = sb("kTm", [128, 16], BF16, st)
                Vh = sb("Vh", [128, NVT, 129], BF16, st)
                qh = sb("qh", [128, CH + 16], BF16, st)
                qbt = sb("qbt", [128, CH], F32, st)
                tmpr = Rot("tmp", [128, 512], F32, 3, st)
                Ar = Rot("A", [128, 512], BF16, 4, st)
                on1 = sb("on1", [128, 4, 128], F32, st); b_on1 = Buf("on1")
                fin = sb("fin", [128, 4, 128], F32, st); b_fin = Buf("fin")
                sqt = sb("sqt", [128, 128], F32, st); b_sqt = Buf("sqt")
                rc = sb("rc", [128, 16], F32, st); b_rc = Buf("rc")
                resb = sb("resb", [128, 4, 128], BF16, st); b_resb = Buf("resb")
                b_kTg, b_kTo, b_kTm, b_Vg, b_Vo, b_Vm, b_qh, b_qbt = (Buf(x) for x in ("kTg", "kTo", "kTm", "Vg", "Vo", "Vm", "qh", "qbt"))
                P.add("dve", lambda e: e.memset(Vh[:, :, 128:129], 1.0), [], [b_Vg, b_Vo, b_Vm])
                acc_ps = [psum[4], psum[5]]
                acc_b = [psb[4], psb[5]]
                c5 = [0]

                def next_ps5():
                    i = c5[0] % 4
                    c5[0] += 1
                    return psum[i], psb[i]

                def attend(h, q0, nq, subs, keytiles, outc0, ti_out):
                    for m_ in range(2):
                        p0 = 64 * m_
                        nkts = len(keytiles)
                        pend = []
                        LA = 2
                        def emit_av(item):
                            A_, Ab, V_ap, vbuf, nk, ki = item
                            fns = []
                            if ki == 0:
                                rows0 = subs[0][1]
                                nb = (len(subs) + 2) // 3
                                for bi in range(nb):
                                    ncol = 129 * min(3, len(subs) - 3 * bi)
                                    fns.append(lambda e, bi=bi, ncol=ncol, rows0=rows0: e.matmul(
                                        acc_ps[bi][0:rows0, 0:ncol], lhsT=cb("zerob", 128)[:, 0:rows0], rhs=cb("maskd", 512)[:, 0:ncol],
                                        start=True, stop=False))
                            for si, (so, rows) in enumerate(subs):
                                accp = acc_ps[si // 3]
                                c0 = (si % 3) * 129
                                last_in_bank = (si % 3 == 2) or (si == len(subs) - 1)
                                fns.append(lambda e, A_=A_, so=so, rows=rows, accp=accp, c0=c0, V_ap=V_ap, nk=nk, ki=ki, lib=last_in_bank: e.matmul(
                                    accp[0:rows, c0:c0 + 129], lhsT=A_[0:nk, so:so + rows], rhs=V_ap[0:nk, :],
                                    start=False, stop=(ki == nkts - 1 and lib)))
                            P.add("pe", fns, [Ab, vbuf, b_cb] + acc_b, acc_b)

                        for ki, (kT_ap, kbuf, V_ap, vbuf, bias_ap, nk, mask_ap) in enumerate(keytiles):
                            ps, pb = next_ps5()
                            fns = [lambda e, ps=ps, kT_ap=kT_ap, nk=nk, mask_ap=mask_ap, p0=p0: e.matmul(
                                ps[0:nk, 0:nq], lhsT=kT_ap[p0:p0 + 64, 0:nk], rhs=qh[p0:p0 + 64, q0:q0 + nq],
                                start=True, stop=(mask_ap is None))]
                            if mask_ap is not None:
                                fns.append(lambda e, ps=ps, nk=nk, mask_ap=mask_ap: e.matmul(
                                    ps[0:nk, 0:nq], lhsT=ident_b[0:nk, 0:nk], rhs=mask_ap, start=False, stop=True))
                            P.add("pe", fns, [kbuf, b_qh, b_cb], [pb])
                            tm, tmb = tmpr.get()
                            P.add("dve", lambda e, tm=tm, ps=ps, nk=nk: e.tensor_tensor(
                                out=tm[0:nk, 0:nq], in0=ps[0:nk, 0:nq], in1=qbt[0:nk, 0:nq] if q0 >= CH else qbt[0:nk, q0:q0 + nq],
                                op=ALU.add), [pb, b_qbt], [tmb])
                            A_, Ab = Ar.get()
                            P.add("act", lambda e, A_=A_, tm=tm, nk=nk, bias_ap=bias_ap: e.activation(
                                out=A_[0:nk, 0:nq], in_=tm[0:nk, 0:nq], func=AF.Exp, bias=bias_ap, scale=1.0), [tmb, b_cf], [Ab])
                            pend.append((A_, Ab, V_ap, vbuf, nk, ki))
                            if len(pend) > LA:
                                emit_av(pend.pop(0))
                            if nq == 512 and ki % 12 == 11:
                                sample_step()
                        while pend:
                            emit_av(pend.pop(0))
                        for si, (so, rows) in enumerate(subs):
                            accp = acc_ps[si // 3]
                            c0 = (si % 3) * 129
                            P.add("dve", lambda e, accp=accp, c0=c0, rows=rows, si=si: e.reciprocal(
                                out=rc[0:rows, si:si + 1], in_=accp[0:rows, c0 + 128:c0 + 129]), acc_b + [b_rc], [b_rc])
                            if m_ == 0:
                                P.add("dve", lambda e, accp=accp, c0=c0, rows=rows, si=si: e.tensor_scalar(
                                    out=on1[0:rows, si, :], in0=accp[0:rows, c0:c0 + 128], scalar1=rc[0:rows, si:si + 1],
                                    scalar2=None, op0=ALU.mult), acc_b + [b_rc, b_on1], [b_on1])
                            else:
                                P.add("dve", lambda e, accp=accp, c0=c0, rows=rows, si=si: e.tensor_scalar(
                                    out=fin[0:rows, si, :], in0=accp[0:rows, c0:c0 + 128], scalar1=rc[0:rows, si:si + 1],
                                    scalar2=None, op0=ALU.mult), acc_b + [b_rc, b_fin], [b_fin])
                                P.add("dve", lambda e, rows=rows, si=si: e.scalar_tensor_tensor(
                                    out=fin[0:rows, si, :], in0=fin[0:rows, si, :], scalar=lam_sb[0:rows, 1:2], in1=on1[0:rows, si, :],
                                    op0=ALU.mult, op1=ALU.add), [b_fin, b_on1, b_lam], [b_fin])
                    for si, (so, rows) in enumerate(subs):
                        P.add("dve", lambda e, rows=rows, si=si: e.tensor_tensor(
                            out=sqt[0:rows, :], in0=fin[0:rows, si, :], in1=fin[0:rows, si, :], op=ALU.mult), [b_fin, b_sqt], [b_sqt])
                        P.add("dve", lambda e, rows=rows, si=si: e.reduce_sum(
                            out=rc[0:rows, 8 + si:9 + si], in_=sqt[0:rows, :], axis=mybir.AxisListType.X), [b_sqt, b_rc], [b_rc])
                        P.add("act", lambda e, rows=rows, si=si: e.activation(
                            out=rc[0:rows, 8 + si:9 + si], in_=rc[0:rows, 8 + si:9 + si], func=AF.Sqrt, bias=eps_col[0:rows, :],
                            scale=1.0 / 128), [b_rc, b_cf], [b_rc])
                        P.add("dve", lambda e, rows=rows, si=si: e.reciprocal(
                            out=rc[0:rows, 8 + si:9 + si], in_=rc[0:rows, 8 + si:9 + si]), [b_rc], [b_rc])
                        P.add("dve", lambda e, rows=rows, si=si: e.scalar_tensor_tensor(
                            out=resb[0:rows, si, :], in0=fin[0:rows, si, :], scalar=rc[0:rows, 8 + si:9 + si], in1=gain08[0:rows, :],
                            op0=ALU.mult, op1=ALU.mult), [b_fin, b_rc, b_g08, b_resb], [b_resb])
                        P.add("pe", lambda e, rows=rows, si=si: e.transpose(
                            out=psbf[:, si * 128:si * 128 + rows], in_=resb[0:rows, si, :], identity=ident_b[0:rows, 0:rows]),
                            [b_resb, b_cb, b_psbf], [b_psbf])
                        evac(hT[:, 8 + h, outc0 + so:outc0 + so + rows], psbf[:, si * 128:si * 128 + rows],
                             [b_psbf, hTb[8 + h][ti_out]], [hTb[8 + h][ti_out]])

                def sample_gen():
                    Ku = Rot("Ku", [128, 2048], BF16, 2, st)
                    Vu = Rot("Vu", [128, 2048], BF16, 2, st)
                    kTu = Rot("kTu", [128, 2048], BF16, 2, st)
                    Eu = Rot("Eu", [128, 128], F32, 2, st)
                    A_all = sb("A_all", [128, PAGE * 8], BF16, st); b_Aall = Buf("Aall")
                    rsum = sb("rsum", [128, 8], F32, st); b_rsum = Buf("rsum")
                    An32 = sb("An32", [4, 8], F32, st); b_An32 = Buf("An32")
                    An = sb("An", [4, 8], BF16, st); b_An = Buf("An")
                    osb = sb("osb", [8, 128], F32, st); b_osb = Buf("osb")
                    orc = sb("orc", [8, 4], F32, st); b_orc = Buf("orc")
                    f4 = sb("f4", [4, 128], F32, st); b_f4 = Buf("f4")
                    sq4 = sb("sq4", [4, 128], F32, st); b_sq4 = Buf("sq4")
                    r4 = sb("r4", [4, 2], F32, st); b_r4 = Buf("r4")
                    res_all = sb("res_all", [4, 8, 128], F32, st); b_res = Buf("res_all")
                    osamp = sb("osamp", [32, 8, 128], F32, st); b_osamp = Buf("osamp")
                    o_ps, o_pb = psum[6], psb[6]
                    for s in range(8):
                        P.add("pe", lambda e: e.matmul(o_ps[0:8, 0:129], lhsT=cb("zerob", 128)[:, 0:8], rhs=cb("maskd", 512)[:, 0:129],
                                                       start=True, stop=False), [b_cb, o_pb], [o_pb])
                        for u in range(NSUB):
                            ku, kub = Ku.get()
                            P.add("pool", lambda e, ku=ku, u=u, s=s: e.indirect_dma_start(
                                out=ku[:], out_offset=None, in_=cks[u][:, :],
                                in_offset=bass.IndirectOffsetOnAxis(ap=pt_sb[:, s:s + 1], axis=0)), [b_pt], [kub], kind="d")
                            vu, vub = Vu.get()
                            P.add("pool", lambda e, vu=vu, u=u, s=s: e.indirect_dma_start(
                                out=vu[:], out_offset=None, in_=cvs[u][:, :],
                                in_offset=bass.IndirectOffsetOnAxis(ap=pt_sb[:, s:s + 1], axis=0)), [b_pt], [vub], kind="d")
                            ktu, ktub = kTu.get()
                            for half in range(2):
                                P.add("pe", [lambda e, tk=tk, ku=ku, half=half: e.transpose(
                                    out=psbf[:, tk * 128:(tk + 1) * 128], in_=ku[:, (half * 8 + tk) * 128:(half * 8 + tk + 1) * 128],
                                    identity=ident_b) for tk in range(8)], [kub, b_cb, b_psbf], [b_psbf])
                                evac(ktu[:, half * 1024:(half + 1) * 1024], psbf[:, :], [b_psbf, ktub], [ktub])
                            ps, pb = next_ps5()
                            P.add("pe", [lambda e, tk=tk, ps=ps, ktu=ktu, s=s: e.matmul(
                                ps[:, tk * 8:(tk + 1) * 8], lhsT=ktu[:, tk * 128:(tk + 1) * 128], rhs=Qblk[:, s, :],
                                start=True, stop=True) for tk in range(16)], [ktub, b_Q], [pb])
                            eu, eub = Eu.get()
                            P.add("act", lambda e, eu=eu, ps=ps: e.activation(out=eu[:], in_=ps[:, 0:128], func=AF.Exp,
                                                                             bias=cf("sbias"), scale=1.0), [pb, b_cf], [eub])
                            P.add("dve", lambda e, eu=eu, u=u: e.tensor_tensor(
                                out=A_all[:, u * 128:(u + 1) * 128], in0=eu[:], in1=cf("wtok", 128, u * 128), op=ALU.mult),
                                [eub, b_cf, b_Aall], [b_Aall])
                            P.add("pe", [lambda e, tk=tk, vu=vu, u=u: e.matmul(
                                o_ps[0:8, 0:128], lhsT=A_all[:, (u * 16 + tk) * 8:(u * 16 + tk) * 8 + 8], rhs=vu[:, tk * 128:(tk + 1) * 128],
                                start=False, stop=False) for tk in range(16)] + [lambda e, tk=tk, u=u: e.matmul(
                                o_ps[0:8, 128:129], lhsT=A_all[:, (u * 16 + tk) * 8:(u * 16 + tk) * 8 + 8], rhs=ones_b[:, 0:1],
                                start=False, stop=False) for tk in range(16)], [b_Aall, vub, o_pb, b_cb], [o_pb])
                            yield
                        ps, pb = next_ps5()
                        P.add("pe", lambda e, ps=ps, s=s: e.matmul(ps[0:4, 0:8], lhsT=ksT[:, 4 * s:4 * s + 4], rhs=Qblk[:, s, :],
                                                                   start=True, stop=True), [b_ksT, b_Q], [pb])
                        P.add("act", lambda e, ps=ps: e.activation(out=An32[:], in_=ps[0:4, 0:8], func=AF.Exp,
                                                                   bias=cf("nbias")[0:4, :], scale=1.0), [pb, b_cf, b_An32], [b_An32])
                        P.add("dve", lambda e: e.tensor_tensor(out=An32[:], in0=An32[:], in1=cf("nmask", 8)[0:4, :], op=ALU.mult),
                              [b_An32, b_cf], [b_An32])
                        P.add("dve", lambda e: e.tensor_copy(out=An[:], in_=An32[:]), [b_An32, b_An], [b_An])
                        P.add("pe", [lambda e, s=s: e.matmul(o_ps[0:8, 0:128], lhsT=An[0:4, :], rhs=Vnew[0:4, s, :], start=False, stop=False),
                                     lambda e: e.matmul(o_ps[0:8, 128:129], lhsT=An[0:4, :], rhs=ones_b[0:4, 0:1], start=False, stop=True)],
                              [b_An, b_Vn, o_pb, b_cb], [o_pb])
                        P.add("dve", lambda e: e.reciprocal(out=orc[:, 0:1], in_=o_ps[0:8, 128:129]), [o_pb, b_orc], [b_orc])
                        P.add("dve", lambda e: e.tensor_scalar(out=osb[:], in0=o_ps[0:8, 0:128], scalar1=orc[:, 0:1], scalar2=None,
                                                               op0=ALU.mult), [o_pb, b_orc, b_osb], [b_osb])
                        ps, pb = next_ps5()
                        P.add("pe", lambda e, ps=ps: e.matmul(ps[0:4, 0:128], lhsT=Cm[0:8, :], rhs=osb[0:8, :], start=True, stop=True),
                              [b_Cm, b_osb], [pb])
                        P.add("dve", lambda e, ps=ps: e.tensor_copy(out=f4[:], in_=ps[0:4, 0:128]), [pb, b_f4], [b_f4])
                        P.add("dve", lambda e: e.tensor_tensor(out=sq4[:], in0=f4[:], in1=f4[:], op=ALU.mult), [b_f4, b_sq4], [b_sq4])
                        P.add("dve", lambda e: e.reduce_sum(out=r4[:, 0:1], in_=sq4[:], axis=mybir.AxisListType.X), [b_sq4, b_r4], [b_r4])
                        P.add("act", lambda e: e.activation(out=r4[:, 0:1], in_=r4[:, 0:1], func=AF.Sqrt, bias=eps_col[0:4, :],
                                                            scale=1.0 / 128), [b_r4, b_cf], [b_r4])
                        P.add("dve", lambda e: e.reciprocal(out=r4[:, 0:1], in_=r4[:, 0:1]), [b_r4], [b_r4])
                        P.add("dve", lambda e, s=s: e.scalar_tensor_tensor(
                            out=res_all[0:4, s, :], in0=f4[:], scalar=r4[:, 0:1], in1=gain08[0:4, :], op0=ALU.mult, op1=ALU.mult),
                            [b_f4, b_r4, b_g08, b_res], [b_res])
                    dma("sp", src_o.rearrange("(s q) d -> q s d", q=4), res_all[0:4, :, :], [b_res], [B["src_o"]])
                    cc(GROUPS4, src_o[:, :], gat_o4[:, :], [B["src_o"]], [B["gat_o4"]])
                    cc(PAIRS, gat_o4[:, :], gat_o8[:, :], [B["gat_o4"]], [B["gat_o8"]])
                    dma("sp", osamp[:], gat_o8.rearrange("(h t) d -> t h d", t=32), [B["gat_o8"]], [b_osamp])
                    ps, pb = next_ps5()
                    P.add("pe", [lambda e, h=h, ps=ps: e.transpose(out=ps[:, h * 32:(h + 1) * 32], in_=osamp[0:32, h, :],
                                                                  identity=ident_f[0:32, 0:32]) for h in range(8)],
                          [b_osamp, b_cf], [pb])
                    evac(hT[:, 8:16, CH + 16:CH + 48], ps[:, 0:256].rearrange("p (h t) -> p h t", t=32),
                         [pb] + [hTb[8 + h][XT] for h in range(8)], [hTb[8 + h][XT] for h in range(8)])
                    P.barrier()


                sgen = sample_gen()
                sdone = [False]

                def sample_step():
                    if not sdone[0]:
                        try:
                            next(sgen)
                        except StopIteration:
                            sdone[0] = True

                for h in range(NH):
                    for t in range(NQT):
                        dma("sp", kTg[:, :, t * 512:(t + 1) * 512], gat_kT[t].rearrange("(r hd) t -> hd r t", r=4)[h * 128:(h + 1) * 128, :, :], [B["gat_kT"]], [b_kTg])
                        dma("sp", kTo[:, t * 512:(t + 1) * 512], src_kT[t][h * 128:(h + 1) * 128, :], [B["src_kT"]], [b_kTo])
                    dma("sp", kTm[:], scr_mk[h * 128:(h + 1) * 128, :], [B["scr_mk"]], [b_kTm])
                    for t in range(NQT):
                        for r in range(4):
                            dma("sp", Vh[:, r * NKT + t * 4:r * NKT + t * 4 + 4, 0:128],
                                gat_V[t][r * 512:(r + 1) * 512, h * 128:(h + 1) * 128].rearrange("(k p) c -> p k c", p=128),
                                [B["gat_V"]], [b_Vg])
                        dma("sp", Vh[:, 4 * NKT + t * 4:4 * NKT + t * 4 + 4, 0:128],
                            src_V[t][:, h * 128:(h + 1) * 128].rearrange("(k p) c -> p k c", p=128), [B["src_V"]], [b_Vo])
                    dma("sp", Vh[0:16, 5 * NKT, 0:128], scr_mv[0:16, h * 128:(h + 1) * 128], [B["scr_mv"]], [b_Vm])
                    dma("sp", qh[:], scr_q[h * 128:(h + 1) * 128, :], [B["scr_q"]], [b_qh])
                    P.add("dve", lambda e, h=h: e.tensor_scalar(out=qbt[:], in0=cf("qrow", CH), scalar1=-SLOPES[h], scalar2=None,
                                                                op0=ALU.mult), [b_cf, b_qbt], [b_qbt])
                    meta_kt = (kTm, b_kTm, Vh[:, 5 * NKT, :], b_Vm, cf("biasm", 1, h)[0:16, :], 16, None)
                    for qt in range(NQT):
                        kts = []
                        for r in range(3):
                            for kt in range(NKT):
                                kts.append((kTg[:, r, kt * 128:(kt + 1) * 128], b_kTg, Vh[:, r * NKT + kt, :], b_Vg,
                                            cf("biasg", 1, (h * 4 + r) * NKT + kt), 128, None))
                        kts.append(meta_kt)
                        for kt in range(4 * qt + 4):
                            m = kt - 4 * qt
                            kts.append((kTo[:, kt * 128:(kt + 1) * 128], b_kTo, Vh[:, 4 * NKT + kt, :], b_Vo,
                                        cf("biaso", 1, h * NKT + kt), 128, cb("maskd", 512, m * 512) if m >= 0 else None))
                        attend(h, qt * 512, 512, [(i * 128, 128) for i in range(4)], kts, qt * 512, qt)
                    mm_kt = (kTm, b_kTm, Vh[:, 5 * NKT, :], b_Vm, cf("biasmm", 1, h)[0:16, :], 16, cb("maskmm", 16)[0:16, :])
                    attend(h, CH, 16, [(0, 16)], [mm_kt], CH, XT)
                while not sdone[0]:
                    sample_step()
                P.barrier()

            if cfg.get("STOP") == "B3":
                return
            with ExitStack() as st:
                wo = Rot("wo", [128, KC, 256], BF16, 3, st)
                for oc0 in range(0, KC, 2):
                    wt, wb = wo.get()
                    P.add("pool", wload(wt[:], wts["w_out"], oc0 * 128, 256, KC), [], [wb], kind="d")
                    for ff in range(2):
                        oc = oc0 + ff
                        for ti, (t0, n) in enumerate(TT):
                            ps, pb = next_ps()
                            P.add("pe", [lambda e, k=k, ps=ps, wt=wt, ff=ff, t0=t0, n=n: e.matmul(
                                ps[:, 0:n], lhsT=wt[:, k, ff * 128:(ff + 1) * 128], rhs=hT[:, k, t0:t0 + n],
                                start=(k == 0), stop=(k == KC - 1)) for k in range(KC)],
                                [wb] + [hTb[k][ti] for k in range(KC)], [pb])
                            P.add("dve", lambda e, ps=ps, oc=oc, t0=t0, n=n: e.tensor_tensor(
                                out=xT[:, oc, t0:t0 + n], in0=ps[:, 0:n], in1=xT[:, oc, t0:t0 + n], op=ALU.add),
                                [pb, xTb[oc][ti]], [xTb[oc][ti]])
                P.barrier()


        ffn(0, wts["w_gate1"], wts["w_up1"], wts["w_down1"])
        if cfg.get("MIXER", True):
            mixer(locals())
        ffn(32, wts["w_gate2"], wts["w_up2"], wts["w_down2"])

        with ExitStack() as st:
            ytok = Rot("ytok", [128, D], F32, 2, st)
            yf = sb("yf", [128, KC, 128], F32, st)
            b_yf = Buf("yf")
            gF = 48
            for ti, (t0, n) in enumerate(TT):
                rs, rsb = rms_rstd(ti)
                for s0 in range(0, n, 128):
                    ns = min(128, n - s0)
                    for k in range(KC):
                        P.add("dve", lambda e, k=k, rs=rs, t0=t0, s0=s0, ns=ns: e.scalar_tensor_tensor(
                            out=yf[:, k, 0:ns], in0=xT[:, k, t0 + s0:t0 + s0 + ns], scalar=cv_sb[:, gF + k:gF + k + 1],
                            in1=rs[:, s0:s0 + ns], op0=ALU.mult, op1=ALU.mult), [xTb[k][ti], rsb, b_cv, b_yf], [b_yf])
                    yt, ytb = ytok.get()
                    for k4 in range(KC // 4):
                        ps, pb = next_ps()
                        P.add("pe", [lambda e, kk=kk, k4=k4, ps=ps, ns=ns: e.transpose(
                            out=ps[0:ns, kk * 128:(kk + 1) * 128], in_=yf[:, k4 * 4 + kk, 0:ns], identity=ident_f)
                            for kk in range(4)], [b_yf, b_cf], [pb])
                        evac(yt[0:ns, k4 * 512:(k4 + 1) * 512], ps[0:ns, :], [pb], [ytb])
                    dma("sp", y_o[t0 + s0:t0 + s0 + ns, :], yt[0:ns, :], [ytb], [B["out"]])
            P.barrier()

        streams = P.emit(nc, sems, None)
        with nc.Block() as block:
            @block.tensor
            def _(e):
                run_streams({"pe": streams["pe"]}, sems, {"pe": e})

            @block.scalar
            def _(e):
                run_streams({"act": streams["act"]}, sems, {"act": e})

            @block.vector
            def _(e):
                run_streams({"dve": streams["dve"]}, sems, {"dve": e})

            @block.gpsimd
            def _(e):
                run_streams({"pool": streams["pool"]}, sems, {"pool": e})

            @block.sync
            def _(e):
                run_streams({"sp": streams["sp"]}, sems, {"sp": e})
    return nc


def host_consts(cfg, core):
    import ml_dtypes
    CH, PAGE = cfg["CH"], cfg["PAGE"]
    NKT = CH // 128
    j = core % 4
    off = {}
    cols = []
    i = np.arange(128, dtype=np.float64)[:, None]

    def addf(name, arr):
        off[name] = sum(a.shape[1] for a in cols)
        a = np.zeros((128, np.asarray(arr).shape[1]), np.float32)
        a[:np.asarray(arr).shape[0]] = np.asarray(arr, np.float32)
        cols.append(a)

    addf("ident", np.eye(128))
    addf("onesf", np.ones((128, 128)))
    addf("eps", np.full((128, 1), EPS))
    sel = np.zeros((128, 5));
    if j == 0:
        sel[:, 4] = 1.0
    else:
        sel[:, j - 1] = 1.0
    addf("sel", sel)
    bg = np.zeros((128, NH * 4 * NKT))
    bo = np.zeros((128, NH * NKT))
    bm = np.zeros((128, NH)); bmm = np.zeros((128, NH))
    for h in range(NH):
        sl = SLOPES[h]
        for r in range(4):
            for kt in range(NKT):
                bg[:, (h * 4 + r) * NKT + kt] = sl * (i[:, 0] + 128 * kt + CH * (r - j)) + (NEG if r >= j else 0.0)
        for kt in range(NKT):
            bo[:, h * NKT + kt] = sl * (i[:, 0] + 128 * kt)
        bm[:, h] = sl * (i[:, 0] - 16 - j * CH)
        bmm[:, h] = sl * i[:, 0]
    addf("biasg", bg); addf("biaso", bo); addf("biasm", bm); addf("biasmm", bmm)
    addf("qrow", np.broadcast_to(np.arange(CH, dtype=np.float64)[None, :], (128, CH)))
    slc = SLOPES[core]
    addf("sbias", slc * PAGE * (i - 127))
    tok = np.repeat(np.arange(PAGE), 8)[None, :]
    addf("wtok", np.broadcast_to(np.exp(slc * (tok - (PAGE - 1))), (128, PAGE * 8)))
    addf("nbias", slc * (1 + i))
    nm = np.zeros((128, 8))
    for jn in range(4):
        for m in range(2):
            for q in range(4):
                nm[jn, m * 4 + q] = 1.0 if jn <= q else 0.0
    addf("nmask", nm)
    invc = np.zeros((128, 128))
    for kc in range(8):
        for t in range(16):
            invc[:, kc * 16 + t] = 1.0 / min(t + 1, WINS[kc // 2])
    addf("invc", invc)
    c0 = np.zeros((128, 4)); c1 = np.zeros((128, 4))
    for q in range(4):
        c0[q, q] = 1.0; c1[4 + q, q] = 1.0
    addf("cb0", c0); addf("cb1", c1)
    cf = np.concatenate(cols, 1)
    colsb = []

    def addb(name, arr):
        off[name] = sum(a.shape[1] for a in colsb)
        colsb.append(np.asarray(arr, np.float32).astype(ml_dtypes.bfloat16))

    addb("identb", np.eye(128))
    addb("onesb", np.ones((128, 128)))
    addb("zerob", np.zeros((128, 128)))
    jq = np.arange(512)[None, :]
    md = np.concatenate([np.where(i + 128 * m <= jq, 0.0, NEG) for m in range(4)], 1)
    addb("maskd", md)
    addb("maskmm", np.where(i <= np.arange(16)[None, :], 0.0, NEG))
    cb = np.concatenate(colsb, 1)
    return off, cf, cb


def fm(v, n):
    return np.ascontiguousarray(np.asarray(v, np.float32).reshape(n, 128).T)


def prep_inputs(cfg, inp):
    CH, NSUB = cfg["CH"], cfg["NSUB"]
    maps = []
    x_prompt = np.asarray(inp["x_prompt"]); x_sample = np.asarray(inp["x_sample"])
    meta = np.asarray(inp["meta_tokens"])
    cvec = np.concatenate([fm(inp["norm_ffn1"][0], 16), fm(inp["norm_mix"][0], 16), fm(inp["norm_ffn2"][0], 16),
                           fm(inp["norm_final"], 16), fm(inp["pool_scale"][0], 8)], 1)
    gain_bc = np.ascontiguousarray(np.broadcast_to(np.asarray(inp["subln_gain"][0], np.float32)[None, :], (128, 128)))
    lamv = np.concatenate([np.asarray(inp[k][0], np.float32) for k in ("lambda_q1", "lambda_k1", "lambda_q2", "lambda_k2")])[None, :]
    w_in = np.asarray(inp["w_in"][0])
    shared = dict(cvec=cvec, gain_bc=gain_bc, lamv=np.ascontiguousarray(lamv),
                  w_gate1=np.asarray(inp["w_gate1"][0]), w_up1=np.asarray(inp["w_up1"][0]), w_down1=np.asarray(inp["w_down1"][0]),
                  w_in=w_in, w_out=np.asarray(inp["w_out"][0]),
                  w_gate2=np.asarray(inp["w_gate2"][0]), w_up2=np.asarray(inp["w_up2"][0]), w_down2=np.asarray(inp["w_down2"][0]),
                  w_pool=np.ascontiguousarray(np.asarray(inp["w_pool"][0]).reshape(1024, 256)),
                  spool=np.ascontiguousarray(np.asarray(inp["state_pool"][0]).reshape(120, 1024)),
                  ptab=np.ascontiguousarray(np.asarray(inp["page_table"]).T.astype(np.int32)))
    ckf = np.asarray(inp["cache_k"][0]); cvf = np.asarray(inp["cache_v"][0])
    npool = ckf.shape[0]
    for c in range(8):
        b, j = c // 4, c % 4
        m = dict(shared)
        m["xin"] = np.ascontiguousarray(np.concatenate([x_prompt[b, j * CH:(j + 1) * CH], meta, x_sample.reshape(32, D)], 0))
        m["w_in_hd"] = np.ascontiguousarray(np.concatenate([w_in[:, 1024 + 128 * c:1152 + 128 * c],
                                                            w_in[:, 2048 + 128 * c:2176 + 128 * c],
                                                            w_in[:, 3072 + 128 * c:3200 + 128 * c]], 1))
        for u in range(NSUB):
            m[f"ck{u}"] = np.ascontiguousarray(ckf[:, 16 * u:16 * u + 16, c, :].reshape(npool, 2048))
            m[f"cv{u}"] = np.ascontiguousarray(cvf[:, 16 * u:16 * u + 16, c, :].reshape(npool, 2048))
        off, cf, cb = host_consts(cfg, c)
        m["cf32"], m["cbf"] = cf, cb
        maps.append(m)
    return maps


def finish_cfg(cfg):
    off, cf, cb = host_consts(cfg, 0)
    cfg["OFF"] = off; cfg["NCF"] = cf.shape[1]; cfg["NCB"] = cb.shape[1]
    return cfg


def assemble(cfg, res):
    CH, SEQ = cfg["CH"], cfg["SEQ"]
    y_prompt = np.zeros((2, SEQ, D), np.float32)
    k_prompt = np.zeros((1, 2, 16 + SEQ, 8, 128), np.float32)
    v_prompt = np.zeros((1, 2, 16 + SEQ, 8, 128), np.float32)
    pool_prompt = np.zeros((1, 2, 15, 1024), np.float32)
    for c in range(8):
        b, j = c // 4, c % 4
        r = res[c]
        y_prompt[b, j * CH:(j + 1) * CH] = r["y"][0:CH]
        k_prompt[0, b, 16 + j * CH:16 + (j + 1) * CH] = r["ko"][0:CH].reshape(CH, 8, 128)
        v_prompt[0, b, 16 + j * CH:16 + (j + 1) * CH] = r["vo"][0:CH].reshape(CH, 8, 128)
        if j == 0:
            k_prompt[0, b, 0:16] = r["ko"][CH:CH + 16].reshape(16, 8, 128)
            v_prompt[0, b, 0:16] = r["vo"][CH:CH + 16].reshape(16, 8, 128)
        if j == 3:
            pool_prompt[0, b] = r["ptail"].reshape(128, 8, 15).transpose(2, 1, 0).reshape(15, 1024)
    r0 = res[0]
    y_sample = np.ascontiguousarray(r0["y"][CH + 16:CH + 48].reshape(8, 4, D))
    k_sample = np.ascontiguousarray(r0["ko"][CH + 16:CH + 48].reshape(1, 8, 4, 8, 128))
    v_sample = np.ascontiguousarray(r0["vo"][CH + 16:CH + 48].reshape(1, 8, 4, 8, 128))
    new = r0["psnew"].reshape(128, 8, 8, 4).transpose(2, 3, 1, 0).reshape(8, 4, 1024)
    old = r0["psold"].reshape(8, 11, 1024)
    pool_sample = np.ascontiguousarray(np.concatenate([old, new], 1)[None])
    return (y_prompt, y_sample, k_prompt, v_prompt, pool_prompt, k_sample, v_sample, pool_sample)


_CACHE = {}


def kernel(**inputs):
    cfg = finish_cfg(make_cfg())
    if "nc" not in _CACHE:
        _CACHE["nc"] = build(cfg)
    nc = _CACHE["nc"]
    maps = prep_inputs(cfg, inputs)
    res = run_bass_kernel_spmd(nc, maps, core_ids=list(range(8)))
    return assemble(cfg, res.results)
```

```python
import numpy as np
import concourse.bass as bass
import concourse.mybir as mybir
from concourse.bass_utils import run_bass_kernel_spmd

F32 = mybir.dt.float32
BF16 = mybir.dt.bfloat16
I32 = mybir.dt.int32
ALU = mybir.AluOpType
AF = mybir.ActivationFunctionType

D = 2048
KC = 16
NH = 8
EPS = 1e-6
NEG = -30000.0
SLOPES = [2.0 ** (-8.0 * (h + 1) / NH) for h in range(NH)]
LAM_INIT = 0.2
WINS = (2, 4, 8, 16)


class Buf:
    __slots__ = ("name", "w", "r")

    def __init__(self, name):
        self.name = name
        self.w = None
        self.r = []


class Prog:
    def __init__(self):
        self.recs = []
        self.nslot = {"sp": 8, "pool": 8}

    def barrier(self):
        self.recs.append(("sp", [], [], [], "bar"))

    def add(self, eng, fns, reads=(), writes=(), kind="c"):
        if not isinstance(fns, (list, tuple)):
            fns = [fns]
        self.recs.append((eng, list(fns), list(reads), list(writes), kind))

    def emit(self, nc, sems, block_engines):
        cnt = {e: 0 for e in ("pe", "act", "dve", "pool")}
        slot_use = {q: [0] * n for q, n in self.nslot.items()}
        slot_next = {q: 0 for q in self.nslot}
        ncc = 0
        waited = {}
        streams = {e: [] for e in ("pe", "act", "dve", "pool", "sp")}

        def need(E, tok):
            if tok is None:
                return
            key = (E, tok[0])
            if waited.get(key, 0) >= tok[1]:
                return
            waited[key] = tok[1]
            streams[E].append(("w", tok[0], tok[1]))

        cc_done = []
        for eng, fns, reads, writes, kind in self.recs:
            E = eng
            if kind == "bar":
                alltok = [(e, cnt[e]) for e in cnt if cnt[e]]
                for q, n in self.nslot.items():
                    for s in range(n):
                        if slot_use[q][s]:
                            alltok.append((f"{q}{s}", 16 * slot_use[q][s]))
                alltok += cc_done
                for E2 in streams:
                    for t in alltok:
                        if t[0] == E2:
                            continue
                        need(E2, t)
                continue
            toks = []
            for b in reads:
                toks.append(b.w)
            for b in writes:
                toks.append(b.w)
                toks.extend(b.r)
            if kind == "c":
                cnt[E] += 1
                tok = (E, cnt[E])
                inc = 1
            elif kind == "d":
                q = E
                s = slot_next[q]
                slot_next[q] = (s + 1) % self.nslot[q]
                prev = slot_use[q][s]
                if prev:
                    toks.append((f"{q}{s}", 16 * prev))
                slot_use[q][s] = prev + 1
                tok = (f"{q}{s}", 16 * (prev + 1))
                inc = 16
            else:
                tok = (f"cc{ncc}", 1)
                cc_done.append(tok)
                ncc += 1
                inc = 1
            for t in toks:
                if t is None:
                    continue
                if t[0] == E and E == "pe":
                    continue
                need(E, t)
            streams[E].append(("i", fns, tok[0], inc))
            for b in writes:
                b.w = tok
                b.r = []
            for b in reads:
                if b not in writes:
                    b.r.append(tok)
        final = []
        for q, n in self.nslot.items():
            for s in range(n):
                if slot_use[q][s]:
                    final.append((f"{q}{s}", 16 * slot_use[q][s]))
        for t in final:
            need("sp", t)
        for e in ("pe", "act", "dve", "pool"):
            if cnt[e]:
                need("sp", (e, cnt[e]))
        return streams


def run_streams(streams, sems, engs):
    for e, lst in streams.items():
        eng = engs[e]
        for it in lst:
            if it[0] == "w":
                eng.wait_ge(sems[it[1]], it[2])
            else:
                _, fns, semname, inc = it
                last = None
                for f in fns:
                    last = f(eng)
                last.then_inc(sems[semname], inc)


def make_cfg(seq=4096, dff=5632, page=128, npool=1280):
    ch = seq // 4
    return dict(SEQ=seq, CH=ch, DFF=dff, PAGE=page, NPOOL=npool, NT=ch + 48, NSUB=page // 16)


def token_tiles(cfg):
    ch = cfg["CH"]
    tl = [(i * 512, 512) for i in range(ch // 512)]
    tl.append((ch, 48))
    return tl


GROUPS4 = [[0, 1, 2, 3], [4, 5, 6, 7]]
PAIRS = [[0, 4], [1, 5], [2, 6], [3, 7]]


def build(cfg):
    from contextlib import ExitStack
    CH, DFF, PAGE, NT, NPOOL, NSUB = cfg["CH"], cfg["DFF"], cfg["PAGE"], cfg["NT"], cfg["NPOOL"], cfg["NSUB"]
    NF = DFF // 128
    TT = token_tiles(cfg)
    NTT = len(TT)
    NKT = CH // 128
    NQT = CH // 512
    nc = bass.Bass("TRN2", target_bir_lowering=False, num_devices=8)
    P = Prog()
    CO = cfg["OFF"]

    def din(name, shape, dt=F32):
        return nc.dram_tensor(name, list(shape), dt, kind="ExternalInput").ap()

    def dout(name, shape, dt=F32):
        return nc.dram_tensor(name, list(shape), dt, kind="ExternalOutput").ap()

    def dint(name, shape, dt):
        return nc.dram_tensor(name, list(shape), dt, kind="Internal").ap()

    xin = din("xin", [NT, D])
    wts = {}
    for nm, shp in (("w_gate1", [D, DFF]), ("w_up1", [D, DFF]), ("w_down1", [DFF, D]), ("w_in", [D, 4096]),
                    ("w_out", [D, D]), ("w_gate2", [D, DFF]), ("w_up2", [D, DFF]), ("w_down2", [DFF, D]),
                    ("w_pool", [1024, 256]), ("w_in_hd", [D, 384])):
        wts[nm] = din(nm, shp)
    cvec = din("cvec", [128, 72])
    gain_bc = din("gain_bc", [128, 128])
    lamv = din("lamv", [1, 256])
    cks = [din(f"ck{u}", [NPOOL, 2048]) for u in range(NSUB)]
    cvs = [din(f"cv{u}", [NPOOL, 2048]) for u in range(NSUB)]
    ptab = din("ptab", [128, 8], I32)
    spool = din("spool", [120, 1024])
    cf32 = din("cf32", [128, cfg["NCF"]])
    cbf = din("cbf", [128, cfg["NCB"]], BF16)

    y_o = dout("y", [NT, D])
    k_o = dout("ko", [NT, 1024])
    v_o = dout("vo", [NT, 1024])
    pt_o = dout("ptail", [128, 120])
    psn_o = dout("psnew", [128, 256])
    pso_o = dout("psold", [88, 1024])

    src_kT = [dint(f"src_kT{t}", [1024, 512], BF16) for t in range(NQT)]
    gat_kT = [dint(f"gat_kT{t}", [4096, 512], BF16) for t in range(NQT)]
    src_V = [dint(f"src_V{t}", [512, 1024], BF16) for t in range(NQT)]
    gat_V = [dint(f"gat_V{t}", [2048, 1024], BF16) for t in range(NQT)]
    src_ph = dint("src_ph", [128, 128], F32)
    gat_ph = dint("gat_ph", [512, 128], F32)
    scr_q = dint("scr_q", [1024, CH + 16], BF16)
    scr_mk = dint("scr_mk", [1024, 16], BF16)
    scr_mv = dint("scr_mv", [16, 1024], BF16)
    src_o = dint("src_o", [32, 128], F32)
    gat_o4 = dint("gat_o4", [128, 128], F32)
    gat_o8 = dint("gat_o8", [256, 128], F32)
    B = {n: Buf(n) for n in ("src_kT", "gat_kT", "src_V", "gat_V", "src_ph", "gat_ph", "scr_q", "scr_mk", "scr_mv",
                             "src_o", "gat_o4", "gat_o8", "out")}

    es = ExitStack()

    uniq = [0]

    def sb(name, shape, dt=F32, st=None):
        uniq[0] += 1
        return (st or es).enter_context(nc.sbuf_tensor(f"{name}_{uniq[0]}", list(shape), dt))

    class Rot:
        def __init__(self, name, shape, dt, n, st=None):
            self.t = [sb(f"{name}{i}", shape, dt, st) for i in range(n)]
            self.b = [Buf(f"{name}{i}") for i in range(n)]
            self.i = 0

        def get(self):
            k = self.i % len(self.t)
            self.i += 1
            return self.t[k], self.b[k]

    with es:
        sems = {}
        for nm in (["pe", "act", "dve", "pool"] + [f"sp{i}" for i in range(8)] + [f"pool{i}" for i in range(8)]
                   + [f"cc{i}" for i in range(8)]):
            sems[nm] = es.enter_context(nc.semaphore("s_" + nm))
        psum = [es.enter_context(nc.psum_tensor(f"ps{i}", [128, 512], F32)) for i in range(7)]
        psb = [Buf(f"ps{i}") for i in range(7)]
        psbf = es.enter_context(nc.psum_tensor("psbf", [128, 1024], BF16))
        b_psbf = Buf("psbf")
        pcount = [0]

        def next_ps():
            i = pcount[0] % 7
            pcount[0] += 1
            return psum[i], psb[i]

        SKIP = cfg.get("SKIP") or ""
        phase = [""]

        def dma(q, out, in_, reads, writes):
            if phase[0] == "A" and "d" in SKIP:
                return
            P.add(q, lambda e, o=out, i=in_: e.dma_start(out=o, in_=i), reads, writes, kind="d")

        def cc(groups, src, dst, reads, writes):
            P.add("pool", lambda e: e.collective_compute("AllGather", ALU.bypass, replica_groups=groups,
                                                          ins=[src], outs=[dst]), reads, writes, kind="cc")

        xT = sb("xT", [128, KC, NT])
        xTb = [[Buf(f"x{k}_{t}") for t in range(NTT)] for k in range(KC)]
        hT = sb("hT", [128, KC, NT], BF16)
        hTb = [[Buf(f"h{k}_{t}") for t in range(NTT)] for k in range(KC)]
        cv_sb = sb("cv_sb", [128, 72]); b_cv = Buf("cv")
        cf_sb = sb("cf_sb", [128, cfg["NCF"]]); b_cf = Buf("cf")
        cb_sb = sb("cb_sb", [128, cfg["NCB"]], BF16); b_cb = Buf("cb")
        gain_sb = sb("gain_sb", [128, 128]); b_gain = Buf("gain")
        pt_sb = sb("pt_sb", [128, 8], I32); b_pt = Buf("pt")
        lam_sb = sb("lam_sb", [128, 4]); b_lam = Buf("lam")
        sqr = Rot("sq", [128, 512], BF16, 3)
        rstd_r = Rot("rstd", [128, 512], F32, 2)

        def cf(name, w=1, o=0):
            return cf_sb[:, CO[name] + o:CO[name] + o + w]

        def cb(name, w, o=0):
            return cb_sb[:, CO[name] + o:CO[name] + o + w]

        ident_f = cf("ident", 128)
        ident_b = cb("identb", 128)
        ones_b = cb("onesb", 128)
        eps_col = cf("eps")

        dma("sp", cv_sb[:], cvec[:, :], [], [b_cv])
        dma("sp", cf_sb[:], cf32[:, :], [], [b_cf])
        dma("sp", cb_sb[:], cbf[:, :], [], [b_cb])
        dma("sp", gain_sb[:], gain_bc[:, :], [], [b_gain])
        dma("sp", pt_sb[:], ptab[:, :], [], [b_pt])

        evac_flip = [0]

        def evac(out, in_, reads, writes, scale=None, eng=None):
            evac_flip[0] ^= 1
            if eng is None:
                eng = "act" if evac_flip[0] else "dve"
            if eng == "act":
                if scale is None:
                    P.add("act", lambda e: e.activation(out=out, in_=in_, func=AF.Copy), reads, writes)
                else:
                    P.add("act", lambda e: e.activation(out=out, in_=in_, func=AF.Copy, scale=scale), reads, writes)
            else:
                if scale is None:
                    P.add("dve", lambda e: e.tensor_copy(out=out, in_=in_), reads, writes)
                else:
                    P.add("dve", lambda e: e.tensor_scalar(out=out, in0=in_, scalar1=scale, scalar2=None, op0=ALU.mult),
                          reads, writes)

        def tile_of(tok):
            for ti, (t0, n) in enumerate(TT):
                if t0 <= tok < t0 + n:
                    return ti
            raise ValueError

        with ExitStack() as st:
            xtok = Rot("xtok", [128, D], F32, 2, st)
            for tt in range((NT + 127) // 128):
                r0 = tt * 128
                nr = min(128, NT - r0)
                xt, xb = xtok.get()
                dma("sp", xt[0:nr, :], xin[r0:r0 + nr, :], [], [xb])
                ti = tile_of(r0)
                for k4 in range(KC // 4):
                    ps, pb = next_ps()
                    fns = []
                    for kk in range(4):
                        k = k4 * 4 + kk
                        fns.append(lambda e, k=k, kk=kk, ps=ps, xt=xt, nr=nr: e.transpose(
                            out=ps[:, kk * 128:kk * 128 + nr], in_=xt[0:nr, k * 128:(k + 1) * 128],
                            identity=ident_f[0:nr, 0:nr]))
                    P.add("pe", fns, [xb, b_cf], [pb])
                    evac(xT[:, k4 * 4:k4 * 4 + 4, r0:r0 + nr], ps[:].rearrange("p (a b) -> p a b", a=4)[:, :, 0:nr],
                         [pb], [xTb[k][ti] for k in range(k4 * 4, k4 * 4 + 4)])
            P.barrier()

        def rms_rstd(ti, tiles=None, xb=None):
            tiles = tiles or TT
            xb = xb or xTb
            t0, n = tiles[ti]
            ps, pb = next_ps()
            for k in range(KC):
                sq, sqb = sqr.get()
                P.add("act", lambda e, sq=sq, k=k: e.activation(out=sq[:, 0:n], in_=xT[:, k, t0:t0 + n], func=AF.Square),
                      [xb[k][ti]], [sqb])
                P.add("pe", lambda e, sq=sq, k=k, ps=ps: e.matmul(ps[:, 0:n], lhsT=ones_b, rhs=sq[:, 0:n],
                                                                 start=(k == 0), stop=(k == KC - 1)),
                      [sqb, b_cb] + ([pb] if k else []), [pb])
            rs, rsb = rstd_r.get()
            P.add("act", lambda e: e.activation(out=rs[:, 0:n], in_=ps[:, 0:n], func=AF.Sqrt, bias=eps_col, scale=1.0 / D),
                  [pb, b_cf], [rsb])
            P.add("dve", lambda e: e.reciprocal(out=rs[:, 0:n], in_=rs[:, 0:n]), [rsb], [rsb])
            return rs, rsb

        def rmsnorm_to_h(gbase, tiles=None, xb=None, hb=None):
            tiles = tiles or TT
            xb = xb or xTb
            hb = hb or hTb
            for ti, (t0, n) in enumerate(tiles):
                rs, rsb = rms_rstd(ti, tiles, xb)
                for k in range(KC):
                    P.add("dve", lambda e, k=k, rs=rs, t0=t0, n=n: e.scalar_tensor_tensor(
                        out=hT[:, k, t0:t0 + n], in0=xT[:, k, t0:t0 + n], scalar=cv_sb[:, gbase + k:gbase + k + 1],
                        in1=rs[:, 0:n], op0=ALU.mult, op1=ALU.mult),
                        [xb[k][ti], rsb, b_cv], [hb[k][ti]])

        n3 = (((NT + 2) // 3) + 1) // 2 * 2
        TTF = [(0, n3), (n3, n3), (2 * n3, NT - 2 * n3)]

        def wload(dst, w2d, n0, wdt, nk):
            return lambda e: e.dma_start(out=dst, in_=w2d.rearrange("(kc p) n -> p kc n", p=128)[:, 0:nk, n0:n0 + wdt])

        FP = 8

        def ffn(gbase, wg, wu, wd):
            with ExitStack() as st:
                TT = TTF
                NTT = len(TTF)
                xTb = [[Buf(f"fx{k}_{t}") for t in range(NTT)] for k in range(KC)]
                hTb = [[Buf(f"fh{k}_{t}") for t in range(NTT)] for k in range(KC)]
                wgu = Rot("wgu", [128, KC, 256], BF16, 4, st)
                hid = sb("hid", [128, FP, NT], BF16, st)
                hidb = [[Buf(f"hid{f}_{t}") for t in range(NTT)] for f in range(FP)]
                wdn = Rot("wdn", [128, FP, 512], BF16, 2, st)
                sgr = Rot("sg", [128, 512], F32, 2, st)
                rmsnorm_to_h(gbase, TT, xTb, hTb)
                f0 = 0
                while f0 < NF:
                    nfp = min(FP, NF - f0)
                    for fp in range(0, nfp, 2):
                        wgt, wgb = wgu.get()
                        P.add("pool", wload(wgt[:], wg, (f0 + fp) * 128, 256, KC), [], [wgb], kind="d")
                        wut, wub = wgu.get()
                        P.add("pool", wload(wut[:], wu, (f0 + fp) * 128, 256, KC), [], [wub], kind="d")
                        for ff in range(2):
                            fi = fp + ff
                            for ti, (t0, n) in enumerate(TT):
                                psA, pbA = next_ps()
                                P.add("pe", [lambda e, k=k, psA=psA, wgt=wgt, ff=ff, t0=t0, n=n: e.matmul(
                                    psA[:, 0:n], lhsT=wgt[:, k, ff * 128:(ff + 1) * 128], rhs=hT[:, k, t0:t0 + n],
                                    start=(k == 0), stop=(k == KC - 1)) for k in range(KC)],
                                    [wgb] + [hTb[k][ti] for k in range(KC)], [pbA])
                                psB, pbB = next_ps()
                                P.add("pe", [lambda e, k=k, psB=psB, wut=wut, ff=ff, t0=t0, n=n: e.matmul(
                                    psB[:, 0:n], lhsT=wut[:, k, ff * 128:(ff + 1) * 128], rhs=hT[:, k, t0:t0 + n],
                                    start=(k == 0), stop=(k == KC - 1)) for k in range(KC)],
                                    [wub] + [hTb[k][ti] for k in range(KC)], [pbB])
                                sg, sgb = sgr.get()
                                P.add("act", lambda e, sg=sg, psA=psA, n=n: e.activation(out=sg[:, 0:n], in_=psA[:, 0:n],
                                                                                        func=AF.Silu), [pbA], [sgb])
                                P.add("dve", lambda e, sg=sg, psB=psB, fi=fi, t0=t0, n=n: e.tensor_tensor(
                                    out=hid[:, fi, t0:t0 + n], in0=sg[:, 0:n], in1=psB[:, 0:n], op=ALU.mult),
                                    [sgb, pbB], [hidb[fi][ti]])
                    for o4 in range(4):
                        wdt_, wdb = wdn.get()
                        P.add("pool", lambda e, wdt_=wdt_, f0=f0, nfp=nfp, o4=o4: e.dma_start(
                            out=wdt_[:, 0:nfp, :],
                            in_=wd.rearrange("(fc p) n -> p fc n", p=128)[:, f0:f0 + nfp, o4 * 512:(o4 + 1) * 512]),
                            [], [wdb], kind="d")
                        for oo in range(4):
                            oc = o4 * 4 + oo
                            for ti, (t0, n) in enumerate(TT):
                                ps, pb = next_ps()
                                P.add("pe", [lambda e, f=f, ps=ps, wdt_=wdt_, oo=oo, t0=t0, n=n, nfp=nfp: e.matmul(
                                    ps[:, 0:n], lhsT=wdt_[:, f, oo * 128:(oo + 1) * 128], rhs=hid[:, f, t0:t0 + n],
                                    start=(f == 0), stop=(f == nfp - 1)) for f in range(nfp)],
                                    [wdb] + [hidb[f][ti] for f in range(nfp)], [pb])
                                P.add("dve", lambda e, ps=ps, oc=oc, t0=t0, n=n: e.scalar_tensor_tensor(
                                    out=xT[:, oc, t0:t0 + n], in0=ps[:, 0:n], scalar=0.5, in1=xT[:, oc, t0:t0 + n],
                                    op0=ALU.mult, op1=ALU.add), [pb, xTb[oc][ti]], [xTb[oc][ti]])
                    f0 += nfp
                P.barrier()

        def mixer(_):
            mst = ExitStack()
            Qblk = sb("Qblk_", [128, 8, 8], BF16); b_Q = Buf("Qblk")
            ksT = sb("ksT_", [128, 32], BF16); b_ksT = Buf("ksT")
            vsf = sb("vsf_", [128, 32], F32); b_vsf = Buf("vsf")
            Vnew = sb("Vnew_", [4, 8, 128], BF16); b_Vn = Buf("Vnew")
            gain08 = sb("gain08_", [128, 128], F32); b_g08 = Buf("g08")
            lv = sb("lv_", [1, 256], F32); b_lv = Buf("lv")
            lt = sb("lt_", [1, 8], F32); b_lt = Buf("lt")
            Cm = sb("Cm_", [8, 4], F32); b_Cm = Buf("Cm")
            pAll = sb("pAll", [128, 8, CH + 15], F32, mst); b_pA = [Buf(f"pA{k}") for k in range(8)]
            pM = sb("pM", [128, 8, 31], F32, mst); b_pM = Buf("pM")
            pS = sb("pS", [128, 8, 152], F32, mst); b_pS = Buf("pS")
            P.add("dve", lambda e: e.memset(pM[:], 0.0), [], [b_pM])
            P.add("dve", lambda e: e.memset(Qblk[:], 0.0), [], [b_Q])
            P.add("dve", lambda e: e.tensor_scalar(out=gain08[:], in0=gain_sb[:], scalar1=1.0 - LAM_INIT, scalar2=None,
                                                   op0=ALU.mult), [b_gain], [b_g08])
            dma("sp", lv[:], lamv[:, :], [], [b_lv])
            P.add("dve", lambda e: e.tensor_tensor(out=lv[0:1, 0:64], in0=lv[0:1, 0:64], in1=lv[0:1, 64:128], op=ALU.mult),
                  [b_lv], [b_lv])
            P.add("dve", lambda e: e.tensor_tensor(out=lv[0:1, 128:192], in0=lv[0:1, 128:192], in1=lv[0:1, 192:256],
                                                   op=ALU.mult), [b_lv], [b_lv])
            P.add("dve", lambda e: e.reduce_sum(out=lt[0:1, 0:1], in_=lv[0:1, 0:64], axis=mybir.AxisListType.X), [b_lv], [b_lt])
            P.add("dve", lambda e: e.reduce_sum(out=lt[0:1, 1:2], in_=lv[0:1, 128:192], axis=mybir.AxisListType.X),
                  [b_lv, b_lt], [b_lt])
            P.add("act", lambda e: e.activation(out=lt[0:1, 2:4], in_=lt[0:1, 0:2], func=AF.Exp), [b_lt], [b_lt])
            P.add("dve", lambda e: e.tensor_tensor(out=lt[0:1, 4:5], in0=lt[0:1, 2:3], in1=lt[0:1, 3:4], op=ALU.subtract),
                  [b_lt], [b_lt])
            P.add("dve", lambda e: e.tensor_scalar(out=lt[0:1, 5:6], in0=lt[0:1, 4:5], scalar1=LAM_INIT, scalar2=None,
                                                   op0=ALU.add), [b_lt], [b_lt])
            ps, pb = next_ps()
            P.add("pe", lambda e, ps=ps: e.matmul(ps[:, 0:1], lhsT=cf("onesf", 128)[0:1, :], rhs=lt[0:1, 5:6],
                                                  start=True, stop=True), [b_lt, b_cf], [pb])
            P.add("dve", lambda e, ps=ps: e.tensor_copy(out=lam_sb[:, 0:1], in_=ps[:, 0:1]), [pb], [b_lam])
            P.add("dve", lambda e, ps=ps: e.tensor_scalar(out=lam_sb[:, 1:2], in0=ps[:, 0:1], scalar1=-1.0, scalar2=None,
                                                          op0=ALU.mult), [pb, b_lam], [b_lam])
            P.add("dve", lambda e: e.scalar_tensor_tensor(out=Cm[:], in0=cf("cb1", 4)[0:8, :], scalar=lam_sb[0:8, 1:2],
                                                          in1=cf("cb0", 4)[0:8, :], op0=ALU.mult, op1=ALU.add),
                  [b_lam, b_cf], [b_Cm])

            if cfg.get("STOP") == "L":
                mst.close(); return
            rmsnorm_to_h(16)
            XT = NTT - 1
            with ExitStack() as st:
                win = Rot("win", [128, KC, 256], BF16, 3, st)
                kfr = Rot("kf", [128, 512], F32, 2, st)
                k16 = Rot("k16", [128, 512], BF16, 2, st)
                kst = Rot("kst", [128, 4, 128], F32, 2, st)
                vst = Rot("vst", [128, 4, 128], BF16, 2, st)
                for t_, b_ in zip(vst.t, vst.b):
                    P.add("dve", lambda e, t_=t_: e.memset(t_[:], 1.0), [], [b_])

                phase[0] = "A"

                def tok_major_out(ft, fb, n, t0, h, dst_o, want_bf):
                    if "t" in SKIP:
                        return
                    ps2, pb2 = next_ps()
                    nsub = (n + 127) // 128
                    rows = min(128, n)
                    P.add("pe", [lambda e, j=j, ps2=ps2: e.transpose(
                        out=ps2[0:min(128, n - j * 128), j * 128:(j + 1) * 128],
                        in_=ft[:, j * 128:j * 128 + min(128, n - j * 128)], identity=ident_f) for j in range(nsub)],
                        [fb, b_cf], [pb2])
                    kt_, ktb = kst.get()
                    src = ps2[0:rows, 0:nsub * 128].rearrange("p (j d) -> p j d", d=128)
                    evac(kt_[0:rows, 0:nsub, :], src, [pb2], [ktb], eng="act")
                    if n == 512:
                        dma("sp", dst_o[t0:t0 + 512, h * 128:(h + 1) * 128].rearrange("(j p) d -> p j d", p=128), kt_[:, :, :],
                            [ktb], [B["out"]])
                    else:
                        dma("sp", dst_o[t0:t0 + n, h * 128:(h + 1) * 128], kt_[0:n, 0, :], [ktb], [B["out"]])
                    if want_bf:
                        vt_, vtb = vst.get()
                        P.add("dve", lambda e, vt_=vt_, kt_=kt_: e.tensor_copy(out=vt_[0:rows, 0:nsub, 0:128], in_=kt_[0:rows, 0:nsub, :]),
                              [ktb], [vtb])
                        if n == 512:
                            dma("sp", src_V[t0 // 512][:, h * 128:(h + 1) * 128].rearrange("(j p) c -> p j c", p=128),
                                vt_[:, :, :], [vtb], [B["src_V"]])
                        else:
                            dma("sp", scr_mv[0:16, h * 128:(h + 1) * 128], vt_[0:16, 0, :], [vtb], [B["scr_mv"]])

                order = list(range(0, 8)) + list(range(16, 32)) + list(range(8, 16))
                for oi in range(0, 32, 2):
                    oc0 = order[oi]
                    wt, wb = win.get()
                    P.add("pool", wload(wt[:], wts["w_in"], oc0 * 128, 256, KC), [], [wb], kind="d")
                    for ff in range(2):
                        oc = oc0 + ff
                        for ti, (t0, n) in enumerate(TT):
                            ps, pb = next_ps()
                            P.add("pe", [lambda e, k=k, ps=ps, wt=wt, ff=ff, t0=t0, n=n: e.matmul(
                                ps[:, 0:n], lhsT=wt[:, k, ff * 128:(ff + 1) * 128], rhs=hT[:, k, t0:t0 + n],
                                start=(k == 0), stop=(k == KC - 1)) for k in range(KC)],
                                [wb] + [hTb[k][ti] for k in range(KC)], [pb])
                            if (oc < 8 and "P" in SKIP) or (8 <= oc < 16 and "Q" in SKIP) or (16 <= oc < 24 and "K" in SKIP) or (oc >= 24 and "V" in SKIP):
                                continue
                            if oc < 8:
                                if ti < XT:
                                    evac(pAll[:, oc, 15 + t0:15 + t0 + n], ps[:, 0:n], [pb], [b_pA[oc]])
                                else:
                                    evac(pM[:, oc, 15:31], ps[:, 0:16], [pb], [b_pM], eng="act")
                                    evac(pS[:, oc, :].rearrange("p (s c) -> p s c", c=19)[:, :, 15:19],
                                         ps[:, 16:48].rearrange("p (s q) -> p s q", q=4), [pb], [b_pS], eng="act")
                            elif oc < 16:
                                h = oc - 8
                                qt_, qb_ = k16.get()
                                evac(qt_[:, 0:n], ps[:, 0:n], [pb], [qb_], scale=0.125)
                                if ti < XT:
                                    dma("sp", scr_q[h * 128:(h + 1) * 128, t0:t0 + n], qt_[:, 0:n], [qb_], [B["scr_q"]])
                                else:
                                    dma("sp", scr_q[h * 128:(h + 1) * 128, CH:CH + 16], qt_[:, 0:16], [qb_], [B["scr_q"]])
                            elif oc < 24:
                                h = oc - 16
                                kf_, kfb = kfr.get()
                                evac(kf_[:, 0:n], ps[:, 0:n], [pb], [kfb], eng="act")
                                kb_, kbb = k16.get()
                                P.add("dve", lambda e, kb_=kb_, kf_=kf_, n=n: e.tensor_copy(out=kb_[:, 0:n], in_=kf_[:, 0:n]),
                                      [kfb], [kbb])
                                if ti < XT:
                                    dma("sp", src_kT[ti][h * 128:(h + 1) * 128, :], kb_[:, 0:n], [kbb], [B["src_kT"]])
                                else:
                                    dma("sp", scr_mk[h * 128:(h + 1) * 128, 0:16], kb_[:, 0:16], [kbb], [B["scr_mk"]])
                                tok_major_out(kf_, kfb, n, t0, h, k_o, False)
                            else:
                                h = oc - 24
                                vf_, vfb = kfr.get()
                                evac(vf_[:, 0:n], ps[:, 0:n], [pb], [vfb])
                                tok_major_out(vf_, vfb, n, t0, h, v_o, True)
                phase[0] = ""
                if cfg.get("STOP") == "A0":
                    P.barrier(); st.close(); mst.close(); return
                for part in range(3):
                    wt, wb = win.get()
                    P.add("pool", wload(wt[:, :, 0:128], wts["w_in_hd"], part * 128, 128, KC), [], [wb], kind="d")
                    ps, pb = next_ps()
                    P.add("pe", [lambda e, k=k, ps=ps, wt=wt: e.matmul(
                        ps[:, 0:32], lhsT=wt[:, k, 0:128], rhs=hT[:, k, CH + 16:CH + 48],
                        start=(k == 0), stop=(k == KC - 1)) for k in range(KC)],
                        [wb] + [hTb[k][XT] for k in range(KC)], [pb])
                    if part == 0:
                        for m_ in range(2):
                            P.add("act", lambda e, ps=ps, m_=m_: e.activation(
                                out=Qblk[64 * m_:64 * m_ + 64, :, 4 * m_:4 * m_ + 4],
                                in_=ps[64 * m_:64 * m_ + 64, 0:32].rearrange("p (s q) -> p s q", q=4),
                                func=AF.Copy, scale=0.125), [pb, b_Q], [b_Q])
                    elif part == 1:
                        evac(ksT[:, :], ps[:, 0:32], [pb], [b_ksT])
                    else:
                        evac(vsf[:, :], ps[:, 0:32], [pb], [b_vsf])
                        for half in range(2):
                            ps2, pb2 = next_ps()
                            P.add("pe", [lambda e, s4=s4, ps2=ps2, half=half: e.transpose(
                                out=ps2[0:4, s4 * 128:(s4 + 1) * 128], in_=vsf[:, (half * 4 + s4) * 4:(half * 4 + s4) * 4 + 4],
                                identity=ident_f) for s4 in range(4)], [b_vsf, b_cf], [pb2])
                            evac(Vnew[0:4, half * 4:half * 4 + 4, :], ps2[0:4, :].rearrange("p (s d) -> p s d", d=128),
                                 [pb2, b_Vn], [b_Vn])
                if cfg.get("STOP") == "A1":
                    P.barrier(); st.close(); mst.close(); return
                phst = kfr.t[0]; b_phst = kfr.b[0]
                P.add("dve", lambda e: e.memset(phst[:, 0:128], 0.0), [b_phst], [b_phst])
                P.add("dve", lambda e: e.tensor_copy(out=phst[:, 0:120].rearrange("p (k c) -> p k c", c=15), in_=pAll[:, :, CH:CH + 15]),
                      [b_phst] + b_pA, [b_phst])
                dma("sp", src_ph[:, :], phst[:, 0:128], [b_phst], [B["src_ph"]])
                dma("sp", pt_o.rearrange("p (k c) -> p k c", c=15), pAll[:, :, CH:CH + 15], b_pA, [B["out"]])
                for t in range(NQT):
                    cc(GROUPS4, src_kT[t][:, :], gat_kT[t][:, :], [B["src_kT"]], [B["gat_kT"]])
                    cc(GROUPS4, src_V[t][:, :], gat_V[t][:, :], [B["src_V"]], [B["gat_V"]])
                cc(GROUPS4, src_ph[:, :], gat_ph[:, :], [B["src_ph"]], [B["gat_ph"]])
                P.barrier()

            if cfg.get("STOP") == "A":
                mst.close(); return
            with ExitStack() as st:
                gph = sb("gph", [128, 4, 128], F32, st); b_gph = Buf("gph")
                spt = sb("spt", [120, 1024], F32, st); b_spt = Buf("spt")
                W1 = sb("W1", [128, 8, 271], F32, st); b_W1 = Buf("W1")
                W2 = sb("W2", [128, 8, 271], F32, st); b_W2 = Buf("W2")
                feat = sb("feat", [128, 8, 256], BF16, st); b_feat = Buf("feat")
                featS = sb("featS", [128, 8, 32], BF16, st); b_featS = Buf("featS")
                psc = sb("psc", [128, 8, 32], F32, st); b_psc = Buf("psc")
                wp_sb = sb("wp_sb", [128, 8, 256], BF16, st); b_wp = Buf("wp")
                P.add("pool", wload(wp_sb[:], wts["w_pool"], 0, 256, 8), [], [b_wp], kind="d")
                dma("sp", gph[:], gat_ph.rearrange("(r p) c -> p r c", p=128), [B["gat_ph"]], [b_gph])
                dma("sp", spt[:], spool[:, :], [], [b_spt])
                dma("sp", pso_o.rearrange("(s r) c -> s r c", r=11), spool.rearrange("(s r) c -> s r c", r=15)[:, 4:15, :],
                    [], [B["out"]])
                halo = pAll[:, :, 0:15]
                P.add("dve", lambda e: e.tensor_scalar(out=halo, in0=pM[:, :, 16:31], scalar1=cf("sel", 1, 4), scalar2=None,
                                                       op0=ALU.mult), [b_pM, b_cf] + b_pA, b_pA)
                for r in range(4):
                    P.add("dve", lambda e, r=r: e.scalar_tensor_tensor(
                        out=halo, in0=gph[:, r, 0:120].rearrange("p (k c) -> p k c", c=15), scalar=cf("sel", 1, r), in1=halo,
                        op0=ALU.mult, op1=ALU.add), [b_gph, b_cf] + b_pA, b_pA)
                for k4 in range(2):
                    ps, pb = next_ps()
                    P.add("pe", [lambda e, kk=kk, k4=k4, ps=ps: e.transpose(
                        out=ps[:, kk * 128:kk * 128 + 120], in_=spt[0:120, (k4 * 4 + kk) * 128:(k4 * 4 + kk + 1) * 128],
                        identity=ident_f[0:120, 0:120]) for kk in range(4)], [b_spt, b_cf], [pb])
                    for kk in range(4):
                        evac(pS[:, k4 * 4 + kk, :].rearrange("p (s c) -> p s c", c=19)[:, :, 0:15],
                             ps[:, kk * 128:kk * 128 + 120].rearrange("p (s c) -> p s c", c=15), [pb, b_pS], [b_pS], eng="act")
                for s in range(8):
                    P.add("dve", lambda e, s=s: e.tensor_copy(out=psc[:, :, s * 4:s * 4 + 4], in_=pS[:, :, s * 19 + 15:s * 19 + 19]),
                          [b_pS, b_psc], [b_psc])
                dma("sp", psn_o.rearrange("p (k t) -> p k t", k=8), psc[:, :, :], [b_psc], [B["out"]])

                def windows(X, L, xbufs):
                    TTa = mybir.AluOpType.add
                    P.add("dve", lambda e: e.tensor_tensor(out=W1[:, 0:8, 1:L], in0=X[:, 0:8, 1:L], in1=X[:, 0:8, 0:L - 1], op=TTa),
                          xbufs + [b_W1], [b_W1])
                    P.add("dve", lambda e: e.tensor_tensor(out=W2[:, 2:8, 3:L], in0=W1[:, 2:8, 3:L], in1=W1[:, 2:8, 1:L - 2], op=TTa),
                          [b_W1, b_W2], [b_W2])
                    P.add("dve", lambda e: e.tensor_tensor(out=W1[:, 4:8, 7:L], in0=W2[:, 4:8, 7:L], in1=W2[:, 4:8, 3:L - 4], op=TTa),
                          [b_W2, b_W1], [b_W1])
                    P.add("dve", lambda e: e.tensor_tensor(out=W2[:, 6:8, 15:L], in0=W1[:, 6:8, 15:L], in1=W1[:, 6:8, 7:L - 8], op=TTa),
                          [b_W1, b_W2], [b_W2])
                    return [W1, W2, W1, W2]

                def pool_mm(ft, fbuf, n, c0, ti):
                    for g in range(4):
                        for dc in range(2):
                            ps, pb = next_ps()
                            P.add("pe", [lambda e, cc_=cc_, ps=ps, g=g, dc=dc: e.matmul(
                                ps[:, 0:n], lhsT=wp_sb[:, 2 * g + cc_, dc * 128:(dc + 1) * 128], rhs=ft[:, 2 * g + cc_, 0:n],
                                start=(cc_ == 0), stop=(cc_ == 1)) for cc_ in range(2)], [b_wp, fbuf], [pb])
                            oc = 2 * g + dc
                            evac(hT[:, oc, c0:c0 + n], ps[:, 0:n], [pb, b_cv, hTb[oc][ti]], [hTb[oc][ti]],
                                 scale=cv_sb[:, 64 + oc:65 + oc])

                for t0 in range(0, CH, 256):
                    ti, n = t0 // 512, 256
                    X = pAll[:, :, t0:t0 + n + 15]
                    res = windows(X, n + 15, list(b_pA))
                    for g in range(4):
                        P.add("dve", lambda e, g=g, R=res[g], X=X, n=n: e.scalar_tensor_tensor(
                            out=feat[:, 2 * g:2 * g + 2, 0:n], in0=R[:, 2 * g:2 * g + 2, 15:15 + n], scalar=1.0 / WINS[g],
                            in1=X[:, 2 * g:2 * g + 2, 15:15 + n], op0=ALU.mult, op1=ALU.subtract),
                            [b_W1, b_W2, b_feat] + b_pA, [b_feat])
                    pool_mm(feat, b_feat, n, t0, ti)
                res = windows(pM, 31, [b_pM])
                for g in range(4):
                    P.add("dve", lambda e, g=g, R=res[g]: e.tensor_tensor(
                        out=R[:, 2 * g:2 * g + 2, 15:31], in0=R[:, 2 * g:2 * g + 2, 15:31],
                        in1=cf("invc", 128).rearrange("p (k t) -> p k t", t=16)[:, 2 * g:2 * g + 2, :], op=ALU.mult),
                        [b_W1, b_W2, b_cf], [b_W1, b_W2])
                    P.add("dve", lambda e, g=g, R=res[g]: e.tensor_tensor(
                        out=feat[:, 2 * g:2 * g + 2, 0:16], in0=R[:, 2 * g:2 * g + 2, 15:31], in1=pM[:, 2 * g:2 * g + 2, 15:31],
                        op=ALU.subtract), [b_W1, b_W2, b_pM, b_feat], [b_feat])
                pool_mm(feat, b_feat, 16, CH, XT)
                res = windows(pS, 152, [b_pS])
                for g in range(4):
                    P.add("dve", lambda e, g=g, R=res[g]: e.scalar_tensor_tensor(
                        out=R[:, 2 * g:2 * g + 2, 15:152], in0=R[:, 2 * g:2 * g + 2, 15:152], scalar=1.0 / WINS[g],
                        in1=pS[:, 2 * g:2 * g + 2, 15:152], op0=ALU.mult, op1=ALU.subtract),
                        [b_W1, b_W2, b_pS], [b_W1, b_W2])
                    for s in range(8):
                        P.add("dve", lambda e, g=g, R=res[g], s=s: e.tensor_copy(
                            out=featS[:, 2 * g:2 * g + 2, s * 4:s * 4 + 4], in_=R[:, 2 * g:2 * g + 2, s * 19 + 15:s * 19 + 19]),
                            [b_W1, b_W2, b_featS], [b_featS])
                pool_mm(featS, b_featS, 32, CH + 16, XT)
                P.barrier()
            mst.close()

            if cfg.get("STOP") == "B1":
                return
            with ExitStack() as st:
                NVT = 5 * NKT + 1
                kTg = sb("kTg", [128, 4, CH], BF16, st)
                kTo = sb("kTo", [128, CH], BF16, st)
                kTm = sb("kTm", [128, 16], BF16, st)
                Vh = sb("Vh", [128, NVT, 129], BF16, st)
                qh = sb("qh", [128, CH + 16], BF16, st)
                qbt = sb("qbt", [128, CH], F32, st)
                tmpr = Rot("tmp", [128, 512], F32, 3, st)
                Ar = Rot("A", [128, 512], BF16, 4, st)
                on1 = sb("on1", [128, 4, 128], F32, st); b_on1 = Buf("on1")
                fin = sb("fin", [128, 4, 128], F32, st); b_fin = Buf("fin")
                sqt = sb("sqt", [128, 128], F32, st); b_sqt = Buf("sqt")
                rc = sb("rc", [128, 16], F32, st); b_rc = Buf("rc")
                resb = sb("resb", [128, 4, 128], BF16, st); b_resb = Buf("resb")
                b_kTg, b_kTo, b_kTm, b_Vg, b_Vo, b_Vm, b_qh, b_qbt = (Buf(x) for x in ("kTg", "kTo", "kTm", "Vg", "Vo", "Vm", "qh", "qbt"))
                P.add("dve", lambda e: e.memset(Vh[:, :, 128:129], 1.0), [], [b_Vg, b_Vo, b_Vm])
                acc_ps = [psum[4], psum[5]]
                acc_b = [psb[4], psb[5]]
                c5 = [0]

                def next_ps5():
                    i = c5[0] % 4
                    c5[0] += 1
                    return psum[i], psb[i]

                def attend(h, q0, nq, subs, keytiles, outc0, ti_out):
                    for m_ in range(2):
                        p0 = 64 * m_
                        nkts = len(keytiles)
                        pend = []
                        LA = 3
                        def emit_av(item):
                            A_, Ab, V_ap, vbuf, nk, ki = item
                            fns = []
                            if ki == 0:
                                rows0 = subs[0][1]
                                nb = (len(subs) + 2) // 3
                                for bi in range(nb):
                                    ncol = 129 * min(3, len(subs) - 3 * bi)
                                    fns.append(lambda e, bi=bi, ncol=ncol, rows0=rows0: e.matmul(
                                        acc_ps[bi][0:rows0, 0:ncol], lhsT=cb("zerob", 128)[:, 0:rows0], rhs=cb("maskd", 512)[:, 0:ncol],
                                        start=True, stop=False))
                            for si, (so, rows) in enumerate(subs):
                                accp = acc_ps[si // 3]
                                c0 = (si % 3) * 129
                                last_in_bank = (si % 3 == 2) or (si == len(subs) - 1)
                                fns.append(lambda e, A_=A_, so=so, rows=rows, accp=accp, c0=c0, V_ap=V_ap, nk=nk, ki=ki, lib=last_in_bank: e.matmul(
                                    accp[0:rows, c0:c0 + 129], lhsT=A_[0:nk, so:so + rows], rhs=V_ap[0:nk, :],
                                    start=False, stop=(ki == nkts - 1 and lib)))
                            P.add("pe", fns, [Ab, vbuf, b_cb] + acc_b, acc_b)

                        for ki, (kT_ap, kbuf, V_ap, vbuf, bias_ap, nk, mask_ap) in enumerate(keytiles):
                            ps, pb = next_ps5()
                            fns = [lambda e, ps=ps, kT_ap=kT_ap, nk=nk, mask_ap=mask_ap, p0=p0: e.matmul(
                                ps[0:nk, 0:nq], lhsT=kT_ap[p0:p0 + 64, 0:nk], rhs=qh[p0:p0 + 64, q0:q0 + nq],
                                start=True, stop=(mask_ap is None))]
                            if mask_ap is not None:
                                fns.append(lambda e, ps=ps, nk=nk, mask_ap=mask_ap: e.matmul(
                                    ps[0:nk, 0:nq], lhsT=ident_b[0:nk, 0:nk], rhs=mask_ap, start=False, stop=True))
                            P.add("pe", fns, [kbuf, b_qh, b_cb], [pb])
                            tm, tmb = tmpr.get()
                            P.add("dve", lambda e, tm=tm, ps=ps, nk=nk: e.tensor_tensor(
                                out=tm[0:nk, 0:nq], in0=ps[0:nk, 0:nq], in1=qbt[0:nk, 0:nq] if q0 >= CH else qbt[0:nk, q0:q0 + nq],
                                op=ALU.add), [pb, b_qbt], [tmb])
                            A_, Ab = Ar.get()
                            P.add("act", lambda e, A_=A_, tm=tm, nk=nk, bias_ap=bias_ap: e.activation(
                                out=A_[0:nk, 0:nq], in_=tm[0:nk, 0:nq], func=AF.Exp, bias=bias_ap, scale=1.0), [tmb, b_cf], [Ab])
                            pend.append((A_, Ab, V_ap, vbuf, nk, ki))
                            if len(pend) > LA:
                                emit_av(pend.pop(0))
                            if nq == 512 and ki % 10 == 9:
                                sample_step()
                        while pend:
                            emit_av(pend.pop(0))
                        for si, (so, rows) in enumerate(subs):
                            accp = acc_ps[si // 3]
                            c0 = (si % 3) * 129
                            P.add("dve", lambda e, accp=accp, c0=c0, rows=rows, si=si: e.reciprocal(
                                out=rc[0:rows, si:si + 1], in_=accp[0:rows, c0 + 128:c0 + 129]), acc_b + [b_rc], [b_rc])
                            if m_ == 0:
                                P.add("dve", lambda e, accp=accp, c0=c0, rows=rows, si=si: e.tensor_scalar(
                                    out=on1[0:rows, si, :], in0=accp[0:rows, c0:c0 + 128], scalar1=rc[0:rows, si:si + 1],
                                    scalar2=None, op0=ALU.mult), acc_b + [b_rc, b_on1], [b_on1])
                            else:
                                P.add("dve", lambda e, accp=accp, c0=c0, rows=rows, si=si: e.tensor_scalar(
                                    out=fin[0:rows, si, :], in0=accp[0:rows, c0:c0 + 128], scalar1=rc[0:rows, si:si + 1],
                                    scalar2=None, op0=ALU.mult), acc_b + [b_rc, b_fin], [b_fin])
                                P.add("dve", lambda e, rows=rows, si=si: e.scalar_tensor_tensor(
                                    out=fin[0:rows, si, :], in0=fin[0:rows, si, :], scalar=lam_sb[0:rows, 1:2], in1=on1[0:rows, si, :],
                                    op0=ALU.mult, op1=ALU.add), [b_fin, b_on1, b_lam], [b_fin])
                    for si, (so, rows) in enumerate(subs):
                        P.add("dve", lambda e, rows=rows, si=si: e.tensor_tensor(
                            out=sqt[0:rows, :], in0=fin[0:rows, si, :], in1=fin[0:rows, si, :], op=ALU.mult), [b_fin, b_sqt], [b_sqt])
                        P.add("dve", lambda e, rows=rows, si=si: e.reduce_sum(
                            out=rc[0:rows, 8 + si:9 + si], in_=sqt[0:rows, :], axis=mybir.AxisListType.X), [b_sqt, b_rc], [b_rc])
                        P.add("act", lambda e, rows=rows, si=si: e.activation(
                            out=rc[0:rows, 8 + si:9 + si], in_=rc[0:rows, 8 + si:9 + si], func=AF.Sqrt, bias=eps_col[0:rows, :],
                            scale=1.0 / 128), [b_rc, b_cf], [b_rc])
                        P.add("dve", lambda e, rows=rows, si=si: e.reciprocal(
                            out=rc[0:rows, 8 + si:9 + si], in_=rc[0:rows, 8 + si:9 + si]), [b_rc], [b_rc])
                        P.add("dve", lambda e, rows=rows, si=si: e.scalar_tensor_tensor(
                            out=resb[0:rows, si, :], in0=fin[0:rows, si, :], scalar=rc[0:rows, 8 + si:9 + si], in1=gain08[0:rows, :],
                            op0=ALU.mult, op1=ALU.mult), [b_fin, b_rc, b_g08, b_resb], [b_resb])
                        P.add("pe", lambda e, rows=rows, si=si: e.transpose(
                            out=psbf[:, si * 128:si * 128 + rows], in_=resb[0:rows, si, :], identity=ident_b[0:rows, 0:rows]),
                            [b_resb, b_cb, b_psbf], [b_psbf])
                        evac(hT[:, 8 + h, outc0 + so:outc0 + so + rows], psbf[:, si * 128:si * 128 + rows],
                             [b_psbf, hTb[8 + h][ti_out]], [hTb[8 + h][ti_out]])

                def sample_gen():
                    Ku = Rot("Ku", [128, 2048], BF16, 2, st)
                    Vu = Rot("Vu", [128, 2048], BF16, 2, st)
                    kTu = Rot("kTu", [128, 2048], BF16, 2, st)
                    Eu = Rot("Eu", [128, 128], F32, 2, st)
                    A_all = sb("A_all", [128, PAGE * 8], BF16, st); b_Aall = Buf("Aall")
                    rsum = sb("rsum", [128, 8], F32, st); b_rsum = Buf("rsum")
                    An32 = sb("An32", [4, 8], F32, st); b_An32 = Buf("An32")
                    An = sb("An", [4, 8], BF16, st); b_An = Buf("An")
                    osb = sb("osb", [8, 128], F32, st); b_osb = Buf("osb")
                    orc = sb("orc", [8, 4], F32, st); b_orc = Buf("orc")
                    f4 = sb("f4", [4, 128], F32, st); b_f4 = Buf("f4")
                    sq4 = sb("sq4", [4, 128], F32, st); b_sq4 = Buf("sq4")
                    r4 = sb("r4", [4, 2], F32, st); b_r4 = Buf("r4")
                    res_all = sb("res_all", [4, 8, 128], F32, st); b_res = Buf("res_all")
                    osamp = sb("osamp", [32, 8, 128], F32, st); b_osamp = Buf("osamp")
                    o_ps, o_pb = psum[6], psb[6]
                    for s in range(8):
                        P.add("pe", lambda e: e.matmul(o_ps[0:8, 0:129], lhsT=cb("zerob", 128)[:, 0:8], rhs=cb("maskd", 512)[:, 0:129],
                                                       start=True, stop=False), [b_cb, o_pb], [o_pb])
                        for u in range(NSUB):
                            ku, kub = Ku.get()
                            P.add("pool", lambda e, ku=ku, u=u, s=s: e.indirect_dma_start(
                                out=ku[:], out_offset=None, in_=cks[u][:, :],
                                in_offset=bass.IndirectOffsetOnAxis(ap=pt_sb[:, s:s + 1], axis=0)), [b_pt], [kub], kind="d")
                            vu, vub = Vu.get()
                            P.add("pool", lambda e, vu=vu, u=u, s=s: e.indirect_dma_start(
                                out=vu[:], out_offset=None, in_=cvs[u][:, :],
                                in_offset=bass.IndirectOffsetOnAxis(ap=pt_sb[:, s:s + 1], axis=0)), [b_pt], [vub], kind="d")
                            ktu, ktub = kTu.get()
                            for half in range(2):
                                P.add("pe", [lambda e, tk=tk, ku=ku, half=half: e.transpose(
                                    out=psbf[:, tk * 128:(tk + 1) * 128], in_=ku[:, (half * 8 + tk) * 128:(half * 8 + tk + 1) * 128],
                                    identity=ident_b) for tk in range(8)], [kub, b_cb, b_psbf], [b_psbf])
                                evac(ktu[:, half * 1024:(half + 1) * 1024], psbf[:, :], [b_psbf, ktub], [ktub])
                            yield
                            ps, pb = next_ps5()
                            P.add("pe", [lambda e, tk=tk, ps=ps, ktu=ktu, s=s: e.matmul(
                                ps[:, tk * 8:(tk + 1) * 8], lhsT=ktu[:, tk * 128:(tk + 1) * 128], rhs=Qblk[:, s, :],
                                start=True, stop=True) for tk in range(16)], [ktub, b_Q], [pb])
                            eu, eub = Eu.get()
                            P.add("act", lambda e, eu=eu, ps=ps: e.activation(out=eu[:], in_=ps[:, 0:128], func=AF.Exp,
                                                                             bias=cf("sbias"), scale=1.0), [pb, b_cf], [eub])
                            P.add("dve", lambda e, eu=eu, u=u: e.tensor_tensor(
                                out=A_all[:, u * 128:(u + 1) * 128], in0=eu[:], in1=cf("wtok", 128, u * 128), op=ALU.mult),
                                [eub, b_cf, b_Aall], [b_Aall])
                            yield
                            P.add("pe", [lambda e, tk=tk, vu=vu, u=u: e.matmul(
                                o_ps[0:8, 0:128], lhsT=A_all[:, (u * 16 + tk) * 8:(u * 16 + tk) * 8 + 8], rhs=vu[:, tk * 128:(tk + 1) * 128],
                                start=False, stop=False) for tk in range(16)] + [lambda e, tk=tk, u=u: e.matmul(
                                o_ps[0:8, 128:129], lhsT=A_all[:, (u * 16 + tk) * 8:(u * 16 + tk) * 8 + 8], rhs=ones_b[:, 0:1],
                                start=False, stop=False) for tk in range(16)], [b_Aall, vub, o_pb, b_cb], [o_pb])
                            yield
                        ps, pb = next_ps5()
                        P.add("pe", lambda e, ps=ps, s=s: e.matmul(ps[0:4, 0:8], lhsT=ksT[:, 4 * s:4 * s + 4], rhs=Qblk[:, s, :],
                                                                   start=True, stop=True), [b_ksT, b_Q], [pb])
                        P.add("act", lambda e, ps=ps: e.activation(out=An32[:], in_=ps[0:4, 0:8], func=AF.Exp,
                                                                   bias=cf("nbias")[0:4, :], scale=1.0), [pb, b_cf, b_An32], [b_An32])
                        P.add("dve", lambda e: e.tensor_tensor(out=An32[:], in0=An32[:], in1=cf("nmask", 8)[0:4, :], op=ALU.mult),
                              [b_An32, b_cf], [b_An32])
                        P.add("dve", lambda e: e.tensor_copy(out=An[:], in_=An32[:]), [b_An32, b_An], [b_An])
                        P.add("pe", [lambda e, s=s: e.matmul(o_ps[0:8, 0:128], lhsT=An[0:4, :], rhs=Vnew[0:4, s, :], start=False, stop=False),
                                     lambda e: e.matmul(o_ps[0:8, 128:129], lhsT=An[0:4, :], rhs=ones_b[0:4, 0:1], start=False, stop=True)],
                              [b_An, b_Vn, o_pb, b_cb], [o_pb])
                        P.add("dve", lambda e: e.reciprocal(out=orc[:, 0:1], in_=o_ps[0:8, 128:129]), [o_pb, b_orc], [b_orc])
                        P.add("dve", lambda e: e.tensor_scalar(out=osb[:], in0=o_ps[0:8, 0:128], scalar1=orc[:, 0:1], scalar2=None,
                                                               op0=ALU.mult), [o_pb, b_orc, b_osb], [b_osb])
                        ps, pb = next_ps5()
                        P.add("pe", lambda e, ps=ps: e.matmul(ps[0:4, 0:128], lhsT=Cm[0:8, :], rhs=osb[0:8, :], start=True, stop=True),
                              [b_Cm, b_osb], [pb])
                        P.add("dve", lambda e, ps=ps: e.tensor_copy(out=f4[:], in_=ps[0:4, 0:128]), [pb, b_f4], [b_f4])
                        P.add("dve", lambda e: e.tensor_tensor(out=sq4[:], in0=f4[:], in1=f4[:], op=ALU.mult), [b_f4, b_sq4], [b_sq4])
                        P.add("dve", lambda e: e.reduce_sum(out=r4[:, 0:1], in_=sq4[:], axis=mybir.AxisListType.X), [b_sq4, b_r4], [b_r4])
                        P.add("act", lambda e: e.activation(out=r4[:, 0:1], in_=r4[:, 0:1], func=AF.Sqrt, bias=eps_col[0:4, :],
                                                            scale=1.0 / 128), [b_r4, b_cf], [b_r4])
                        P.add("dve", lambda e: e.reciprocal(out=r4[:, 0:1], in_=r4[:, 0:1]), [b_r4], [b_r4])
                        P.add("dve", lambda e, s=s: e.scalar_tensor_tensor(
                            out=res_all[0:4, s, :], in0=f4[:], scalar=r4[:, 0:1], in1=gain08[0:4, :], op0=ALU.mult, op1=ALU.mult),
                            [b_f4, b_r4, b_g08, b_res], [b_res])
                    dma("sp", src_o.rearrange("(s q) d -> q s d", q=4), res_all[0:4, :, :], [b_res], [B["src_o"]])
                    cc(GROUPS4, src_o[:, :], gat_o4[:, :], [B["src_o"]], [B["gat_o4"]])
                    cc(PAIRS, gat_o4[:, :], gat_o8[:, :], [B["gat_o4"]], [B["gat_o8"]])
                    dma("sp", osamp[:], gat_o8.rearrange("(h t) d -> t h d", t=32), [B["gat_o8"]], [b_osamp])
                    ps, pb = next_ps5()
                    P.add("pe", [lambda e, h=h, ps=ps: e.transpose(out=ps[:, h * 32:(h + 1) * 32], in_=osamp[0:32, h, :],
                                                                  identity=ident_f[0:32, 0:32]) for h in range(8)],
                          [b_osamp, b_cf], [pb])
                    evac(hT[:, 8:16, CH + 16:CH + 48], ps[:, 0:256].rearrange("p (h t) -> p h t", t=32),
                         [pb] + [hTb[8 + h][XT] for h in range(8)], [hTb[8 + h][XT] for h in range(8)])
                    P.barrier()


                sgen = sample_gen()
                sdone = [False]

                def sample_step():
                    if not sdone[0]:
                        try:
                            next(sgen)
                        except StopIteration:
                            sdone[0] = True

                for h in range(NH):
                    for t in range(NQT):
                        dma("sp", kTg[:, :, t * 512:(t + 1) * 512], gat_kT[t].rearrange("(r hd) t -> hd r t", r=4)[h * 128:(h + 1) * 128, :, :], [B["gat_kT"]], [b_kTg])
                        dma("sp", kTo[:, t * 512:(t + 1) * 512], src_kT[t][h * 128:(h + 1) * 128, :], [B["src_kT"]], [b_kTo])
                    dma("sp", kTm[:], scr_mk[h * 128:(h + 1) * 128, :], [B["scr_mk"]], [b_kTm])
                    for t in range(NQT):
                        for r in range(4):
                            dma("sp", Vh[:, r * NKT + t * 4:r * NKT + t * 4 + 4, 0:128],
                                gat_V[t][r * 512:(r + 1) * 512, h * 128:(h + 1) * 128].rearrange("(k p) c -> p k c", p=128),
                                [B["gat_V"]], [b_Vg])
                        dma("sp", Vh[:, 4 * NKT + t * 4:4 * NKT + t * 4 + 4, 0:128],
                            src_V[t][:, h * 128:(h + 1) * 128].rearrange("(k p) c -> p k c", p=128), [B["src_V"]], [b_Vo])
                    dma("sp", Vh[0:16, 5 * NKT, 0:128], scr_mv[0:16, h * 128:(h + 1) * 128], [B["scr_mv"]], [b_Vm])
                    dma("sp", qh[:], scr_q[h * 128:(h + 1) * 128, :], [B["scr_q"]], [b_qh])
                    P.add("dve", lambda e, h=h: e.tensor_scalar(out=qbt[:], in0=cf("qrow", CH), scalar1=-SLOPES[h], scalar2=None,
                                                                op0=ALU.mult), [b_cf, b_qbt], [b_qbt])
                    meta_kt = (kTm, b_kTm, Vh[:, 5 * NKT, :], b_Vm, cf("biasm", 1, h)[0:16, :], 16, None)
                    for qt in range(NQT):
                        kts = []
                        for r in range(3):
                            for kt in range(NKT):
                                kts.append((kTg[:, r, kt * 128:(kt + 1) * 128], b_kTg, Vh[:, r * NKT + kt, :], b_Vg,
                                            cf("biasg", 1, (h * 4 + r) * NKT + kt), 128, None))
                        kts.append(meta_kt)
                        for kt in range(4 * qt + 4):
                            m = kt - 4 * qt
                            kts.append((kTo[:, kt * 128:(kt + 1) * 128], b_kTo, Vh[:, 4 * NKT + kt, :], b_Vo,
                                        cf("biaso", 1, h * NKT + kt), 128, cb("maskd", 512, m * 512) if m >= 0 else None))
                        attend(h, qt * 512, 512, [(i * 128, 128) for i in range(4)], kts, qt * 512, qt)
                    mm_kt = (kTm, b_kTm, Vh[:, 5 * NKT, :], b_Vm, cf("biasmm", 1, h)[0:16, :], 16, cb("maskmm", 16)[0:16, :])
                    attend(h, CH, 16, [(0, 16)], [mm_kt], CH, XT)
                while not sdone[0]:
                    sample_step()
                P.barrier()

            if cfg.get("STOP") == "B3":
                return
            with ExitStack() as st:
                wo = Rot("wo", [128, KC, 256], BF16, 3, st)
                xTbo = [[Buf(f"ox{k}_{t}") for t in range(len(TTF))] for k in range(KC)]
                hTbo = [[Buf(f"oh{k}_{t}") for t in range(len(TTF))] for k in range(KC)]
                for oc0 in range(0, KC, 2):
                    wt, wb = wo.get()
                    P.add("pool", wload(wt[:], wts["w_out"], oc0 * 128, 256, KC), [], [wb], kind="d")
                    for ff in range(2):
                        oc = oc0 + ff
                        for ti, (t0, n) in enumerate(TTF):
                            ps, pb = next_ps()
                            P.add("pe", [lambda e, k=k, ps=ps, wt=wt, ff=ff, t0=t0, n=n: e.matmul(
                                ps[:, 0:n], lhsT=wt[:, k, ff * 128:(ff + 1) * 128], rhs=hT[:, k, t0:t0 + n],
                                start=(k == 0), stop=(k == KC - 1)) for k in range(KC)],
                                [wb] + [hTbo[k][ti] for k in range(KC)], [pb])
                            P.add("dve", lambda e, ps=ps, oc=oc, t0=t0, n=n: e.tensor_tensor(
                                out=xT[:, oc, t0:t0 + n], in0=ps[:, 0:n], in1=xT[:, oc, t0:t0 + n], op=ALU.add),
                                [pb, xTbo[oc][ti]], [xTbo[oc][ti]])
                P.barrier()


        ffn(0, wts["w_gate1"], wts["w_up1"], wts["w_down1"])
        if cfg.get("MIXER", True):
            mixer(locals())
        ffn(32, wts["w_gate2"], wts["w_up2"], wts["w_down2"])

        with ExitStack() as st:
            ytok = Rot("ytok", [128, D], F32, 2, st)
            yf = sb("yf", [128, KC, 128], F32, st)
            b_yf = Buf("yf")
            gF = 48
            for ti, (t0, n) in enumerate(TT):
                rs, rsb = rms_rstd(ti)
                for s0 in range(0, n, 128):
                    ns = min(128, n - s0)
                    for k in range(KC):
                        P.add("dve", lambda e, k=k, rs=rs, t0=t0, s0=s0, ns=ns: e.scalar_tensor_tensor(
                            out=yf[:, k, 0:ns], in0=xT[:, k, t0 + s0:t0 + s0 + ns], scalar=cv_sb[:, gF + k:gF + k + 1],
                            in1=rs[:, s0:s0 + ns], op0=ALU.mult, op1=ALU.mult), [xTb[k][ti], rsb, b_cv, b_yf], [b_yf])
                    yt, ytb = ytok.get()
                    for k4 in range(KC // 4):
                        ps, pb = next_ps()
                        P.add("pe", [lambda e, kk=kk, k4=k4, ps=ps, ns=ns: e.transpose(
                            out=ps[0:ns, kk * 128:(kk + 1) * 128], in_=yf[:, k4 * 4 + kk, 0:ns], identity=ident_f)
                            for kk in range(4)], [b_yf, b_cf], [pb])
                        evac(yt[0:ns, k4 * 512:(k4 + 1) * 512], ps[0:ns, :], [pb], [ytb])
                    dma("sp", y_o[t0 + s0:t0 + s0 + ns, :], yt[0:ns, :], [ytb], [B["out"]])
            P.barrier()

        streams = P.emit(nc, sems, None)
        with nc.Block() as block:
            @block.tensor
            def _(e):
                run_streams({"pe": streams["pe"]}, sems, {"pe": e})

            @block.scalar
            def _(e):
                run_streams({"act": streams["act"]}, sems, {"act": e})

            @block.vector
            def _(e):
                run_streams({"dve": streams["dve"]}, sems, {"dve": e})

            @block.gpsimd
            def _(e):
                run_streams({"pool": streams["pool"]}, sems, {"pool": e})

            @block.sync
            def _(e):
                run_streams({"sp": streams["sp"]}, sems, {"sp": e})
    return nc


def host_consts(cfg, core):
    import ml_dtypes
    CH, PAGE = cfg["CH"], cfg["PAGE"]
    NKT = CH // 128
    j = core % 4
    off = {}
    cols = []
    i = np.arange(128, dtype=np.float64)[:, None]

    def addf(name, arr):
        off[name] = sum(a.shape[1] for a in cols)
        a = np.zeros((128, np.asarray(arr).shape[1]), np.float32)
        a[:np.asarray(arr).shape[0]] = np.asarray(arr, np.float32)
        cols.append(a)

    addf("ident", np.eye(128))
    addf("onesf", np.ones((128, 128)))
    addf("eps", np.full((128, 1), EPS))
    sel = np.zeros((128, 5));
    if j == 0:
        sel[:, 4] = 1.0
    else:
        sel[:, j - 1] = 1.0
    addf("sel", sel)
    bg = np.zeros((128, NH * 4 * NKT))
    bo = np.zeros((128, NH * NKT))
    bm = np.zeros((128, NH)); bmm = np.zeros((128, NH))
    for h in range(NH):
        sl = SLOPES[h]
        for r in range(4):
            for kt in range(NKT):
                bg[:, (h * 4 + r) * NKT + kt] = sl * (i[:, 0] + 128 * kt + CH * (r - j)) + (NEG if r >= j else 0.0)
        for kt in range(NKT):
            bo[:, h * NKT + kt] = sl * (i[:, 0] + 128 * kt)
        bm[:, h] = sl * (i[:, 0] - 16 - j * CH)
        bmm[:, h] = sl * i[:, 0]
    addf("biasg", bg); addf("biaso", bo); addf("biasm", bm); addf("biasmm", bmm)
    addf("qrow", np.broadcast_to(np.arange(CH, dtype=np.float64)[None, :], (128, CH)))
    slc = SLOPES[core]
    addf("sbias", slc * PAGE * (i - 127))
    tok = np.repeat(np.arange(PAGE), 8)[None, :]
    addf("wtok", np.broadcast_to(np.exp(slc * (tok - (PAGE - 1))), (128, PAGE * 8)))
    addf("nbias", slc * (1 + i))
    nm = np.zeros((128, 8))
    for jn in range(4):
        for m in range(2):
            for q in range(4):
                nm[jn, m * 4 + q] = 1.0 if jn <= q else 0.0
    addf("nmask", nm)
    invc = np.zeros((128, 128))
    for kc in range(8):
        for t in range(16):
            invc[:, kc * 16 + t] = 1.0 / min(t + 1, WINS[kc // 2])
    addf("invc", invc)
    c0 = np.zeros((128, 4)); c1 = np.zeros((128, 4))
    for q in range(4):
        c0[q, q] = 1.0; c1[4 + q, q] = 1.0
    addf("cb0", c0); addf("cb1", c1)
    cf = np.concatenate(cols, 1)
    colsb = []

    def addb(name, arr):
        off[name] = sum(a.shape[1] for a in colsb)
        colsb.append(np.asarray(arr, np.float32).astype(ml_dtypes.bfloat16))

    addb("identb", np.eye(128))
    addb("onesb", np.ones((128, 128)))
    addb("zerob", np.zeros((128, 128)))
    jq = np.arange(512)[None, :]
    md = np.concatenate([np.where(i + 128 * m <= jq, 0.0, NEG) for m in range(4)], 1)
    addb("maskd", md)
    addb("maskmm", np.where(i <= np.arange(16)[None, :], 0.0, NEG))
    cb = np.concatenate(colsb, 1)
    return off, cf, cb


def fm(v, n):
    return np.ascontiguousarray(np.asarray(v, np.float32).reshape(n, 128).T)


def prep_inputs(cfg, inp):
    CH, NSUB = cfg["CH"], cfg["NSUB"]
    maps = []
    x_prompt = np.asarray(inp["x_prompt"]); x_sample = np.asarray(inp["x_sample"])
    meta = np.asarray(inp["meta_tokens"])
    cvec = np.concatenate([fm(inp["norm_ffn1"][0], 16), fm(inp["norm_mix"][0], 16), fm(inp["norm_ffn2"][0], 16),
                           fm(inp["norm_final"], 16), fm(inp["pool_scale"][0], 8)], 1)
    gain_bc = np.ascontiguousarray(np.broadcast_to(np.asarray(inp["subln_gain"][0], np.float32)[None, :], (128, 128)))
    lamv = np.concatenate([np.asarray(inp[k][0], np.float32) for k in ("lambda_q1", "lambda_k1", "lambda_q2", "lambda_k2")])[None, :]
    w_in = np.asarray(inp["w_in"][0])
    shared = dict(cvec=cvec, gain_bc=gain_bc, lamv=np.ascontiguousarray(lamv),
                  w_gate1=np.asarray(inp["w_gate1"][0]), w_up1=np.asarray(inp["w_up1"][0]), w_down1=np.asarray(inp["w_down1"][0]),
                  w_in=w_in, w_out=np.asarray(inp["w_out"][0]),
                  w_gate2=np.asarray(inp["w_gate2"][0]), w_up2=np.asarray(inp["w_up2"][0]), w_down2=np.asarray(inp["w_down2"][0]),
                  w_pool=np.ascontiguousarray(np.asarray(inp["w_pool"][0]).reshape(1024, 256)),
                  spool=np.ascontiguousarray(np.asarray(inp["state_pool"][0]).reshape(120, 1024)),
                  ptab=np.ascontiguousarray(np.asarray(inp["page_table"]).T.astype(np.int32)))
    ckf = np.asarray(inp["cache_k"][0]); cvf = np.asarray(inp["cache_v"][0])
    npool = ckf.shape[0]
    for c in range(8):
        b, j = c // 4, c % 4
        m = dict(shared)
        m["xin"] = np.ascontiguousarray(np.concatenate([x_prompt[b, j * CH:(j + 1) * CH], meta, x_sample.reshape(32, D)], 0))
        m["w_in_hd"] = np.ascontiguousarray(np.concatenate([w_in[:, 1024 + 128 * c:1152 + 128 * c],
                                                            w_in[:, 2048 + 128 * c:2176 + 128 * c],
                                                            w_in[:, 3072 + 128 * c:3200 + 128 * c]], 1))
        for u in range(NSUB):
            m[f"ck{u}"] = np.ascontiguousarray(ckf[:, 16 * u:16 * u + 16, c, :].reshape(npool, 2048))
            m[f"cv{u}"] = np.ascontiguousarray(cvf[:, 16 * u:16 * u + 16, c, :].reshape(npool, 2048))
        off, cf, cb = host_consts(cfg, c)
        m["cf32"], m["cbf"] = cf, cb
        maps.append(m)
    return maps


def finish_cfg(cfg):
    off, cf, cb = host_consts(cfg, 0)
    cfg["OFF"] = off; cfg["NCF"] = cf.shape[1]; cfg["NCB"] = cb.shape[1]
    return cfg


def assemble(cfg, res):
    CH, SEQ = cfg["CH"], cfg["SEQ"]
    y_prompt = np.zeros((2, SEQ, D), np.float32)
    k_prompt = np.zeros((1, 2, 16 + SEQ, 8, 128), np.float32)
    v_prompt = np.zeros((1, 2, 16 + SEQ, 8, 128), np.float32)
    pool_prompt = np.zeros((1, 2, 15, 1024), np.float32)
    for c in range(8):
        b, j = c // 4, c % 4
        r = res[c]
        y_prompt[b, j * CH:(j + 1) * CH] = r["y"][0:CH]
        k_prompt[0, b, 16 + j * CH:16 + (j + 1) * CH] = r["ko"][0:CH].reshape(CH, 8, 128)
        v_prompt[0, b, 16 + j * CH:16 + (j + 1) * CH] = r["vo"][0:CH].reshape(CH, 8, 128)
        if j == 0:
            k_prompt[0, b, 0:16] = r["ko"][CH:CH + 16].reshape(16, 8, 128)
            v_prompt[0, b, 0:16] = r["vo"][CH:CH + 16].reshape(16, 8, 128)
        if j == 3:
            pool_prompt[0, b] = r["ptail"].reshape(128, 8, 15).transpose(2, 1, 0).reshape(15, 1024)
    r0 = res[0]
    y_sample = np.ascontiguousarray(r0["y"][CH + 16:CH + 48].reshape(8, 4, D))
    k_sample = np.ascontiguousarray(r0["ko"][CH + 16:CH + 48].reshape(1, 8, 4, 8, 128))
    v_sample = np.ascontiguousarray(r0["vo"][CH + 16:CH + 48].reshape(1, 8, 4, 8, 128))
    new = r0["psnew"].reshape(128, 8, 8, 4).transpose(2, 3, 1, 0).reshape(8, 4, 1024)
    old = r0["psold"].reshape(8, 11, 1024)
    pool_sample = np.ascontiguousarray(np.concatenate([old, new], 1)[None])
    return (y_prompt, y_sample, k_prompt, v_prompt, pool_prompt, k_sample, v_sample, pool_sample)


_CACHE = {}


def kernel(**inputs):
    cfg = finish_cfg(make_cfg())
    if "nc" not in _CACHE:
        _CACHE["nc"] = build(cfg)
    nc = _CACHE["nc"]
    maps = prep_inputs(cfg, inputs)
    res = run_bass_kernel_spmd(nc, maps, core_ids=list(range(8)))
    return assemble(cfg, res.results)
```

```python
import numpy as np
import concourse.bass as bass
import concourse.mybir as mybir
from concourse.bass_utils import run_bass_kernel_spmd

F32 = mybir.dt.float32
BF16 = mybir.dt.bfloat16
I32 = mybir.dt.int32
ALU = mybir.AluOpType
AF = mybir.ActivationFunctionType

D = 2048
KC = 16
NH = 8
EPS = 1e-6
NEG = -30000.0
SLOPES = [2.0 ** (-8.0 * (h + 1) / NH) for h in range(NH)]
LAM_INIT = 0.2
WINS = (2, 4, 8, 16)


class Buf:
    __slots__ = ("name", "w", "r")

    def __init__(self, name):
        self.name = name
        self.w = None
        self.r = []


class Prog:
    def __init__(self):
        self.recs = []
        self.nslot = {"sp": 8, "pool": 8}

    def barrier(self):
        self.recs.append(("sp", [], [], [], "bar"))

    def add(self, eng, fns, reads=(), writes=(), kind="c"):
        if not isinstance(fns, (list, tuple)):
            fns = [fns]
        self.recs.append((eng, list(fns), list(reads), list(writes), kind))

    def emit(self, nc, sems, block_engines):
        cnt = {e: 0 for e in ("pe", "act", "dve", "pool")}
        slot_use = {q: [0] * n for q, n in self.nslot.items()}
        slot_next = {q: 0 for q in self.nslot}
        ncc = 0
        waited = {}
        streams = {e: [] for e in ("pe", "act", "dve", "pool", "sp")}

        def need(E, tok):
            if tok is None:
                return
            key = (E, tok[0])
            if waited.get(key, 0) >= tok[1]:
                return
            waited[key] = tok[1]
            streams[E].append(("w", tok[0], tok[1]))

        cc_done = []
        for eng, fns, reads, writes, kind in self.recs:
            E = eng
            if kind == "bar":
                alltok = [(e, cnt[e]) for e in cnt if cnt[e]]
                for q, n in self.nslot.items():
                    for s in range(n):
                        if slot_use[q][s]:
                            alltok.append((f"{q}{s}", 16 * slot_use[q][s]))
                alltok += cc_done
                for E2 in streams:
                    for t in alltok:
                        if t[0] == E2:
                            continue
                        need(E2, t)
                continue
            toks = []
            for b in reads:
                toks.append(b.w)
            for b in writes:
                toks.append(b.w)
                toks.extend(b.r)
            if kind == "c":
                cnt[E] += 1
                tok = (E, cnt[E])
                inc = 1
            elif kind == "d":
                q = E
                s = slot_next[q]
                slot_next[q] = (s + 1) % self.nslot[q]
                prev = slot_use[q][s]
                if prev:
                    toks.append((f"{q}{s}", 16 * prev))
                slot_use[q][s] = prev + 1
                tok = (f"{q}{s}", 16 * (prev + 1))
                inc = 16
            else:
                tok = (f"cc{ncc}", 1)
                cc_done.append(tok)
                ncc += 1
                inc = 1
            for t in toks:
                if t is None:
                    continue
                if t[0] == E and E == "pe":
                    continue
                need(E, t)
            streams[E].append(("i", fns, tok[0], inc))
            for b in writes:
                b.w = tok
                b.r = []
            for b in reads:
                if b not in writes:
                    b.r.append(tok)
        final = []
        for q, n in self.nslot.items():
            for s in range(n):
                if slot_use[q][s]:
                    final.append((f"{q}{s}", 16 * slot_use[q][s]))
        for t in final:
            need("sp", t)
        for e in ("pe", "act", "dve", "pool"):
            if cnt[e]:
                need("sp", (e, cnt[e]))
        return streams


def run_streams(streams, sems, engs):
    for e, lst in streams.items():
        eng = engs[e]
        for it in lst:
            if it[0] == "w":
                eng.wait_ge(sems[it[1]], it[2])
            else:
                _, fns, semname, inc = it
                last = None
                for f in fns:
                    last = f(eng)
                last.then_inc(sems[semname], inc)


def make_cfg(seq=4096, dff=5632, page=128, npool=1280):
    ch = seq // 4
    return dict(SEQ=seq, CH=ch, DFF=dff, PAGE=page, NPOOL=npool, NT=ch + 48, NSUB=page // 16)


def token_tiles(cfg):
    ch = cfg["CH"]
    tl = [(i * 512, 512) for i in range(ch // 512)]
    tl.append((ch, 48))
    return tl


GROUPS4 = [[0, 1, 2, 3], [4, 5, 6, 7]]
PAIRS = [[0, 4], [1, 5], [2, 6], [3, 7]]


def build(cfg):
    from contextlib import ExitStack
    CH, DFF, PAGE, NT, NPOOL, NSUB = cfg["CH"], cfg["DFF"], cfg["PAGE"], cfg["NT"], cfg["NPOOL"], cfg["NSUB"]
    NF = DFF // 128
    TT = token_tiles(cfg)
    NTT = len(TT)
    NKT = CH // 128
    NQT = CH // 512
    nc = bass.Bass("TRN2", target_bir_lowering=False, num_devices=8)
    P = Prog()
    CO = cfg["OFF"]

    def din(name, shape, dt=F32):
        return nc.dram_tensor(name, list(shape), dt, kind="ExternalInput").ap()

    def dout(name, shape, dt=F32):
        return nc.dram_tensor(name, list(shape), dt, kind="ExternalOutput").ap()

    def dint(name, shape, dt):
        return nc.dram_tensor(name, list(shape), dt, kind="Internal").ap()

    xin = din("xin", [NT, D])
    wts = {}
    for nm, shp in (("w_gate1", [D, DFF]), ("w_up1", [D, DFF]), ("w_down1", [DFF, D]), ("w_in", [D, 4096]),
                    ("w_out", [D, D]), ("w_gate2", [D, DFF]), ("w_up2", [D, DFF]), ("w_down2", [DFF, D]),
                    ("w_pool", [1024, 256]), ("w_in_hd", [D, 384])):
        wts[nm] = din(nm, shp)
    cvec = din("cvec", [128, 72])
    gain_bc = din("gain_bc", [128, 128])
    lamv = din("lamv", [1, 256])
    cks = [din(f"ck{u}", [NPOOL, 2048]) for u in range(NSUB)]
    cvs = [din(f"cv{u}", [NPOOL, 2048]) for u in range(NSUB)]
    ptab = din("ptab", [128, 8], I32)
    spool = din("spool", [120, 1024])
    cf32 = din("cf32", [128, cfg["NCF"]])
    cbf = din("cbf", [128, cfg["NCB"]], BF16)

    y_o = dout("y", [NT, D])
    k_o = dout("ko", [NT, 1024])
    v_o = dout("vo", [NT, 1024])
    pt_o = dout("ptail", [128, 120])
    psn_o = dout("psnew", [128, 256])
    pso_o = dout("psold", [88, 1024])

    src_kT = [dint(f"src_kT{t}", [1024, 512], BF16) for t in range(NQT)]
    gat_kT = [dint(f"gat_kT{t}", [4096, 512], BF16) for t in range(NQT)]
    src_V = [dint(f"src_V{t}", [512, 1024], BF16) for t in range(NQT)]
    gat_V = [dint(f"gat_V{t}", [2048, 1024], BF16) for t in range(NQT)]
    src_ph = dint("src_ph", [128, 128], F32)
    gat_ph = dint("gat_ph", [512, 128], F32)
    scr_q = dint("scr_q", [1024, CH + 16], BF16)
    scr_mk = dint("scr_mk", [1024, 16], BF16)
    scr_mv = dint("scr_mv", [16, 1024], BF16)
    src_o = dint("src_o", [32, 128], F32)
    gat_o4 = dint("gat_o4", [128, 128], F32)
    gat_o8 = dint("gat_o8", [256, 128], F32)
    B = {n: Buf(n) for n in ("src_kT", "gat_kT", "src_V", "gat_V", "src_ph", "gat_ph", "scr_q", "scr_mk", "scr_mv",
                             "src_o", "gat_o4", "gat_o8", "out")}

    es = ExitStack()

    uniq = [0]

    def sb(name, shape, dt=F32, st=None):
        uniq[0] += 1
        return (st or es).enter_context(nc.sbuf_tensor(f"{name}_{uniq[0]}", list(shape), dt))

    class Rot:
        def __init__(self, name, shape, dt, n, st=None):
            self.t = [sb(f"{name}{i}", shape, dt, st) for i in range(n)]
            self.b = [Buf(f"{name}{i}") for i in range(n)]
            self.i = 0

        def get(self):
            k = self.i % len(self.t)
            self.i += 1
            return self.t[k], self.b[k]

    with es:
        sems = {}
        for nm in (["pe", "act", "dve", "pool"] + [f"sp{i}" for i in range(8)] + [f"pool{i}" for i in range(8)]
                   + [f"cc{i}" for i in range(8)]):
            sems[nm] = es.enter_context(nc.semaphore("s_" + nm))
        psum = [es.enter_context(nc.psum_tensor(f"ps{i}", [128, 512], F32)) for i in range(7)]
        psb = [Buf(f"ps{i}") for i in range(7)]
        psbf = es.enter_context(nc.psum_tensor("psbf", [128, 1024], BF16))
        b_psbf = Buf("psbf")
        pcount = [0]

        def next_ps():
            i = pcount[0] % 7
            pcount[0] += 1
            return psum[i], psb[i]

        SKIP = cfg.get("SKIP") or ""
        phase = [""]

        def dma(q, out, in_, reads, writes):
            if phase[0] == "A" and "d" in SKIP:
                return
            P.add(q, lambda e, o=out, i=in_: e.dma_start(out=o, in_=i), reads, writes, kind="d")

        def cc(groups, src, dst, reads, writes):
            P.add("pool", lambda e: e.collective_compute("AllGather", ALU.bypass, replica_groups=groups,
                                                          ins=[src], outs=[dst]), reads, writes, kind="cc")

        xT = sb("xT", [128, KC, NT])
        xTb = [[Buf(f"x{k}_{t}") for t in range(NTT)] for k in range(KC)]
        hT = sb("hT", [128, KC, NT], BF16)
        hTb = [[Buf(f"h{k}_{t}") for t in range(NTT)] for k in range(KC)]
        cv_sb = sb("cv_sb", [128, 72]); b_cv = Buf("cv")
        cf_sb = sb("cf_sb", [128, cfg["NCF"]]); b_cf = Buf("cf")
        cb_sb = sb("cb_sb", [128, cfg["NCB"]], BF16); b_cb = Buf("cb")
        gain_sb = sb("gain_sb", [128, 128]); b_gain = Buf("gain")
        pt_sb = sb("pt_sb", [128, 8], I32); b_pt = Buf("pt")
        lam_sb = sb("lam_sb", [128, 4]); b_lam = Buf("lam")
        sqr = Rot("sq", [128, 512], BF16, 3)
        rstd_r = Rot("rstd", [128, 512], F32, 2)

        def cf(name, w=1, o=0):
            return cf_sb[:, CO[name] + o:CO[name] + o + w]

        def cb(name, w, o=0):
            return cb_sb[:, CO[name] + o:CO[name] + o + w]

        ident_f = cf("ident", 128)
        ident_b = cb("identb", 128)
        ones_b = cb("onesb", 128)
        eps_col = cf("eps")

        dma("sp", cv_sb[:], cvec[:, :], [], [b_cv])
        dma("sp", cf_sb[:], cf32[:, :], [], [b_cf])
        dma("sp", cb_sb[:], cbf[:, :], [], [b_cb])
        dma("sp", gain_sb[:], gain_bc[:, :], [], [b_gain])
        dma("sp", pt_sb[:], ptab[:, :], [], [b_pt])

        evac_flip = [0]

        def evac(out, in_, reads, writes, scale=None, eng=None):
            evac_flip[0] ^= 1
            if eng is None:
                eng = "act" if evac_flip[0] else "dve"
            if eng == "act":
                if scale is None:
                    P.add("act", lambda e: e.activation(out=out, in_=in_, func=AF.Copy), reads, writes)
                else:
                    P.add("act", lambda e: e.activation(out=out, in_=in_, func=AF.Copy, scale=scale), reads, writes)
            else:
                if scale is None:
                    P.add("dve", lambda e: e.tensor_copy(out=out, in_=in_), reads, writes)
                else:
                    P.add("dve", lambda e: e.tensor_scalar(out=out, in0=in_, scalar1=scale, scalar2=None, op0=ALU.mult),
                          reads, writes)

        def tile_of(tok):
            for ti, (t0, n) in enumerate(TT):
                if t0 <= tok < t0 + n:
                    return ti
            raise ValueError

        with ExitStack() as st:
            xtok = Rot("xtok", [128, D], F32, 2, st)
            for tt in range((NT + 127) // 128):
                r0 = tt * 128
                nr = min(128, NT - r0)
                xt, xb = xtok.get()
                dma("sp", xt[0:nr, :], xin[r0:r0 + nr, :], [], [xb])
                ti = tile_of(r0)
                for k4 in range(KC // 4):
                    ps, pb = next_ps()
                    fns = []
                    for kk in range(4):
                        k = k4 * 4 + kk
                        fns.append(lambda e, k=k, kk=kk, ps=ps, xt=xt, nr=nr: e.transpose(
                            out=ps[:, kk * 128:kk * 128 + nr], in_=xt[0:nr, k * 128:(k + 1) * 128],
                            identity=ident_f[0:nr, 0:nr]))
                    P.add("pe", fns, [xb, b_cf], [pb])
                    evac(xT[:, k4 * 4:k4 * 4 + 4, r0:r0 + nr], ps[:].rearrange("p (a b) -> p a b", a=4)[:, :, 0:nr],
                         [pb], [xTb[k][ti] for k in range(k4 * 4, k4 * 4 + 4)])
            P.barrier()

        def rms_rstd(ti, tiles=None, xb=None):
            tiles = tiles or TT
            xb = xb or xTb
            t0, n = tiles[ti]
            ps, pb = next_ps()
            for k in range(KC):
                sq, sqb = sqr.get()
                P.add("act", lambda e, sq=sq, k=k: e.activation(out=sq[:, 0:n], in_=xT[:, k, t0:t0 + n], func=AF.Square),
                      [xb[k][ti]], [sqb])
                P.add("pe", lambda e, sq=sq, k=k, ps=ps: e.matmul(ps[:, 0:n], lhsT=ones_b, rhs=sq[:, 0:n],
                                                                 start=(k == 0), stop=(k == KC - 1)),
                      [sqb, b_cb] + ([pb] if k else []), [pb])
            rs, rsb = rstd_r.get()
            P.add("act", lambda e: e.activation(out=rs[:, 0:n], in_=ps[:, 0:n], func=AF.Sqrt, bias=eps_col, scale=1.0 / D),
                  [pb, b_cf], [rsb])
            P.add("dve", lambda e: e.reciprocal(out=rs[:, 0:n], in_=rs[:, 0:n]), [rsb], [rsb])
            return rs, rsb

        def rmsnorm_to_h(gbase, tiles=None, xb=None, hb=None):
            tiles = tiles or TT
            xb = xb or xTb
            hb = hb or hTb
            for ti, (t0, n) in enumerate(tiles):
                rs, rsb = rms_rstd(ti, tiles, xb)
                for k in range(KC):
                    P.add("dve", lambda e, k=k, rs=rs, t0=t0, n=n: e.scalar_tensor_tensor(
                        out=hT[:, k, t0:t0 + n], in0=xT[:, k, t0:t0 + n], scalar=cv_sb[:, gbase + k:gbase + k + 1],
                        in1=rs[:, 0:n], op0=ALU.mult, op1=ALU.mult),
                        [xb[k][ti], rsb, b_cv], [hb[k][ti]])

        n3 = (((NT + 2) // 3) + 1) // 2 * 2
        TTF = [(0, n3), (n3, n3), (2 * n3, NT - 2 * n3)]

        def wload(dst, w2d, n0, wdt, nk):
            return lambda e: e.dma_start(out=dst, in_=w2d.rearrange("(kc p) n -> p kc n", p=128)[:, 0:nk, n0:n0 + wdt])

        FP = 8

        def ffn(gbase, wg, wu, wd):
            with ExitStack() as st:
                TT = TTF
                NTT = len(TTF)
                xTb = [[Buf(f"fx{k}_{t}") for t in range(NTT)] for k in range(KC)]
                hTb = [[Buf(f"fh{k}_{t}") for t in range(NTT)] for k in range(KC)]
                wgu = Rot("wgu", [128, KC, 256], BF16, 4, st)
                hid = sb("hid", [128, FP, NT], BF16, st)
                hidb = [[Buf(f"hid{f}_{t}") for t in range(NTT)] for f in range(FP)]
                wdn = Rot("wdn", [128, FP, 512], BF16, 2, st)
                sgr = Rot("sg", [128, 512], F32, 2, st)
                rmsnorm_to_h(gbase, TT, xTb, hTb)
                f0 = 0
                while f0 < NF:
                    nfp = min(FP, NF - f0)
                    for fp in range(0, nfp, 2):
                        wgt, wgb = wgu.get()
                        P.add("pool", wload(wgt[:], wg, (f0 + fp) * 128, 256, KC), [], [wgb], kind="d")
                        wut, wub = wgu.get()
                        P.add("pool", wload(wut[:], wu, (f0 + fp) * 128, 256, KC), [], [wub], kind="d")
                        for ff in range(2):
                            fi = fp + ff
                            for ti, (t0, n) in enumerate(TT):
                                psA, pbA = next_ps()
                                P.add("pe", [lambda e, k=k, psA=psA, wgt=wgt, ff=ff, t0=t0, n=n: e.matmul(
                                    psA[:, 0:n], lhsT=wgt[:, k, ff * 128:(ff + 1) * 128], rhs=hT[:, k, t0:t0 + n],
                                    start=(k == 0), stop=(k == KC - 1)) for k in range(KC)],
                                    [wgb] + [hTb[k][ti] for k in range(KC)], [pbA])
                                psB, pbB = next_ps()
                                P.add("pe", [lambda e, k=k, psB=psB, wut=wut, ff=ff, t0=t0, n=n: e.matmul(
                                    psB[:, 0:n], lhsT=wut[:, k, ff * 128:(ff + 1) * 128], rhs=hT[:, k, t0:t0 + n],
                                    start=(k == 0), stop=(k == KC - 1)) for k in range(KC)],
                                    [wub] + [hTb[k][ti] for k in range(KC)], [pbB])
                                sg, sgb = sgr.get()
                                P.add("act", lambda e, sg=sg, psA=psA, n=n: e.activation(out=sg[:, 0:n], in_=psA[:, 0:n],
                                                                                        func=AF.Silu), [pbA], [sgb])
                                P.add("dve", lambda e, sg=sg, psB=psB, fi=fi, t0=t0, n=n: e.tensor_tensor(
                                    out=hid[:, fi, t0:t0 + n], in0=sg[:, 0:n], in1=psB[:, 0:n], op=ALU.mult),
                                    [sgb, pbB], [hidb[fi][ti]])
                    for o4 in range(4):
                        wdt_, wdb = wdn.get()
                        P.add("pool", lambda e, wdt_=wdt_, f0=f0, nfp=nfp, o4=o4: e.dma_start(
                            out=wdt_[:, 0:nfp, :],
                            in_=wd.rearrange("(fc p) n -> p fc n", p=128)[:, f0:f0 + nfp, o4 * 512:(o4 + 1) * 512]),
                            [], [wdb], kind="d")
                        for oo in range(4):
                            oc = o4 * 4 + oo
                            for ti, (t0, n) in enumerate(TT):
                                ps, pb = next_ps()
                                P.add("pe", [lambda e, f=f, ps=ps, wdt_=wdt_, oo=oo, t0=t0, n=n, nfp=nfp: e.matmul(
                                    ps[:, 0:n], lhsT=wdt_[:, f, oo * 128:(oo + 1) * 128], rhs=hid[:, f, t0:t0 + n],
                                    start=(f == 0), stop=(f == nfp - 1)) for f in range(nfp)],
                                    [wdb] + [hidb[f][ti] for f in range(nfp)], [pb])
                                P.add("dve", lambda e, ps=ps, oc=oc, t0=t0, n=n: e.scalar_tensor_tensor(
                                    out=xT[:, oc, t0:t0 + n], in0=ps[:, 0:n], scalar=0.5, in1=xT[:, oc, t0:t0 + n],
                                    op0=ALU.mult, op1=ALU.add), [pb, xTb[oc][ti]], [xTb[oc][ti]])
                    f0 += nfp
                P.barrier()

        def mixer(_):
            mst = ExitStack()
            Qblk = sb("Qblk_", [128, 8, 8], BF16); b_Q = Buf("Qblk")
            ksT = sb("ksT_", [128, 32], BF16); b_ksT = Buf("ksT")
            vsf = sb("vsf_", [128, 32], F32); b_vsf = Buf("vsf")
            Vnew = sb("Vnew_", [4, 8, 128], BF16); b_Vn = Buf("Vnew")
            gain08 = sb("gain08_", [128, 128], F32); b_g08 = Buf("g08")
            lv = sb("lv_", [1, 256], F32); b_lv = Buf("lv")
            lt = sb("lt_", [1, 8], F32); b_lt = Buf("lt")
            Cm = sb("Cm_", [8, 4], F32); b_Cm = Buf("Cm")
            pAll = sb("pAll", [128, 8, CH + 15], F32, mst); b_pA = [Buf(f"pA{k}") for k in range(8)]
            pM = sb("pM", [128, 8, 31], F32, mst); b_pM = Buf("pM")
            pS = sb("pS", [128, 8, 152], F32, mst); b_pS = Buf("pS")
            P.add("dve", lambda e: e.memset(pM[:], 0.0), [], [b_pM])
            P.add("dve", lambda e: e.memset(Qblk[:], 0.0), [], [b_Q])
            P.add("dve", lambda e: e.tensor_scalar(out=gain08[:], in0=gain_sb[:], scalar1=1.0 - LAM_INIT, scalar2=None,
                                                   op0=ALU.mult), [b_gain], [b_g08])
            dma("sp", lv[:], lamv[:, :], [], [b_lv])
            P.add("dve", lambda e: e.tensor_tensor(out=lv[0:1, 0:64], in0=lv[0:1, 0:64], in1=lv[0:1, 64:128], op=ALU.mult),
                  [b_lv], [b_lv])
            P.add("dve", lambda e: e.tensor_tensor(out=lv[0:1, 128:192], in0=lv[0:1, 128:192], in1=lv[0:1, 192:256],
                                                   op=ALU.mult), [b_lv], [b_lv])
            P.add("dve", lambda e: e.reduce_sum(out=lt[0:1, 0:1], in_=lv[0:1, 0:64], axis=mybir.AxisListType.X), [b_lv], [b_lt])
            P.add("dve", lambda e: e.reduce_sum(out=lt[0:1, 1:2], in_=lv[0:1, 128:192], axis=mybir.AxisListType.X),
                  [b_lv, b_lt], [b_lt])
            P.add("act", lambda e: e.activation(out=lt[0:1, 2:4], in_=lt[0:1, 0:2], func=AF.Exp), [b_lt], [b_lt])
            P.add("dve", lambda e: e.tensor_tensor(out=lt[0:1, 4:5], in0=lt[0:1, 2:3], in1=lt[0:1, 3:4], op=ALU.subtract),
                  [b_lt], [b_lt])
            P.add("dve", lambda e: e.tensor_scalar(out=lt[0:1, 5:6], in0=lt[0:1, 4:5], scalar1=LAM_INIT, scalar2=None,
                                                   op0=ALU.add), [b_lt], [b_lt])
            ps, pb = next_ps()
            P.add("pe", lambda e, ps=ps: e.matmul(ps[:, 0:1], lhsT=cf("onesf", 128)[0:1, :], rhs=lt[0:1, 5:6],
                                                  start=True, stop=True), [b_lt, b_cf], [pb])
            P.add("dve", lambda e, ps=ps: e.tensor_copy(out=lam_sb[:, 0:1], in_=ps[:, 0:1]), [pb], [b_lam])
            P.add("dve", lambda e, ps=ps: e.tensor_scalar(out=lam_sb[:, 1:2], in0=ps[:, 0:1], scalar1=-1.0, scalar2=None,
                                                          op0=ALU.mult), [pb, b_lam], [b_lam])
            P.add("dve", lambda e: e.scalar_tensor_tensor(out=Cm[:], in0=cf("cb1", 4)[0:8, :], scalar=lam_sb[0:8, 1:2],
                                                          in1=cf("cb0", 4)[0:8, :], op0=ALU.mult, op1=ALU.add),
                  [b_lam, b_cf], [b_Cm])

            if cfg.get("STOP") == "L":
                mst.close(); return
            rmsnorm_to_h(16)
            XT = NTT - 1
            with ExitStack() as st:
                win = Rot("win", [128, KC, 256], BF16, 3, st)
                kfr = Rot("kf", [128, 512], F32, 2, st)
                k16 = Rot("k16", [128, 512], BF16, 2, st)
                kst = Rot("kst", [128, 4, 128], F32, 2, st)
                vst = Rot("vst", [128, 4, 128], BF16, 2, st)
                for t_, b_ in zip(vst.t, vst.b):
                    P.add("dve", lambda e, t_=t_: e.memset(t_[:], 1.0), [], [b_])

                phase[0] = "A"

                def tok_major_out(ft, fb, n, t0, h, dst_o, want_bf):
                    if "t" in SKIP:
                        return
                    ps2, pb2 = next_ps()
                    nsub = (n + 127) // 128
                    rows = min(128, n)
                    P.add("pe", [lambda e, j=j, ps2=ps2: e.transpose(
                        out=ps2[0:min(128, n - j * 128), j * 128:(j + 1) * 128],
                        in_=ft[:, j * 128:j * 128 + min(128, n - j * 128)], identity=ident_f) for j in range(nsub)],
                        [fb, b_cf], [pb2])
                    kt_, ktb = kst.get()
                    src = ps2[0:rows, 0:nsub * 128].rearrange("p (j d) -> p j d", d=128)
                    evac(kt_[0:rows, 0:nsub, :], src, [pb2], [ktb], eng="act")
                    if n == 512:
                        dma("sp", dst_o[t0:t0 + 512, h * 128:(h + 1) * 128].rearrange("(j p) d -> p j d", p=128), kt_[:, :, :],
                            [ktb], [B["out"]])
                    else:
                        dma("sp", dst_o[t0:t0 + n, h * 128:(h + 1) * 128], kt_[0:n, 0, :], [ktb], [B["out"]])
                    if want_bf:
                        vt_, vtb = vst.get()
                        P.add("dve", lambda e, vt_=vt_, kt_=kt_: e.tensor_copy(out=vt_[0:rows, 0:nsub, 0:128], in_=kt_[0:rows, 0:nsub, :]),
                              [ktb], [vtb])
                        if n == 512:
                            dma("sp", src_V[t0 // 512][:, h * 128:(h + 1) * 128].rearrange("(j p) c -> p j c", p=128),
                                vt_[:, :, :], [vtb], [B["src_V"]])
                        else:
                            dma("sp", scr_mv[0:16, h * 128:(h + 1) * 128], vt_[0:16, 0, :], [vtb], [B["scr_mv"]])

                order = list(range(0, 8)) + list(range(16, 32)) + list(range(8, 16))
                for oi in range(0, 32, 2):
                    oc0 = order[oi]
                    wt, wb = win.get()
                    P.add("pool", wload(wt[:], wts["w_in"], oc0 * 128, 256, KC), [], [wb], kind="d")
                    for ff in range(2):
                        oc = oc0 + ff
                        for ti, (t0, n) in enumerate(TT):
                            ps, pb = next_ps()
                            P.add("pe", [lambda e, k=k, ps=ps, wt=wt, ff=ff, t0=t0, n=n: e.matmul(
                                ps[:, 0:n], lhsT=wt[:, k, ff * 128:(ff + 1) * 128], rhs=hT[:, k, t0:t0 + n],
                                start=(k == 0), stop=(k == KC - 1)) for k in range(KC)],
                                [wb] + [hTb[k][ti] for k in range(KC)], [pb])
                            if (oc < 8 and "P" in SKIP) or (8 <= oc < 16 and "Q" in SKIP) or (16 <= oc < 24 and "K" in SKIP) or (oc >= 24 and "V" in SKIP):
                                continue
                            if oc < 8:
                                if ti < XT:
                                    evac(pAll[:, oc, 15 + t0:15 + t0 + n], ps[:, 0:n], [pb], [b_pA[oc]])
                                else:
                                    evac(pM[:, oc, 15:31], ps[:, 0:16], [pb], [b_pM], eng="act")
                                    evac(pS[:, oc, :].rearrange("p (s c) -> p s c", c=19)[:, :, 15:19],
                                         ps[:, 16:48].rearrange("p (s q) -> p s q", q=4), [pb], [b_pS], eng="act")
                            elif oc < 16:
                                h = oc - 8
                                qt_, qb_ = k16.get()
                                evac(qt_[:, 0:n], ps[:, 0:n], [pb], [qb_], scale=0.125)
                                if ti < XT:
                                    dma("sp", scr_q[h * 128:(h + 1) * 128, t0:t0 + n], qt_[:, 0:n], [qb_], [B["scr_q"]])
                                else:
                                    dma("sp", scr_q[h * 128:(h + 1) * 128, CH:CH + 16], qt_[:, 0:16], [qb_], [B["scr_q"]])
                            elif oc < 24:
                                h = oc - 16
                                kf_, kfb = kfr.get()
                                evac(kf_[:, 0:n], ps[:, 0:n], [pb], [kfb], eng="act")
                                kb_, kbb = k16.get()
                                P.add("dve", lambda e, kb_=kb_, kf_=kf_, n=n: e.tensor_copy(out=kb_[:, 0:n], in_=kf_[:, 0:n]),
                                      [kfb], [kbb])
                                if ti < XT:
                                    dma("sp", src_kT[ti][h * 128:(h + 1) * 128, :], kb_[:, 0:n], [kbb], [B["src_kT"]])
                                else:
                                    dma("sp", scr_mk[h * 128:(h + 1) * 128, 0:16], kb_[:, 0:16], [kbb], [B["scr_mk"]])
                                tok_major_out(kf_, kfb, n, t0, h, k_o, False)
                            else:
                                h = oc - 24
                                vf_, vfb = kfr.get()
                                evac(vf_[:, 0:n], ps[:, 0:n], [pb], [vfb])
                                tok_major_out(vf_, vfb, n, t0, h, v_o, True)
                phase[0] = ""
                if cfg.get("STOP") == "A0":
                    P.barrier(); st.close(); mst.close(); return
                for part in range(3):
                    wt, wb = win.get()
                    P.add("pool", wload(wt[:, :, 0:128], wts["w_in_hd"], part * 128, 128, KC), [], [wb], kind="d")
                    ps, pb = next_ps()
                    P.add("pe", [lambda e, k=k, ps=ps, wt=wt: e.matmul(
                        ps[:, 0:32], lhsT=wt[:, k, 0:128], rhs=hT[:, k, CH + 16:CH + 48],
                        start=(k == 0), stop=(k == KC - 1)) for k in range(KC)],
                        [wb] + [hTb[k][XT] for k in range(KC)], [pb])
                    if part == 0:
                        for m_ in range(2):
                            P.add("act", lambda e, ps=ps, m_=m_: e.activation(
                                out=Qblk[64 * m_:64 * m_ + 64, :, 4 * m_:4 * m_ + 4],
                                in_=ps[64 * m_:64 * m_ + 64, 0:32].rearrange("p (s q) -> p s q", q=4),
                                func=AF.Copy, scale=0.125), [pb, b_Q], [b_Q])
                    elif part == 1:
                        evac(ksT[:, :], ps[:, 0:32], [pb], [b_ksT])
                    else:
                        evac(vsf[:, :], ps[:, 0:32], [pb], [b_vsf])
                        for half in range(2):
                            ps2, pb2 = next_ps()
                            P.add("pe", [lambda e, s4=s4, ps2=ps2, half=half: e.transpose(
                                out=ps2[0:4, s4 * 128:(s4 + 1) * 128], in_=vsf[:, (half * 4 + s4) * 4:(half * 4 + s4) * 4 + 4],
                                identity=ident_f) for s4 in range(4)], [b_vsf, b_cf], [pb2])
                            evac(Vnew[0:4, half * 4:half * 4 + 4, :], ps2[0:4, :].rearrange("p (s d) -> p s d", d=128),
                                 [pb2, b_Vn], [b_Vn])
                if cfg.get("STOP") == "A1":
                    P.barrier(); st.close(); mst.close(); return
                phst = kfr.t[0]; b_phst = kfr.b[0]
                P.add("dve", lambda e: e.memset(phst[:, 0:128], 0.0), [b_phst], [b_phst])
                P.add("dve", lambda e: e.tensor_copy(out=phst[:, 0:120].rearrange("p (k c) -> p k c", c=15), in_=pAll[:, :, CH:CH + 15]),
                      [b_phst] + b_pA, [b_phst])
                dma("sp", src_ph[:, :], phst[:, 0:128], [b_phst], [B["src_ph"]])
                dma("sp", pt_o.rearrange("p (k c) -> p k c", c=15), pAll[:, :, CH:CH + 15], b_pA, [B["out"]])
                for t in range(NQT):
                    cc(GROUPS4, src_kT[t][:, :], gat_kT[t][:, :], [B["src_kT"]], [B["gat_kT"]])
                    cc(GROUPS4, src_V[t][:, :], gat_V[t][:, :], [B["src_V"]], [B["gat_V"]])
                cc(GROUPS4, src_ph[:, :], gat_ph[:, :], [B["src_ph"]], [B["gat_ph"]])
                P.barrier()

            if cfg.get("STOP") == "A":
                mst.close(); return
            with ExitStack() as st:
                gph = sb("gph", [128, 4, 128], F32, st); b_gph = Buf("gph")
                spt = sb("spt", [120, 1024], F32, st); b_spt = Buf("spt")
                W1 = sb("W1", [128, 8, 271], F32, st); b_W1 = Buf("W1")
                W2 = sb("W2", [128, 8, 271], F32, st); b_W2 = Buf("W2")
                feat = sb("feat", [128, 8, 256], BF16, st); b_feat = Buf("feat")
                featS = sb("featS", [128, 8, 32], BF16, st); b_featS = Buf("featS")
                psc = sb("psc", [128, 8, 32], F32, st); b_psc = Buf("psc")
                wp_sb = sb("wp_sb", [128, 8, 256], BF16, st); b_wp = Buf("wp")
                P.add("pool", wload(wp_sb[:], wts["w_pool"], 0, 256, 8), [], [b_wp], kind="d")
                dma("sp", gph[:], gat_ph.rearrange("(r p) c -> p r c", p=128), [B["gat_ph"]], [b_gph])
                dma("sp", spt[:], spool[:, :], [], [b_spt])
                dma("sp", pso_o.rearrange("(s r) c -> s r c", r=11), spool.rearrange("(s r) c -> s r c", r=15)[:, 4:15, :],
                    [], [B["out"]])
                halo = pAll[:, :, 0:15]
                P.add("dve", lambda e: e.tensor_scalar(out=halo, in0=pM[:, :, 16:31], scalar1=cf("sel", 1, 4), scalar2=None,
                                                       op0=ALU.mult), [b_pM, b_cf] + b_pA, b_pA)
                for r in range(4):
                    P.add("dve", lambda e, r=r: e.scalar_tensor_tensor(
                        out=halo, in0=gph[:, r, 0:120].rearrange("p (k c) -> p k c", c=15), scalar=cf("sel", 1, r), in1=halo,
                        op0=ALU.mult, op1=ALU.add), [b_gph, b_cf] + b_pA, b_pA)
                for k4 in range(2):
                    ps, pb = next_ps()
                    P.add("pe", [lambda e, kk=kk, k4=k4, ps=ps: e.transpose(
                        out=ps[:, kk * 128:kk * 128 + 120], in_=spt[0:120, (k4 * 4 + kk) * 128:(k4 * 4 + kk + 1) * 128],
                        identity=ident_f[0:120, 0:120]) for kk in range(4)], [b_spt, b_cf], [pb])
                    for kk in range(4):
                        evac(pS[:, k4 * 4 + kk, :].rearrange("p (s c) -> p s c", c=19)[:, :, 0:15],
                             ps[:, kk * 128:kk * 128 + 120].rearrange("p (s c) -> p s c", c=15), [pb, b_pS], [b_pS], eng="act")
                for s in range(8):
                    P.add("dve", lambda e, s=s: e.tensor_copy(out=psc[:, :, s * 4:s * 4 + 4], in_=pS[:, :, s * 19 + 15:s * 19 + 19]),
                          [b_pS, b_psc], [b_psc])
                dma("sp", psn_o.rearrange("p (k t) -> p k t", k=8), psc[:, :, :], [b_psc], [B["out"]])

                def windows(X, L, xbufs):
                    TTa = mybir.AluOpType.add
                    P.add("dve", lambda e: e.tensor_tensor(out=W1[:, 0:8, 1:L], in0=X[:, 0:8, 1:L], in1=X[:, 0:8, 0:L - 1], op=TTa),
                          xbufs + [b_W1], [b_W1])
                    P.add("dve", lambda e: e.tensor_tensor(out=W2[:, 2:8, 3:L], in0=W1[:, 2:8, 3:L], in1=W1[:, 2:8, 1:L - 2], op=TTa),
                          [b_W1, b_W2], [b_W2])
                    P.add("dve", lambda e: e.tensor_tensor(out=W1[:, 4:8, 7:L], in0=W2[:, 4:8, 7:L], in1=W2[:, 4:8, 3:L - 4], op=TTa),
                          [b_W2, b_W1], [b_W1])
                    P.add("dve", lambda e: e.tensor_tensor(out=W2[:, 6:8, 15:L], in0=W1[:, 6:8, 15:L], in1=W1[:, 6:8, 7:L - 8], op=TTa),
                          [b_W1, b_W2], [b_W2])
                    return [W1, W2, W1, W2]

                def pool_mm(ft, fbuf, n, c0, ti):
                    for g in range(4):
                        for dc in range(2):
                            ps, pb = next_ps()
                            P.add("pe", [lambda e, cc_=cc_, ps=ps, g=g, dc=dc: e.matmul(
                                ps[:, 0:n], lhsT=wp_sb[:, 2 * g + cc_, dc * 128:(dc + 1) * 128], rhs=ft[:, 2 * g + cc_, 0:n],
                                start=(cc_ == 0), stop=(cc_ == 1)) for cc_ in range(2)], [b_wp, fbuf], [pb])
                            oc = 2 * g + dc
                            evac(hT[:, oc, c0:c0 + n], ps[:, 0:n], [pb, b_cv, hTb[oc][ti]], [hTb[oc][ti]],
                                 scale=cv_sb[:, 64 + oc:65 + oc])

                for t0 in range(0, CH, 256):
                    ti, n = t0 // 512, 256
                    X = pAll[:, :, t0:t0 + n + 15]
                    res = windows(X, n + 15, list(b_pA))
                    for g in range(4):
                        P.add("dve", lambda e, g=g, R=res[g], X=X, n=n: e.scalar_tensor_tensor(
                            out=feat[:, 2 * g:2 * g + 2, 0:n], in0=R[:, 2 * g:2 * g + 2, 15:15 + n], scalar=1.0 / WINS[g],
                            in1=X[:, 2 * g:2 * g + 2, 15:15 + n], op0=ALU.mult, op1=ALU.subtract),
                            [b_W1, b_W2, b_feat] + b_pA, [b_feat])
                    pool_mm(feat, b_feat, n, t0, ti)
                res = windows(pM, 31, [b_pM])
                for g in range(4):
                    P.add("dve", lambda e, g=g, R=res[g]: e.tensor_tensor(
                        out=R[:, 2 * g:2 * g + 2, 15:31], in0=R[:, 2 * g:2 * g + 2, 15:31],
                        in1=cf("invc", 128).rearrange("p (k t) -> p k t", t=16)[:, 2 * g:2 * g + 2, :], op=ALU.mult),
                        [b_W1, b_W2, b_cf], [b_W1, b_W2])
                    P.add("dve", lambda e, g=g, R=res[g]: e.tensor_tensor(
                        out=feat[:, 2 * g:2 * g + 2, 0:16], in0=R[:, 2 * g:2 * g + 2, 15:31], in1=pM[:, 2 * g:2 * g + 2, 15:31],
                        op=ALU.subtract), [b_W1, b_W2, b_pM, b_feat], [b_feat])
                pool_mm(feat, b_feat, 16, CH, XT)
                res = windows(pS, 152, [b_pS])
                for g in range(4):
                    P.add("dve", lambda e, g=g, R=res[g]: e.scalar_tensor_tensor(
                        out=R[:, 2 * g:2 * g + 2, 15:152], in0=R[:, 2 * g:2 * g + 2, 15:152], scalar=1.0 / WINS[g],
                        in1=pS[:, 2 * g:2 * g + 2, 15:152], op0=ALU.mult, op1=ALU.subtract),
                        [b_W1, b_W2, b_pS], [b_W1, b_W2])
                    for s in range(8):
                        P.add("dve", lambda e, g=g, R=res[g], s=s: e.tensor_copy(
                            out=featS[:, 2 * g:2 * g + 2, s * 4:s * 4 + 4], in_=R[:, 2 * g:2 * g + 2, s * 19 + 15:s * 19 + 19]),
                            [b_W1, b_W2, b_featS], [b_featS])
                pool_mm(featS, b_featS, 32, CH + 16, XT)
                P.barrier()
            mst.close()

            if cfg.get("STOP") == "B1":
                return
            with ExitStack() as st:
                NVT = 5 * NKT + 1
                kTg = sb("kTg", [128, 4, CH], BF16, st)
                kTo = sb("kTo", [128, CH], BF16, st)
                kTm = sb("kTm", [128, 16], BF16, st)
                Vh = sb("Vh", [128, NVT, 129], BF16, st)
                qh = sb("qh", [128, CH + 16], BF16, st)
                qbt = sb("qbt", [128, CH], F32, st)
                tmpr = Rot("tmp", [128, 512], F32, 3, st)
                Ar = Rot("A", [128, 512], BF16, 4, st)
                on1 = sb("on1", [128, 4, 128], F32, st); b_on1 = Buf("on1")
                fin = sb("fin", [128, 4, 128], F32, st); b_fin = Buf("fin")
                sqt = sb("sqt", [128, 128], F32, st); b_sqt = Buf("sqt")
                rc = sb("rc", [128, 16], F32, st); b_rc = Buf("rc")
                resb = sb("resb", [128, 4, 128], BF16, st); b_resb = Buf("resb")
                b_kTg, b_kTo, b_kTm, b_Vg, b_Vo, b_Vm, b_qh, b_qbt = (Buf(x) for x in ("kTg", "kTo", "kTm", "Vg", "Vo", "Vm", "qh", "qbt"))
                P.add("dve", lambda e: e.memset(Vh[:, :, 128:129], 1.0), [], [b_Vg, b_Vo, b_Vm])
                acc_ps = [psum[4], psum[5]]
                acc_b = [psb[4], psb[5]]
                c5 = [0]

                def next_ps5():
                    i = c5[0] % 4
                    c5[0] += 1
                    return psum[i], psb[i]

                def attend(h, q0, nq, subs, keytiles, outc0, ti_out):
                    for m_ in range(2):
                        p0 = 64 * m_
                        nkts = len(keytiles)
                        pend = []
                        LA = 3
                        def emit_av(item):
                            A_, Ab, V_ap, vbuf, nk, ki = item
                            fns = []
                            if ki == 0:
                                rows0 = subs[0][1]
                                nb = (len(subs) + 2) // 3
                                for bi in range(nb):
                                    ncol = 129 * min(3, len(subs) - 3 * bi)
                                    fns.append(lambda e, bi=bi, ncol=ncol, rows0=rows0: e.matmul(
                                        acc_ps[bi][0:rows0, 0:ncol], lhsT=cb("zerob", 128)[:, 0:rows0], rhs=cb("maskd", 512)[:, 0:ncol],
                                        start=True, stop=False))
                            for si, (so, rows) in enumerate(subs):
                                accp = acc_ps[si // 3]
                                c0 = (si % 3) * 129
                                last_in_bank = (si % 3 == 2) or (si == len(subs) - 1)
                                fns.append(lambda e, A_=A_, so=so, rows=rows, accp=accp, c0=c0, V_ap=V_ap, nk=nk, ki=ki, lib=last_in_bank: e.matmul(
                                    accp[0:rows, c0:c0 + 129], lhsT=A_[0:nk, so:so + rows], rhs=V_ap[0:nk, :],
                                    start=False, stop=(ki == nkts - 1 and lib)))
                            P.add("pe", fns, [Ab, vbuf, b_cb] + acc_b, acc_b)

                        for ki, (kT_ap, kbuf, V_ap, vbuf, bias_ap, nk, mask_ap) in enumerate(keytiles):
                            ps, pb = next_ps5()
                            fns = [lambda e, ps=ps, kT_ap=kT_ap, nk=nk, mask_ap=mask_ap, p0=p0: e.matmul(
                                ps[0:nk, 0:nq], lhsT=kT_ap[p0:p0 + 64, 0:nk], rhs=qh[p0:p0 + 64, q0:q0 + nq],
                                start=True, stop=(mask_ap is None))]
                            if mask_ap is not None:
                                fns.append(lambda e, ps=ps, nk=nk, mask_ap=mask_ap: e.matmul(
                                    ps[0:nk, 0:nq], lhsT=ident_b[0:nk, 0:nk], rhs=mask_ap, start=False, stop=True))
                            P.add("pe", fns, [kbuf, b_qh, b_cb], [pb])
                            tm, tmb = tmpr.get()
                            P.add("dve", lambda e, tm=tm, ps=ps, nk=nk: e.tensor_tensor(
                                out=tm[0:nk, 0:nq], in0=ps[0:nk, 0:nq], in1=qbt[0:nk, 0:nq] if q0 >= CH else qbt[0:nk, q0:q0 + nq],
                                op=ALU.add), [pb, b_qbt], [tmb])
                            A_, Ab = Ar.get()
                            P.add("act", lambda e, A_=A_, tm=tm, nk=nk, bias_ap=bias_ap: e.activation(
                                out=A_[0:nk, 0:nq], in_=tm[0:nk, 0:nq], func=AF.Exp, bias=bias_ap, scale=1.0), [tmb, b_cf], [Ab])
                            pend.append((A_, Ab, V_ap, vbuf, nk, ki))
                            if len(pend) > LA:
                                emit_av(pend.pop(0))
                            if nq == 512 and ki % 4 == 3:
                                sample_step()
                        while pend:
                            emit_av(pend.pop(0))
                        for si, (so, rows) in enumerate(subs):
                            accp = acc_ps[si // 3]
                            c0 = (si % 3) * 129
                            P.add("dve", lambda e, accp=accp, c0=c0, rows=rows, si=si: e.reciprocal(
                                out=rc[0:rows, si:si + 1], in_=accp[0:rows, c0 + 128:c0 + 129]), acc_b + [b_rc], [b_rc])
                            if m_ == 0:
                                P.add("dve", lambda e, accp=accp, c0=c0, rows=rows, si=si: e.tensor_scalar(
                                    out=on1[0:rows, si, :], in0=accp[0:rows, c0:c0 + 128], scalar1=rc[0:rows, si:si + 1],
                                    scalar2=None, op0=ALU.mult), acc_b + [b_rc, b_on1], [b_on1])
                            else:
                                P.add("dve", lambda e, accp=accp, c0=c0, rows=rows, si=si: e.tensor_scalar(
                                    out=fin[0:rows, si, :], in0=accp[0:rows, c0:c0 + 128], scalar1=rc[0:rows, si:si + 1],
                                    scalar2=None, op0=ALU.mult), acc_b + [b_rc, b_fin], [b_fin])
                                P.add("dve", lambda e, rows=rows, si=si: e.scalar_tensor_tensor(
                                    out=fin[0:rows, si, :], in0=fin[0:rows, si, :], scalar=lam_sb[0:rows, 1:2], in1=on1[0:rows, si, :],
                                    op0=ALU.mult, op1=ALU.add), [b_fin, b_on1, b_lam], [b_fin])
                    for si, (so, rows) in enumerate(subs):
                        P.add("dve", lambda e, rows=rows, si=si: e.tensor_tensor(
                            out=sqt[0:rows, :], in0=fin[0:rows, si, :], in1=fin[0:rows, si, :], op=ALU.mult), [b_fin, b_sqt], [b_sqt])
                        P.add("dve", lambda e, rows=rows, si=si: e.reduce_sum(
                            out=rc[0:rows, 8 + si:9 + si], in_=sqt[0:rows, :], axis=mybir.AxisListType.X), [b_sqt, b_rc], [b_rc])
                        P.add("act", lambda e, rows=rows, si=si: e.activation(
                            out=rc[0:rows, 8 + si:9 + si], in_=rc[0:rows, 8 + si:9 + si], func=AF.Sqrt, bias=eps_col[0:rows, :],
                            scale=1.0 / 128), [b_rc, b_cf], [b_rc])
                        P.add("dve", lambda e, rows=rows, si=si: e.reciprocal(
                            out=rc[0:rows, 8 + si:9 + si], in_=rc[0:rows, 8 + si:9 + si]), [b_rc], [b_rc])
                        P.add("dve", lambda e, rows=rows, si=si: e.scalar_tensor_tensor(
                            out=resb[0:rows, si, :], in0=fin[0:rows, si, :], scalar=rc[0:rows, 8 + si:9 + si], in1=gain08[0:rows, :],
                            op0=ALU.mult, op1=ALU.mult), [b_fin, b_rc, b_g08, b_resb], [b_resb])
                        P.add("pe", lambda e, rows=rows, si=si: e.transpose(
                            out=psbf[:, si * 128:si * 128 + rows], in_=resb[0:rows, si, :], identity=ident_b[0:rows, 0:rows]),
                            [b_resb, b_cb, b_psbf], [b_psbf])
                        evac(hT[:, 8 + h, outc0 + so:outc0 + so + rows], psbf[:, si * 128:si * 128 + rows],
                             [b_psbf, hTb[8 + h][ti_out]], [hTb[8 + h][ti_out]])

                def sample_gen():
                    Ku = Rot("Ku", [128, 2048], BF16, 2, st)
                    Vu = Rot("Vu", [128, 2048], BF16, 2, st)
                    kTu = Rot("kTu", [128, 2048], BF16, 2, st)
                    Eu = Rot("Eu", [128, 128], F32, 2, st)
                    A_all = sb("A_all", [128, PAGE * 8], BF16, st); b_Aall = Buf("Aall")
                    rsum = sb("rsum", [128, 8], F32, st); b_rsum = Buf("rsum")
                    An32 = sb("An32", [4, 8], F32, st); b_An32 = Buf("An32")
                    An = sb("An", [4, 8], BF16, st); b_An = Buf("An")
                    osb = sb("osb", [8, 128], F32, st); b_osb = Buf("osb")
                    orc = sb("orc", [8, 4], F32, st); b_orc = Buf("orc")
                    f4 = sb("f4", [4, 128], F32, st); b_f4 = Buf("f4")
                    sq4 = sb("sq4", [4, 128], F32, st); b_sq4 = Buf("sq4")
                    r4 = sb("r4", [4, 2], F32, st); b_r4 = Buf("r4")
                    res_all = sb("res_all", [4, 8, 128], F32, st); b_res = Buf("res_all")
                    osamp = sb("osamp", [32, 8, 128], F32, st); b_osamp = Buf("osamp")
                    o_ps, o_pb = psum[6], psb[6]
                    for s in range(8):
                        P.add("pe", lambda e: e.matmul(o_ps[0:8, 0:129], lhsT=cb("zerob", 128)[:, 0:8], rhs=cb("maskd", 512)[:, 0:129],
                                                       start=True, stop=False), [b_cb, o_pb], [o_pb])
                        for u in range(NSUB):
                            ku, kub = Ku.get()
                            P.add("pool", lambda e, ku=ku, u=u, s=s: e.indirect_dma_start(
                                out=ku[:], out_offset=None, in_=cks[u][:, :],
                                in_offset=bass.IndirectOffsetOnAxis(ap=pt_sb[:, s:s + 1], axis=0)), [b_pt], [kub], kind="d")
                            vu, vub = Vu.get()
                            P.add("pool", lambda e, vu=vu, u=u, s=s: e.indirect_dma_start(
                                out=vu[:], out_offset=None, in_=cvs[u][:, :],
                                in_offset=bass.IndirectOffsetOnAxis(ap=pt_sb[:, s:s + 1], axis=0)), [b_pt], [vub], kind="d")
                            ktu, ktub = kTu.get()
                            for half in range(2):
                                P.add("pe", [lambda e, tk=tk, ku=ku, half=half: e.transpose(
                                    out=psbf[:, tk * 128:(tk + 1) * 128], in_=ku[:, (half * 8 + tk) * 128:(half * 8 + tk + 1) * 128],
                                    identity=ident_b) for tk in range(8)], [kub, b_cb, b_psbf], [b_psbf])
                                evac(ktu[:, half * 1024:(half + 1) * 1024], psbf[:, :], [b_psbf, ktub], [ktub])
                            yield
                            ps, pb = next_ps5()
                            P.add("pe", [lambda e, tk=tk, ps=ps, ktu=ktu, s=s: e.matmul(
                                ps[:, tk * 8:(tk + 1) * 8], lhsT=ktu[:, tk * 128:(tk + 1) * 128], rhs=Qblk[:, s, :],
                                start=True, stop=True) for tk in range(16)], [ktub, b_Q], [pb])
                            eu, eub = Eu.get()
                            P.add("act", lambda e, eu=eu, ps=ps: e.activation(out=eu[:], in_=ps[:, 0:128], func=AF.Exp,
                                                                             bias=cf("sbias"), scale=1.0), [pb, b_cf], [eub])
                            P.add("dve", lambda e, eu=eu, u=u: e.tensor_tensor(
                                out=A_all[:, u * 128:(u + 1) * 128], in0=eu[:], in1=cf("wtok", 128, u * 128), op=ALU.mult),
                                [eub, b_cf, b_Aall], [b_Aall])
                            yield
                            P.add("pe", [lambda e, tk=tk, vu=vu, u=u: e.matmul(
                                o_ps[0:8, 0:128], lhsT=A_all[:, (u * 16 + tk) * 8:(u * 16 + tk) * 8 + 8], rhs=vu[:, tk * 128:(tk + 1) * 128],
                                start=False, stop=False) for tk in range(16)] + [lambda e, tk=tk, u=u: e.matmul(
                                o_ps[0:8, 128:129], lhsT=A_all[:, (u * 16 + tk) * 8:(u * 16 + tk) * 8 + 8], rhs=ones_b[:, 0:1],
                                start=False, stop=False) for tk in range(16)], [b_Aall, vub, o_pb, b_cb], [o_pb])
                            yield
                        ps, pb = next_ps5()
                        P.add("pe", lambda e, ps=ps, s=s: e.matmul(ps[0:4, 0:8], lhsT=ksT[:, 4 * s:4 * s + 4], rhs=Qblk[:, s, :],
                                                                   start=True, stop=True), [b_ksT, b_Q], [pb])
                        P.add("act", lambda e, ps=ps: e.activation(out=An32[:], in_=ps[0:4, 0:8], func=AF.Exp,
                                                                   bias=cf("nbias")[0:4, :], scale=1.0), [pb, b_cf, b_An32], [b_An32])
                        P.add("dve", lambda e: e.tensor_tensor(out=An32[:], in0=An32[:], in1=cf("nmask", 8)[0:4, :], op=ALU.mult),
                              [b_An32, b_cf], [b_An32])
                        P.add("dve", lambda e: e.tensor_copy(out=An[:], in_=An32[:]), [b_An32, b_An], [b_An])
                        P.add("pe", [lambda e, s=s: e.matmul(o_ps[0:8, 0:128], lhsT=An[0:4, :], rhs=Vnew[0:4, s, :], start=False, stop=False),
                                     lambda e: e.matmul(o_ps[0:8, 128:129], lhsT=An[0:4, :], rhs=ones_b[0:4, 0:1], start=False, stop=True)],
                              [b_An, b_Vn, o_pb, b_cb], [o_pb])
                        P.add("dve", lambda e: e.reciprocal(out=orc[:, 0:1], in_=o_ps[0:8, 128:129]), [o_pb, b_orc], [b_orc])
                        P.add("dve", lambda e: e.tensor_scalar(out=osb[:], in0=o_ps[0:8, 0:128], scalar1=orc[:, 0:1], scalar2=None,
                                                               op0=ALU.mult), [o_pb, b_orc, b_osb], [b_osb])
                        ps, pb = next_ps5()
                        P.add("pe", lambda e, ps=ps: e.matmul(ps[0:4, 0:128], lhsT=Cm[0:8, :], rhs=osb[0:8, :], start=True, stop=True),
                              [b_Cm, b_osb], [pb])
                        P.add("dve", lambda e, ps=ps: e.tensor_copy(out=f4[:], in_=ps[0:4, 0:128]), [pb, b_f4], [b_f4])
                        P.add("dve", lambda e: e.tensor_tensor(out=sq4[:], in0=f4[:], in1=f4[:], op=ALU.mult), [b_f4, b_sq4], [b_sq4])
                        P.add("dve", lambda e: e.reduce_sum(out=r4[:, 0:1], in_=sq4[:], axis=mybir.AxisListType.X), [b_sq4, b_r4], [b_r4])
                        P.add("act", lambda e: e.activation(out=r4[:, 0:1], in_=r4[:, 0:1], func=AF.Sqrt, bias=eps_col[0:4, :],
                                                            scale=1.0 / 128), [b_r4, b_cf], [b_r4])
                        P.add("dve", lambda e: e.reciprocal(out=r4[:, 0:1], in_=r4[:, 0:1]), [b_r4], [b_r4])
                        P.add("dve", lambda e, s=s: e.scalar_tensor_tensor(
                            out=res_all[0:4, s, :], in0=f4[:], scalar=r4[:, 0:1], in1=gain08[0:4, :], op0=ALU.mult, op1=ALU.mult),
                            [b_f4, b_r4, b_g08, b_res], [b_res])
                    dma("sp", src_o.rearrange("(s q) d -> q s d", q=4), res_all[0:4, :, :], [b_res], [B["src_o"]])
                    cc(GROUPS4, src_o[:, :], gat_o4[:, :], [B["src_o"]], [B["gat_o4"]])
                    cc(PAIRS, gat_o4[:, :], gat_o8[:, :], [B["gat_o4"]], [B["gat_o8"]])
                    dma("sp", osamp[:], gat_o8.rearrange("(h t) d -> t h d", t=32), [B["gat_o8"]], [b_osamp])
                    ps, pb = next_ps5()
                    P.add("pe", [lambda e, h=h, ps=ps: e.transpose(out=ps[:, h * 32:(h + 1) * 32], in_=osamp[0:32, h, :],
                                                                  identity=ident_f[0:32, 0:32]) for h in range(8)],
                          [b_osamp, b_cf], [pb])
                    evac(hT[:, 8:16, CH + 16:CH + 48], ps[:, 0:256].rearrange("p (h t) -> p h t", t=32),
                         [pb] + [hTb[8 + h][XT] for h in range(8)], [hTb[8 + h][XT] for h in range(8)])
                    P.barrier()


                sgen = sample_gen()
                sdone = [False]

                def sample_step():
                    if not sdone[0]:
                        try:
                            next(sgen)
                        except StopIteration:
                            sdone[0] = True

                for h in range(NH):
                    for t in range(NQT):
                        dma("sp", kTg[:, :, t * 512:(t + 1) * 512], gat_kT[t].rearrange("(r hd) t -> hd r t", r=4)[h * 128:(h + 1) * 128, :, :], [B["gat_kT"]], [b_kTg])
                        dma("sp", kTo[:, t * 512:(t + 1) * 512], src_kT[t][h * 128:(h + 1) * 128, :], [B["src_kT"]], [b_kTo])
                    dma("sp", kTm[:], scr_mk[h * 128:(h + 1) * 128, :], [B["scr_mk"]], [b_kTm])
                    for t in range(NQT):
                        for r in range(4):
                            dma("sp", Vh[:, r * NKT + t * 4:r * NKT + t * 4 + 4, 0:128],
                                gat_V[t][r * 512:(r + 1) * 512, h * 128:(h + 1) * 128].rearrange("(k p) c -> p k c", p=128),
                                [B["gat_V"]], [b_Vg])
                        dma("sp", Vh[:, 4 * NKT + t * 4:4 * NKT + t * 4 + 4, 0:128],
                            src_V[t][:, h * 128:(h + 1) * 128].rearrange("(k p) c -> p k c", p=128), [B["src_V"]], [b_Vo])
                    dma("sp", Vh[0:16, 5 * NKT, 0:128], scr_mv[0:16, h * 128:(h + 1) * 128], [B["scr_mv"]], [b_Vm])
                    dma("sp", qh[:], scr_q[h * 128:(h + 1) * 128, :], [B["scr_q"]], [b_qh])
                    P.add("dve", lambda e, h=h: e.tensor_scalar(out=qbt[:], in0=cf("qrow", CH), scalar1=-SLOPES[h], scalar2=None,
                                                                op0=ALU.mult), [b_cf, b_qbt], [b_qbt])
                    meta_kt = (kTm, b_kTm, Vh[:, 5 * NKT, :], b_Vm, cf("biasm", 1, h)[0:16, :], 16, None)
                    for qt in range(NQT):
                        kts = []
                        for r in range(3):
                            for kt in range(NKT):
                                kts.append((kTg[:, r, kt * 128:(kt + 1) * 128], b_kTg, Vh[:, r * NKT + kt, :], b_Vg,
                                            cf("biasg", 1, (h * 4 + r) * NKT + kt), 128, None))
                        kts.append(meta_kt)
                        for kt in range(4 * qt + 4):
                            m = kt - 4 * qt
                            kts.append((kTo[:, kt * 128:(kt + 1) * 128], b_kTo, Vh[:, 4 * NKT + kt, :], b_Vo,
                                        cf("biaso", 1, h * NKT + kt), 128, cb("maskd", 512, m * 512) if m >= 0 else None))
                        attend(h, qt * 512, 512, [(i * 128, 128) for i in range(4)], kts, qt * 512, qt)
                    mm_kt = (kTm, b_kTm, Vh[:, 5 * NKT, :], b_Vm, cf("biasmm", 1, h)[0:16, :], 16, cb("maskmm", 16)[0:16, :])
                    attend(h, CH, 16, [(0, 16)], [mm_kt], CH, XT)
                while not sdone[0]:
                    sample_step()
                P.barrier()

            if cfg.get("STOP") == "B3":
                return
            with ExitStack() as st:
                wo = Rot("wo", [128, KC, 256], BF16, 3, st)
                xTbo = [[Buf(f"ox{k}_{t}") for t in range(len(TTF))] for k in range(KC)]
                hTbo = [[Buf(f"oh{k}_{t}") for t in range(len(TTF))] for k in range(KC)]
                for oc0 in range(0, KC, 2):
                    wt, wb = wo.get()
                    P.add("pool", wload(wt[:], wts["w_out"], oc0 * 128, 256, KC), [], [wb], kind="d")
                    for ff in range(2):
                        oc = oc0 + ff
                        for ti, (t0, n) in enumerate(TTF):
                            ps, pb = next_ps()
                            P.add("pe", [lambda e, k=k, ps=ps, wt=wt, ff=ff, t0=t0, n=n: e.matmul(
                                ps[:, 0:n], lhsT=wt[:, k, ff * 128:(ff + 1) * 128], rhs=hT[:, k, t0:t0 + n],
                                start=(k == 0), stop=(k == KC - 1)) for k in range(KC)],
                                [wb] + [hTbo[k][ti] for k in range(KC)], [pb])
                            P.add("dve", lambda e, ps=ps, oc=oc, t0=t0, n=n: e.tensor_tensor(
                                out=xT[:, oc, t0:t0 + n], in0=ps[:, 0:n], in1=xT[:, oc, t0:t0 + n], op=ALU.add),
                                [pb, xTbo[oc][ti]], [xTbo[oc][ti]])
                P.barrier()


        ffn(0, wts["w_gate1"], wts["w_up1"], wts["w_down1"])
        if cfg.get("MIXER", True):
            mixer(locals())
        ffn(32, wts["w_gate2"], wts["w_up2"], wts["w_down2"])

        with ExitStack() as st:
            ytok = Rot("ytok", [128, D], F32, 2, st)
            yf = sb("yf", [128, KC, 128], F32, st)
            b_yf = Buf("yf")
            gF = 48
            for ti, (t0, n) in enumerate(TT):
                rs, rsb = rms_rstd(ti)
                for s0 in range(0, n, 128):
                    ns = min(128, n - s0)
                    for k in range(KC):
                        P.add("dve", lambda e, k=k, rs=rs, t0=t0, s0=s0, ns=ns: e.scalar_tensor_tensor(
                            out=yf[:, k, 0:ns], in0=xT[:, k, t0 + s0:t0 + s0 + ns], scalar=cv_sb[:, gF + k:gF + k + 1],
                            in1=rs[:, s0:s0 + ns], op0=ALU.mult, op1=ALU.mult), [xTb[k][ti], rsb, b_cv, b_yf], [b_yf])
                    yt, ytb = ytok.get()
                    for k4 in range(KC // 4):
                        ps, pb = next_ps()
                        P.add("pe", [lambda e, kk=kk, k4=k4, ps=ps, ns=ns: e.transpose(
                            out=ps[0:ns, kk * 128:(kk + 1) * 128], in_=yf[:, k4 * 4 + kk, 0:ns], identity=ident_f)
                            for kk in range(4)], [b_yf, b_cf], [pb])
                        evac(yt[0:ns, k4 * 512:(k4 + 1) * 512], ps[0:ns, :], [pb], [ytb])
                    dma("sp", y_o[t0 + s0:t0 + s0 + ns, :], yt[0:ns, :], [ytb], [B["out"]])
            P.barrier()

        streams = P.emit(nc, sems, None)
        with nc.Block() as block:
            @block.tensor
            def _(e):
                run_streams({"pe": streams["pe"]}, sems, {"pe": e})

            @block.scalar
            def _(e):
                run_streams({"act": streams["act"]}, sems, {"act": e})

            @block.vector
            def _(e):
                run_streams({"dve": streams["dve"]}, sems, {"dve": e})

            @block.gpsimd
            def _(e):
                run_streams({"pool": streams["pool"]}, sems, {"pool": e})

            @block.sync
            def _(e):
                run_streams({"sp": streams["sp"]}, sems, {"sp": e})
    return nc


def host_consts(cfg, core):
    import ml_dtypes
    CH, PAGE = cfg["CH"], cfg["PAGE"]
    NKT = CH // 128
    j = core % 4
    off = {}
    cols = []
    i = np.arange(128, dtype=np.float64)[:, None]

    def addf(name, arr):
        off[name] = sum(a.shape[1] for a in cols)
        a = np.zeros((128, np.asarray(arr).shape[1]), np.float32)
        a[:np.asarray(arr).shape[0]] = np.asarray(arr, np.float32)
        cols.append(a)

    addf("ident", np.eye(128))
    addf("onesf", np.ones((128, 128)))
    addf("eps", np.full((128, 1), EPS))
    sel = np.zeros((128, 5));
    if j == 0:
        sel[:, 4] = 1.0
    else:
        sel[:, j - 1] = 1.0
    addf("sel", sel)
    bg = np.zeros((128, NH * 4 * NKT))
    bo = np.zeros((128, NH * NKT))
    bm = np.zeros((128, NH)); bmm = np.zeros((128, NH))
    for h in range(NH):
        sl = SLOPES[h]
        for r in range(4):
            for kt in range(NKT):
                bg[:, (h * 4 + r) * NKT + kt] = sl * (i[:, 0] + 128 * kt + CH * (r - j)) + (NEG if r >= j else 0.0)
        for kt in range(NKT):
            bo[:, h * NKT + kt] = sl * (i[:, 0] + 128 * kt)
        bm[:, h] = sl * (i[:, 0] - 16 - j * CH)
        bmm[:, h] = sl * i[:, 0]
    addf("biasg", bg); addf("biaso", bo); addf("biasm", bm); addf("biasmm", bmm)
    addf("qrow", np.broadcast_to(np.arange(CH, dtype=np.float64)[None, :], (128, CH)))
    slc = SLOPES[core]
    addf("sbias", slc * PAGE * (i - 127))
    tok = np.repeat(np.arange(PAGE), 8)[None, :]
    addf("wtok", np.broadcast_to(np.exp(slc * (tok - (PAGE - 1))), (128, PAGE * 8)))
    addf("nbias", slc * (1 + i))
    nm = np.zeros((128, 8))
    for jn in range(4):
        for m in range(2):
            for q in range(4):
                nm[jn, m * 4 + q] = 1.0 if jn <= q else 0.0
    addf("nmask", nm)
    invc = np.zeros((128, 128))
    for kc in range(8):
        for t in range(16):
            invc[:, kc * 16 + t] = 1.0 / min(t + 1, WINS[kc // 2])
    addf("invc", invc)
    c0 = np.zeros((128, 4)); c1 = np.zeros((128, 4))
    for q in range(4):
        c0[q, q] = 1.0; c1[4 + q, q] = 1.0
    addf("cb0", c0); addf("cb1", c1)
    cf = np.concatenate(cols, 1)
    colsb = []

    def addb(name, arr):
        off[name] = sum(a.shape[1] for a in colsb)
        colsb.append(np.asarray(arr, np.float32).astype(ml_dtypes.bfloat16))

    addb("identb", np.eye(128))
    addb("onesb", np.ones((128, 128)))
    addb("zerob", np.zeros((128, 128)))
    jq = np.arange(512)[None, :]
    md = np.concatenate([np.where(i + 128 * m <= jq, 0.0, NEG) for m in range(4)], 1)
    addb("maskd", md)
    addb("maskmm", np.where(i <= np.arange(16)[None, :], 0.0, NEG))
    cb = np.concatenate(colsb, 1)
    return off, cf, cb


def fm(v, n):
    return np.ascontiguousarray(np.asarray(v, np.float32).reshape(n, 128).T)


def prep_inputs(cfg, inp):
    CH, NSUB = cfg["CH"], cfg["NSUB"]
    maps = []
    x_prompt = np.asarray(inp["x_prompt"]); x_sample = np.asarray(inp["x_sample"])
    meta = np.asarray(inp["meta_tokens"])
    cvec = np.concatenate([fm(inp["norm_ffn1"][0], 16), fm(inp["norm_mix"][0], 16), fm(inp["norm_ffn2"][0], 16),
                           fm(inp["norm_final"], 16), fm(inp["pool_scale"][0], 8)], 1)
    gain_bc = np.ascontiguousarray(np.broadcast_to(np.asarray(inp["subln_gain"][0], np.float32)[None, :], (128, 128)))
    lamv = np.concatenate([np.asarray(inp[k][0], np.float32) for k in ("lambda_q1", "lambda_k1", "lambda_q2", "lambda_k2")])[None, :]
    w_in = np.asarray(inp["w_in"][0])
    shared = dict(cvec=cvec, gain_bc=gain_bc, lamv=np.ascontiguousarray(lamv),
                  w_gate1=np.asarray(inp["w_gate1"][0]), w_up1=np.asarray(inp["w_up1"][0]), w_down1=np.asarray(inp["w_down1"][0]),
                  w_in=w_in, w_out=np.asarray(inp["w_out"][0]),
                  w_gate2=np.asarray(inp["w_gate2"][0]), w_up2=np.asarray(inp["w_up2"][0]), w_down2=np.asarray(inp["w_down2"][0]),
                  w_pool=np.ascontiguousarray(np.asarray(inp["w_pool"][0]).reshape(1024, 256)),
                  spool=np.ascontiguousarray(np.asarray(inp["state_pool"][0]).reshape(120, 1024)),
                  ptab=np.ascontiguousarray(np.asarray(inp["page_table"]).T.astype(np.int32)))
    ckf = np.asarray(inp["cache_k"][0]); cvf = np.asarray(inp["cache_v"][0])
    npool = ckf.shape[0]
    for c in range(8):
        b, j = c // 4, c % 4
        m = dict(shared)
        m["xin"] = np.ascontiguousarray(np.concatenate([x_prompt[b, j * CH:(j + 1) * CH], meta, x_sample.reshape(32, D)], 0))
        m["w_in_hd"] = np.ascontiguousarray(np.concatenate([w_in[:, 1024 + 128 * c:1152 + 128 * c],
                                                            w_in[:, 2048 + 128 * c:2176 + 128 * c],
                                                            w_in[:, 3072 + 128 * c:3200 + 128 * c]], 1))
        for u in range(NSUB):
            m[f"ck{u}"] = np.ascontiguousarray(ckf[:, 16 * u:16 * u + 16, c, :].reshape(npool, 2048))
            m[f"cv{u}"] = np.ascontiguousarray(cvf[:, 16 * u:16 * u + 16, c, :].reshape(npool, 2048))
        off, cf, cb = host_consts(cfg, c)
        m["cf32"], m["cbf"] = cf, cb
        maps.append(m)
    return maps


def finish_cfg(cfg):
    off, cf, cb = host_consts(cfg, 0)
    cfg["OFF"] = off; cfg["NCF"] = cf.shape[1]; cfg["NCB"] = cb.shape[1]
    return cfg


def assemble(cfg, res):
    CH, SEQ = cfg["CH"], cfg["SEQ"]
    y_prompt = np.zeros((2, SEQ, D), np.float32)
    k_prompt = np.zeros((1, 2, 16 + SEQ, 8, 128), np.float32)
    v_prompt = np.zeros((1, 2, 16 + SEQ, 8, 128), np.float32)
    pool_prompt = np.zeros((1, 2, 15, 1024), np.float32)
    for c in range(8):
        b, j = c // 4, c % 4
        r = res[c]
        y_prompt[b, j * CH:(j + 1) * CH] = r["y"][0:CH]
        k_prompt[0, b, 16 + j * CH:16 + (j + 1) * CH] = r["ko"][0:CH].reshape(CH, 8, 128)
        v_prompt[0, b, 16 + j * CH:16 + (j + 1) * CH] = r["vo"][0:CH].reshape(CH, 8, 128)
        if j == 0:
            k_prompt[0, b, 0:16] = r["ko"][CH:CH + 16].reshape(16, 8, 128)
            v_prompt[0, b, 0:16] = r["vo"][CH:CH + 16].reshape(16, 8, 128)
        if j == 3:
            pool_prompt[0, b] = r["ptail"].reshape(128, 8, 15).transpose(2, 1, 0).reshape(15, 1024)
    r0 = res[0]
    y_sample = np.ascontiguousarray(r0["y"][CH + 16:CH + 48].reshape(8, 4, D))
    k_sample = np.ascontiguousarray(r0["ko"][CH + 16:CH + 48].reshape(1, 8, 4, 8, 128))
    v_sample = np.ascontiguousarray(r0["vo"][CH + 16:CH + 48].reshape(1, 8, 4, 8, 128))
    new = r0["psnew"].reshape(128, 8, 8, 4).transpose(2, 3, 1, 0).reshape(8, 4, 1024)
    old = r0["psold"].reshape(8, 11, 1024)
    pool_sample = np.ascontiguousarray(np.concatenate([old, new], 1)[None])
    return (y_prompt, y_sample, k_prompt, v_prompt, pool_prompt, k_sample, v_sample, pool_sample)


_CACHE = {}


def kernel(**inputs):
    cfg = finish_cfg(make_cfg())
    if "nc" not in _CACHE:
        _CACHE["nc"] = build(cfg)
    nc = _CACHE["nc"]
    maps = prep_inputs(cfg, inputs)
    res = run_bass_kernel_spmd(nc, maps, core_ids=list(range(8)))
    return assemble(cfg, res.results)
```
